# Optimizing a Trainium2 kernel written in Bass

```python
import jax, jax.numpy as jnp
from jax import lax
import numpy as np

D_MODEL = 1024
BATCH = 8
SEQ = 8192
DEPTH = 2

HEAD_DIM = 64
N_Q_HEADS = 8
N_KV_HEADS = 2
GQA_GROUPS = N_Q_HEADS // N_KV_HEADS
WINDOW = 128
BLOCK = 128
ATTN_WIDTH = N_Q_HEADS * HEAD_DIM
KV_WIDTH = N_KV_HEADS * HEAD_DIM
POOL_WINDOWS = (2, 4, 8, 16)
N_POOL_GROUPS = len(POOL_WINDOWS)
POOL_WIDTH = D_MODEL - ATTN_WIDTH
POOL_GROUP = POOL_WIDTH // N_POOL_GROUPS
AB_IN_WIDTH = ATTN_WIDTH + 2 * KV_WIDTH + POOL_WIDTH
AB_OUT_WIDTH = ATTN_WIDTH + POOL_WIDTH
RWKV_HEAD = 64
RWKV_HEADS = D_MODEL // RWKV_HEAD
DECAY_LORA = 64
AAA_LORA = 64
GATE_LORA = 128
N_SHIFT_MIX = 6
D_FF = 2816
N_A_LAYERS = (DEPTH + 1) // 2
N_C_LAYERS = DEPTH // 2
RMS_EPS = 1e-6
GN_EPS = 64e-5

kernel_name = "hybrid_swa_pool_rwkv7_macaron"


def rms_norm(x, gain):
    xf = x.astype(jnp.float32)
    y = xf * lax.rsqrt(jnp.mean(xf * xf, axis=-1, keepdims=True) + RMS_EPS)
    return (y * gain.astype(jnp.float32)).astype(x.dtype)


def swiglu(h, w_gate, w_up, w_down):
    return (jax.nn.silu(h @ w_gate) * (h @ w_up)) @ w_down


def alibi_slopes():
    return 2.0 ** (-8.0 * jnp.arange(1, N_Q_HEADS + 1, dtype=jnp.float32) / N_Q_HEADS)


def sliding_window_attention(q, k, v, sinks):
    b, t = q.shape[:2]
    nb = t // BLOCK
    qb = q.reshape(b, nb, BLOCK, N_KV_HEADS, GQA_GROUPS, HEAD_DIM).astype(jnp.float32)

    def band(z):
        zb = z.reshape(b, nb, BLOCK, N_KV_HEADS, HEAD_DIM)
        prev = jnp.concatenate([jnp.zeros_like(zb[:, :1]), zb[:, :-1]], axis=1)
        return jnp.concatenate([prev, zb], axis=2).astype(jnp.float32)

    kb, vb = band(k), band(v)
    qi = jnp.arange(BLOCK)[:, None]
    kj = jnp.arange(2 * BLOCK)[None, :]
    dist = qi - kj + BLOCK
    key_pos = jnp.arange(nb)[:, None, None] * BLOCK + kj[None] - BLOCK
    valid = (dist >= 0) & (dist < WINDOW) & (key_pos >= 0)
    slopes = alibi_slopes().reshape(N_KV_HEADS, GQA_GROUPS)[None, :, :, None, None, None]
    scores = jnp.einsum('bnqhgd,bnkhd->bhgnqk', qb, kb) * (HEAD_DIM ** -0.5)
    scores = jnp.where(valid, scores - slopes * dist.astype(jnp.float32), -jnp.inf)
    sink = sinks.astype(jnp.float32).reshape(1, N_KV_HEADS, GQA_GROUPS, 1, 1, 1)
    m = jnp.maximum(jnp.max(scores, axis=-1, keepdims=True), sink)
    p = jnp.exp(scores - m)
    denom = jnp.sum(p, axis=-1, keepdims=True) + jnp.exp(sink - m)
    out = jnp.einsum('bhgnqk,bnkhd->bnqhgd', p / denom, vb)
    return out.reshape(b, t, ATTN_WIDTH)


def multiscale_pool(p, pool_w, pool_scale):
    b, t, _ = p.shape
    pf = p.astype(jnp.float32)
    count = jnp.arange(1, t + 1, dtype=jnp.float32)[None, :, None]
    outs = []
    for gi, w in enumerate(POOL_WINDOWS):
        pg = pf[..., gi * POOL_GROUP:(gi + 1) * POOL_GROUP]
        cs = jnp.cumsum(pg, axis=1)
        lag = jnp.concatenate([jnp.zeros_like(cs[:, :w]), cs[:, :-w]], axis=1)
        mean = (cs - lag) / jnp.minimum(count, float(w))
        outs.append(jnp.einsum('btc,cd->btd', mean - pg, pool_w[gi].astype(jnp.float32)))
    return (jnp.concatenate(outs, axis=-1) * pool_scale.astype(jnp.float32)).astype(p.dtype)


def mixer_ab(h, w_in, q_norm, k_norm, sinks, pool_w, pool_scale, w_out):
    b, t, _ = h.shape
    z = h @ w_in
    q = z[..., :ATTN_WIDTH].reshape(b, t, N_Q_HEADS, HEAD_DIM)
    k = z[..., ATTN_WIDTH:ATTN_WIDTH + KV_WIDTH].reshape(b, t, N_KV_HEADS, HEAD_DIM)
    v = z[..., ATTN_WIDTH + KV_WIDTH:ATTN_WIDTH + 2 * KV_WIDTH].reshape(b, t, N_KV_HEADS, HEAD_DIM)
    p = z[..., ATTN_WIDTH + 2 * KV_WIDTH:]
    q = rms_norm(q, q_norm)
    k = rms_norm(k, k_norm)
    o_attn = sliding_window_attention(q, k, v, sinks).astype(h.dtype)
    o_pool = multiscale_pool(p, pool_w, pool_scale)
    return jnp.concatenate([o_attn, o_pool], axis=-1) @ w_out


def wkv7_scan(r, w, k, v, a, bvec):
    bsz, t, nh, n = r.shape

    def step(S, inp):
        r_t, w_t, k_t, v_t, a_t, b_t = inp
        sa = jnp.einsum('bhij,bhj->bhi', S, a_t)
        S = S * w_t[:, :, None, :] + sa[..., None] * b_t[:, :, None, :] + v_t[..., None] * k_t[:, :, None, :]
        return S, jnp.einsum('bhij,bhj->bhi', S, r_t)

    S0 = jnp.zeros((bsz, nh, n, n), jnp.float32)
    xs = tuple(jnp.swapaxes(z, 0, 1) for z in (r, w, k, v, a, bvec))
    _, y = lax.scan(step, S0, xs)
    return jnp.swapaxes(y, 0, 1)


def rwkv7_time_mix(h, mu, w_r, w_k, w_v, w0, w1, w2, a0, a1, a2, g1, g2,
                   k_k, k_a, r_k, lnx_w, lnx_b, w_o):
    b, t, d = h.shape
    xx = jnp.concatenate([jnp.zeros_like(h[:, :1]), h[:, :-1]], axis=1) - h
    xr, xw, xk, xv, xa, xg = (h + xx * mu[i] for i in range(N_SHIFT_MIX))
    r = xr @ w_r
    k = xk @ w_k
    v = xv @ w_v
    w = -jax.nn.softplus(-(w0 + jnp.tanh(xw @ w1) @ w2)) - 0.5
    a = jax.nn.sigmoid(a0 + (xa @ a1) @ a2)
    g = jax.nn.sigmoid(xg @ g1) @ g2

    def heads(z):
        return z.reshape(b, t, RWKV_HEADS, RWKV_HEAD).astype(jnp.float32)

    kk = heads(k * k_k)
    kk = kk * lax.rsqrt(jnp.maximum(jnp.sum(kk * kk, axis=-1, keepdims=True), 1e-24))
    k = k * (1 + (a - 1) * k_a)
    r_h, k_h, v_h, a_h = heads(r), heads(k), heads(v), heads(a)
    decay = jnp.exp(-jnp.exp(heads(w)))
    y = wkv7_scan(r_h, decay, k_h, v_h, -kk, kk * a_h)
    mean = jnp.mean(y, axis=-1, keepdims=True)
    var = jnp.mean(jnp.square(y - mean), axis=-1, keepdims=True)
    y = ((y - mean) * lax.rsqrt(var + GN_EPS)).reshape(b, t, d)
    y = y * lnx_w.astype(jnp.float32) + lnx_b.astype(jnp.float32)
    bonus = jnp.sum(r_h * k_h * r_k.astype(jnp.float32), axis=-1, keepdims=True) * v_h
    y = y + bonus.reshape(b, t, d)
    return (y * g.astype(jnp.float32)).astype(h.dtype) @ w_o


def setup_inputs(seed: int = 0) -> dict:
    key = jax.random.key(seed)
    ks = iter(jax.random.split(key, 40))
    f32 = jnp.float32

    def nrm(shape, scale):
        return jax.random.normal(next(ks), shape, f32) * scale

    def gain(shape):
        return 1.0 + nrm(shape, 0.02)

    def unif(shape, lo, hi):
        return jax.random.uniform(next(ks), shape, f32, lo, hi)

    D, F, NA, NC = D_MODEL, D_FF, N_A_LAYERS, N_C_LAYERS
    return {
        "x": nrm((BATCH, SEQ, D), 1.0),
        "ffn_norm": gain((DEPTH, 2, D)),
        "ffn_w_gate": nrm((DEPTH, 2, D, F), D ** -0.5),
        "ffn_w_up": nrm((DEPTH, 2, D, F), D ** -0.5),
        "ffn_w_down": nrm((DEPTH, 2, F, D), F ** -0.5),
        "ab_norm": gain((NA, D)),
        "ab_w_in": nrm((NA, D, AB_IN_WIDTH), D ** -0.5),
        "q_norm": gain((NA, HEAD_DIM)),
        "k_norm": gain((NA, HEAD_DIM)),
        "attn_sinks": nrm((NA, N_Q_HEADS), 1.0),
        "pool_w": nrm((NA, N_POOL_GROUPS, POOL_GROUP, POOL_GROUP), POOL_GROUP ** -0.5),
        "pool_scale": 0.5 + nrm((NA, POOL_WIDTH), 0.05),
        "ab_w_out": nrm((NA, AB_OUT_WIDTH, D), AB_OUT_WIDTH ** -0.5),
        "c_norm": gain((NC, D)),
        "c_mu": unif((NC, N_SHIFT_MIX, D), 0.0, 1.0),
        "c_w_r": nrm((NC, D, D), D ** -0.5),
        "c_w_k": nrm((NC, D, D), D ** -0.5),
        "c_w_v": nrm((NC, D, D), D ** -0.5),
        "c_w0": unif((NC, D), -5.0, 1.0),
        "c_w1": nrm((NC, D, DECAY_LORA), D ** -0.5),
        "c_w2": nrm((NC, DECAY_LORA, D), 0.1 * DECAY_LORA ** -0.5),
        "c_a0": nrm((NC, D), 0.1),
        "c_a1": nrm((NC, D, AAA_LORA), D ** -0.5),
        "c_a2": nrm((NC, AAA_LORA, D), 0.1 * AAA_LORA ** -0.5),
        "c_g1": nrm((NC, D, GATE_LORA), D ** -0.5),
        "c_g2": nrm((NC, GATE_LORA, D), GATE_LORA ** -0.5),
        "c_k_k": 0.85 + nrm((NC, D), 0.02),
        "c_k_a": 1.0 + nrm((NC, D), 0.02),
        "c_r_k": nrm((NC, RWKV_HEADS, RWKV_HEAD), 0.1),
        "c_lnx_w": gain((NC, D)),
        "c_lnx_b": nrm((NC, D), 0.02),
        "c_w_o": nrm((NC, D, D), D ** -0.5),
    }


def reference(x, ffn_norm, ffn_w_gate, ffn_w_up, ffn_w_down,
              ab_norm, ab_w_in, q_norm, k_norm, attn_sinks, pool_w, pool_scale, ab_w_out,
              c_norm, c_mu, c_w_r, c_w_k, c_w_v, c_w0, c_w1, c_w2, c_a0, c_a1, c_a2,
              c_g1, c_g2, c_k_k, c_k_a, c_r_k, c_lnx_w, c_lnx_b, c_w_o):
    for layer in range(DEPTH):
        x = x + 0.5 * swiglu(rms_norm(x, ffn_norm[layer, 0]), ffn_w_gate[layer, 0],
                             ffn_w_up[layer, 0], ffn_w_down[layer, 0])
        j = layer // 2
        if layer % 2 == 0:
            x = x + mixer_ab(rms_norm(x, ab_norm[j]), ab_w_in[j], q_norm[j], k_norm[j],
                             attn_sinks[j], pool_w[j], pool_scale[j], ab_w_out[j])
        else:
            x = x + rwkv7_time_mix(rms_norm(x, c_norm[j]), c_mu[j], c_w_r[j], c_w_k[j], c_w_v[j],
                                   c_w0[j], c_w1[j], c_w2[j], c_a0[j], c_a1[j], c_a2[j],
                                   c_g1[j], c_g2[j], c_k_k[j], c_k_a[j], c_r_k[j],
                                   c_lnx_w[j], c_lnx_b[j], c_w_o[j])
        x = x + 0.5 * swiglu(rms_norm(x, ffn_norm[layer, 1]), ffn_w_gate[layer, 1],
                             ffn_w_up[layer, 1], ffn_w_down[layer, 1])
    return x
```

```python
import contextlib
import os
import numpy as np
import concourse.bass as bass
import concourse.mybir as mybir
from concourse.bass_utils import run_bass_kernel_spmd

F32 = mybir.dt.float32
BF16 = mybir.dt.bfloat16
ALU = mybir.AluOpType
AF = mybir.ActivationFunctionType
AX = mybir.AxisListType

D = 1024
DC = 8
DFF = 2816
FC = 22
TT = 512
NB = TT // 128
SEQ = 8192
RMS_EPS = 1e-6
GN_EPS = 64e-5
NEG_E = -float(np.exp(-0.5))
ENGS = ["pe", "act", "dve", "pool", "sp"]
SEM_CH = int(os.environ.get("SEM_CH", "12000"))
ARENA_KB = 106
RWDBG = float(os.environ.get('RWDBG', '9'))
NG_RING = 4


class Tk:
    __slots__ = ("name", "w", "r", "excl")

    def __init__(self, name, excl=False):
        self.name = name
        self.w = {}
        self.r = {}
        self.excl = excl


class Buf:
    __slots__ = ("ap", "tks")

    def __init__(self, ap, tks):
        self.ap = ap
        self.tks = tks


class Slot:
    def __init__(self, key, group=False):
        self.key = key
        self.count = 0
        self.sem = None
        self.group = group


class Op:
    __slots__ = ("eng", "fn", "deps", "tok", "slot", "snap", "waits", "signal", "sig")

    def __init__(self, eng, fn, deps, tok, slot):
        self.eng = eng
        self.fn = fn
        self.deps = deps
        self.tok = tok
        self.slot = slot
        self.snap = None
        self.waits = ()
        self.signal = False
        self.sig = None


class Ctx:
    def __init__(self, nc):
        self.nc = nc
        self.oplist = []
        self.eng_ops = {e: [] for e in ENGS}
        self.tokmap = {}
        self.slots = []
        self.es = contextlib.ExitStack()
        self.out_slots = []

    def sb(self, name, shape, dt):
        return self.es.enter_context(self.nc.sbuf_tensor("sb_" + name, list(shape), dt))

    def ps(self, name, shape, dt=F32):
        return self.es.enter_context(self.nc.psum_tensor("ps_" + name, list(shape), dt))

    def slot(self, name, group=False):
        s = Slot(("dma", name), group)
        self.slots.append(s)
        return s

    def emit(self, eng, fn, reads=(), writes=(), slot=None):
        if any(t.excl for t in reads):
            writes = list(writes) + [t for t in reads if t.excl]
            reads = [t for t in reads if not t.excl]
        deps = {}
        for t in reads:
            for k, v in t.w.items():
                if deps.get(k, -1) < v:
                    deps[k] = v
        for t in writes:
            for k, v in t.w.items():
                if deps.get(k, -1) < v:
                    deps[k] = v
            for k, v in t.r.items():
                if deps.get(k, -1) < v:
                    deps[k] = v
        if eng == "pe":
            deps.pop("pe", None)
        if slot is not None:
            slot.count += 1
            tok = (slot.key, slot.count)
        else:
            tok = (eng, len(self.eng_ops[eng]))
        op = Op(eng, fn, deps, tok, slot)
        self.oplist.append(op)
        self.eng_ops[eng].append(op)
        self.tokmap[tok] = op
        k, v = tok
        for t in reads:
            if t.r.get(k, -1) < v:
                t.r[k] = v
        for t in writes:
            t.w = {k: v}
            t.r = {}
        return tok

    def finalize(self):
        nc = self.nc
        known = {e: {} for e in ENGS}
        for op in self.oplist:
            kn = known[op.eng]
            waits = []
            copied = False
            for k, v in op.deps.items():
                if kn.get(k, -1) >= v:
                    continue
                if not copied:
                    kn = dict(kn)
                    known[op.eng] = kn
                    copied = True
                waits.append((k, v))
                prod = self.tokmap[(k, v)]
                prod.signal = True
                for kk, vv in prod.snap.items():
                    if kn.get(kk, -1) < vv:
                        kn[kk] = vv
                kn[k] = v
            op.waits = waits
            op.snap = kn
        self.eng_sems = {e: [] for e in ENGS}
        for e in ENGS:
            n = 0
            for op in self.eng_ops[e]:
                if op.slot is None and op.signal:
                    op.sig = n
                    n += 1
            nch = (n + SEM_CH - 1) // SEM_CH
            for c in range(nch):
                self.eng_sems[e].append(self.es.enter_context(nc.semaphore(f"s_{e}_{c}")))
        for s in self.slots:
            if s.count > 0:
                s.sem = self.es.enter_context(nc.semaphore("d_" + s.key[1]))

        def resolve(k, v):
            if isinstance(k, tuple):
                s = self.tokmap[(k, v)].slot
                return s.sem, 16 * (s.count if s.group else v)
            sig = self.tokmap[(k, v)].sig
            return self.eng_sems[k][sig // SEM_CH], sig % SEM_CH + 1

        ctx = self

        def run_engine(ename):
            def body(e):
                for op in ctx.eng_ops[ename]:
                    ws = [resolve(k, v) for (k, v) in op.waits]
                    if op.slot is not None:
                        for (s, v) in ws:
                            e.wait_ge(s, v)
                        ins = op.fn(e)
                        ins.then_inc(op.slot.sem, 16)
                    else:
                        for (s, v) in ws[:-1]:
                            e.wait_ge(s, v)
                        ins = op.fn(e)
                        if ws:
                            ins._wait_ge(*ws[-1])
                        if op.signal:
                            ins.then_inc(ctx.eng_sems[ename][op.sig // SEM_CH], 1)
                if ename == "sp":
                    for s in ctx.out_slots:
                        if s.count:
                            e.wait_ge(s.sem, 16 * s.count)
            return body

        with nc.Block() as block:
            block.tensor(run_engine("pe"))
            block.scalar(run_engine("act"))
            block.vector(run_engine("dve"))
            block.gpsimd(run_engine("pool"))
            block.sync(run_engine("sp"))
        self.stats = {e: len(self.eng_ops[e]) for e in ENGS}
        self.stats["waits"] = sum(len(op.waits) for op in self.oplist)
        self.stats["sems"] = sum(len(v) for v in self.eng_sems.values()) + sum(1 for s in self.slots if s.count)


CV_FFN = 0
CV_AB = 32
CV_C = 40
CV_QN = 48
CV_KN = 49
CV_PS = 50
CV_MU = 54
CV_SINK = 102
CV_LNW = 110
CV_LNB = 118
CV_GNEPS = 126
CV_EPS = 127
NCOL = 128
CM_IDENT = 0
CM_ONES = 128
CM_BD = 256
CM_AM = 384
CM_INVC = CM_AM + 2048
CM_TRIS = CM_INVC + 64
CM_ONESS = CM_TRIS + 128
CM_SU = CM_ONESS + 128
CM_UI = CM_SU + 128
CM_SL = CM_UI + 128
NCMAT = CM_SL + 256


def host_consts():
    cm = np.zeros((128, NCMAT), np.float32)
    cm[:, CM_IDENT:CM_IDENT + 128] = np.eye(128, dtype=np.float32)
    cm[:, CM_ONES:CM_ONES + 128] = 1.0
    cm[0:64, CM_BD:CM_BD + 64] = 1.0
    cm[64:128, CM_BD + 64:CM_BD + 128] = 1.0
    kk = np.arange(128)[:, None].astype(np.float64)
    qq = np.arange(128)[None, :].astype(np.float64)
    am = np.zeros((128, 2, 2, 4, 128), np.float64)
    for hk in range(2):
        for g in range(4):
            hq = hk * 4 + g
            slope = 2.0 ** (-8.0 * (hq + 1) / 8.0)
            dist_cur = qq - kk
            am[:, hk, 1, g, :] = np.where(dist_cur >= 0, np.exp(-slope * dist_cur), 0.0)
            dist_prev = qq - kk + 128
            am[:, hk, 0, g, :] = np.where(dist_prev < 128, np.exp(-slope * dist_prev), 0.0)
    cm[:, CM_AM:CM_AM + 2048] = am.reshape(128, 2048).astype(np.float32)
    invc = np.zeros((4, 16), np.float64)
    for gi, w in enumerate((2, 4, 8, 16)):
        invc[gi] = 1.0 / np.minimum(np.arange(1, 17), w)
    cm[:, CM_INVC:CM_INVC + 64] = invc.reshape(1, 64).astype(np.float32)
    s_ = np.arange(128)
    cm[:, CM_TRIS:CM_TRIS + 128] = NEG_E * (s_[:, None] <= s_[None, :]).astype(np.float32)
    cm[:, CM_ONESS:CM_ONESS + 128] = NEG_E
    cm[:, CM_SU:CM_SU + 128] = (s_[:, None] < s_[None, :]).astype(np.float32)
    cm[:, CM_UI:CM_UI + 128] = (s_[:, None] <= s_[None, :]).astype(np.float32)
    sl = (s_[:, None] > s_[None, :]).astype(np.float32)
    cm[:, CM_SL:CM_SL + 128] = sl
    cm[:, CM_SL + 128:CM_SL + 256] = sl
    return cm


def host_colvec(inp):
    cv = np.zeros((128, NCOL), np.float32)
    cv[:, CV_EPS] = RMS_EPS
    cv[:, CV_GNEPS] = GN_EPS
    fn = np.asarray(inp["ffn_norm"], np.float32).reshape(4, DC, 128)
    for li in range(4):
        cv[:, CV_FFN + li * 8:CV_FFN + li * 8 + 8] = fn[li].T
    cv[:, CV_AB:CV_AB + 8] = np.asarray(inp["ab_norm"], np.float32).reshape(DC, 128).T
    cv[:, CV_C:CV_C + 8] = np.asarray(inp["c_norm"], np.float32).reshape(DC, 128).T
    cv[:, CV_QN] = np.tile(np.asarray(inp["q_norm"], np.float32).reshape(64), 2)
    cv[:, CV_KN] = np.tile(np.asarray(inp["k_norm"], np.float32).reshape(64), 2)
    cv[:, CV_PS:CV_PS + 4] = np.asarray(inp["pool_scale"], np.float32).reshape(4, 128).T
    cv[:, CV_SINK:CV_SINK + 8] = np.asarray(inp["attn_sinks"], np.float32).reshape(1, 8)
    mu = np.asarray(inp["c_mu"], np.float32).reshape(6, DC, 128)
    for i in range(6):
        cv[:, CV_MU + i * 8:CV_MU + i * 8 + 8] = mu[i].T
    cv[:, CV_LNW:CV_LNW + 8] = np.asarray(inp["c_lnx_w"], np.float32).reshape(DC, 128).T
    cv[:, CV_LNB:CV_LNB + 8] = np.asarray(inp["c_lnx_b"], np.float32).reshape(DC, 128).T
    return cv


class Builder:
    def __init__(self, T, stages):
        self.T = T
        self.NT = T // TT
        self.stages = stages
        nc = bass.Bass("TRN2", target_bir_lowering=False)
        self.nc = nc
        self.c = Ctx(nc)

    def dram_in(self, name, shape):
        return self.nc.dram_tensor(name, list(shape), F32, kind="ExternalInput").ap()

    def E(self, eng, fn, reads=(), writes=()):
        self.c.emit(eng, fn, reads=reads, writes=writes)

    def ar_reset(self):
        self.ar_ptr = 0

    def ar_alloc(self, free_shape, dt):
        n = 1
        for s_ in free_shape:
            n *= s_
        nbytes = n * (2 if dt == BF16 else 4)
        nb = (nbytes + 255) // 256
        b0 = self.ar_ptr
        self.ar_ptr += nb
        assert self.ar_ptr <= ARENA_KB * 4, f"arena overflow {self.ar_ptr}"
        self.ar_peak = max(getattr(self, "ar_peak", 0), self.ar_ptr)
        ap = self.arena[:, b0 * 64:b0 * 64 + (nbytes + 3) // 4]
        if dt == BF16:
            ap = ap.bitcast(BF16)
        if len(free_shape) == 2:
            ap = ap.rearrange("p (a b) -> p a b", a=free_shape[0])
        elif len(free_shape) == 3:
            ap = ap.rearrange("p (a b c) -> p a b c", a=free_shape[0], b=free_shape[1])
        return Buf(ap, self.ar_t[b0:b0 + nb])

    def bank(self, b):
        return self.psb[b], [self.psh_t[b]]

    def bk(self, b, lo, hi):
        return self.psb[b][:, lo:hi], [self.psh_t[b]]

    def gload(self, src):
        i = self.gcount % NG_RING
        self.gcount += 1
        buf, t, sl = self.gring[i], self.gring_t[i], self.gring_s[i]
        assert (not t.w) or t.r, "G ring buffer overwritten before it was consumed"
        self.c.emit("pool", lambda e: e.dma_start(out=buf[:].rearrange("p (c n) -> p c n", c=DC), in_=src),
                    writes=[t], slot=sl)
        return buf[:].rearrange("p (c n) -> p c n", c=DC), t

    def dload(self, src):
        i = self.dcount % 2
        self.dcount += 1
        buf, t, sl = self.dring[i], self.dring_t[i], self.dring_s[i]
        assert (not t.w) or t.r, "D ring buffer overwritten before it was consumed"
        self.c.emit("pool", lambda e: e.dma_start(out=buf[:].rearrange("p (c n) -> p c n", c=FC), in_=src),
                    writes=[t], slot=sl)
        return buf[:].rearrange("p (c n) -> p c n", c=FC), t

    def lazy(self, srcs, ahead):
        items = [None] * len(srcs)
        state = {"next": 0}

        def get(i):
            while state["next"] <= min(i + ahead, len(srcs) - 1):
                j = state["next"]
                items[j] = self.gload(srcs[j])
                state["next"] += 1
            return items[i]
        return get

    def build(self):
        nc, c = self.nc, self.c
        T = self.T
        st = self.stages
        self.x = self.dram_in("x", [T, D])
        self.y = nc.dram_tensor("y", [T, D], F32, kind="ExternalOutput").ap()
        self.colvec_d = self.dram_in("colvec", [128, NCOL])
        self.cmat_d = self.dram_in("cmat", [128, NCMAT])
        need_ffn = any(s.startswith("ffn") for s in st)
        if need_ffn:
            self.wg = self.dram_in("wg", [4, D, DFF])
            self.wu = self.dram_in("wu", [4, D, DFF])
            self.wd = self.dram_in("wd", [4, DFF, D])
        self.alloc_common()
        self.setup_consts()
        if "ab" in st:
            self.setup_ab()
        if "rwkv" in st:
            self.setup_rwkv()
        for ti in range(self.NT):
            self.load_x(ti)
            for s in st:
                self.run_stage(ti, s)
            self.store_x(ti)
        c.finalize()
        return nc

    def alloc_common(self):
        c = self.c
        self.colvec = c.sb("colvec", [128, NCOL], F32)
        self.colvec_t = Tk("colvec")
        self.cmat = c.sb("cmat", [128, NCMAT], F32)
        self.cmat_t = Tk("cmat")
        self.cbf = c.sb("cbf", [128, 512], BF16)
        self.cbf_t = Tk("cbf")
        self.xT = c.sb("xT", [128, DC, TT], F32)
        self.xT_t = [Tk(f"xT{i}") for i in range(DC)]
        self.xin_slot = c.slot("xin")
        self.xout_slot = c.slot("xout")
        c.out_slots.append(self.xout_slot)
        self.h = c.sb("h", [128, DC, TT + 1], BF16)
        self.h_t = [Tk(f"h{i}") for i in range(DC)]
        self.arena = c.sb("arena", [128, ARENA_KB * 256], F32)
        self.ar_t = [Tk(f"ar{i}") for i in range(ARENA_KB * 4)]
        self.ar_ptr = 0
        self.psb = [c.ps(f"psb{i}", [128, 512], F32) for i in range(8)]
        self.psh_t = [Tk(f"psb{i}", excl=True) for i in range(8)]
        self.const_slot = c.slot("const", group=True)
        self.gring = [c.sb(f"gring{i}", [128, 2048], BF16) for i in range(NG_RING)]
        self.gring_t = [Tk(f"gring{i}") for i in range(NG_RING)]
        self.gring_s = [c.slot(f"gring{i}") for i in range(NG_RING)]
        self.dring = [c.sb(f"dring{i}", [128, FC * 128], BF16) for i in range(2)]
        self.dring_t = [Tk(f"dring{i}") for i in range(2)]
        self.dring_s = [c.slot(f"dring{i}") for i in range(2)]
        self.gcount = 0
        self.dcount = 0

    def setup_consts(self):
        c = self.c
        cv, cm = self.colvec, self.cmat
        c.emit("sp", lambda e: e.dma_start(out=cv[:], in_=self.colvec_d[:, :]),
               writes=[self.colvec_t], slot=self.const_slot)
        c.emit("sp", lambda e: e.dma_start(out=cm[:], in_=self.cmat_d[:, :]),
               writes=[self.cmat_t], slot=self.const_slot)
        cb = self.cbf
        self.E("dve", lambda e: e.tensor_copy(cb[:, 0:128], cm[:, CM_ONES:CM_ONES + 128]),
               [self.cmat_t], [self.cbf_t])
        self.E("dve", lambda e: e.tensor_copy(cb[:, 128:256], cm[:, CM_IDENT:CM_IDENT + 128]),
               [self.cmat_t, self.cbf_t], [self.cbf_t])
        self.E("dve", lambda e: e.tensor_copy(cb[:, 256:384], cm[:, CM_BD:CM_BD + 128]),
               [self.cmat_t, self.cbf_t], [self.cbf_t])
        self.E("dve", lambda e: e.tensor_copy(cb[:, 384:512], cm[:, CM_UI:CM_UI + 128]),
               [self.cmat_t, self.cbf_t], [self.cbf_t])
        self.E("dve", lambda e: e.memset(self.h[:], 0.0), [], self.h_t)

    def load_x(self, ti):
        c = self.c
        self.ar_reset()
        xin = self.ar_alloc([NB, D], F32)
        src = self.x[ti * TT:(ti + 1) * TT, :].rearrange("(b p) d -> p b d", p=128)
        c.emit("sp", lambda e: e.dma_start(out=xin.ap, in_=src), writes=xin.tks, slot=self.xin_slot)
        ident = self.cmat[:, CM_IDENT:CM_IDENT + 128]
        for dc in range(DC):
            ps, pst = self.bank(dc % 2)
            for b in range(NB):
                self.E("pe", lambda e, b=b, dc=dc, ps=ps: e.transpose(
                    ps[:, b * 128:(b + 1) * 128], xin.ap[:, b, dc * 128:(dc + 1) * 128], ident),
                    xin.tks + [self.cmat_t], pst)
            if dc % 2 == 0:
                self.E("act", lambda e, dc=dc, ps=ps: e.copy(self.xT[:, dc, :], ps[:]), pst, [self.xT_t[dc]])
            else:
                self.E("dve", lambda e, dc=dc, ps=ps: e.tensor_copy(self.xT[:, dc, :], ps[:]), pst, [self.xT_t[dc]])

    def store_x(self, ti):
        c = self.c
        self.ar_reset()
        xo = self.ar_alloc([NB, D], F32)
        ident = self.cmat[:, CM_IDENT:CM_IDENT + 128]
        k = 0
        for b in range(NB):
            for half in range(2):
                ps, pst = self.bank(k % 2)
                k += 1
                for q in range(4):
                    dc = half * 4 + q
                    self.E("pe", lambda e, b=b, dc=dc, q=q, ps=ps: e.transpose(
                        ps[:, q * 128:(q + 1) * 128], self.xT[:, dc, b * 128:(b + 1) * 128], ident),
                        [self.xT_t[dc], self.cmat_t], pst)
                if k % 2 == 0:
                    self.E("act", lambda e, b=b, half=half, ps=ps: e.copy(xo.ap[:, b, half * 512:(half + 1) * 512], ps[:]),
                           pst, xo.tks)
                else:
                    self.E("dve", lambda e, b=b, half=half, ps=ps: e.tensor_copy(xo.ap[:, b, half * 512:(half + 1) * 512], ps[:]),
                           pst, xo.tks)
        dst = self.y[ti * TT:(ti + 1) * TT, :].rearrange("(b p) d -> p b d", p=128)
        c.emit("sp", lambda e: e.dma_start(out=dst, in_=xo.ap), reads=xo.tks, slot=self.xout_slot)

    def rmsnorm(self, gcol):
        ones = self.cbf[:, 0:128]
        sq = [self.ar_alloc([TT], BF16) for _ in range(2)]
        rstd = self.ar_alloc([TT], F32)
        ps, pst = self.bank(2)
        for dc in range(DC):
            s_ = sq[dc % 2]
            self.E("act", lambda e, dc=dc, s_=s_: e.activation(s_.ap, self.xT[:, dc, :], AF.Square),
                   [self.xT_t[dc]], s_.tks)
            self.E("pe", lambda e, dc=dc, s_=s_: e.matmul(ps[:], ones, s_.ap, start=(dc == 0), stop=(dc == DC - 1)),
                   s_.tks + [self.cbf_t], pst)
        self.E("act", lambda e: e.activation(rstd.ap, ps[:], AF.Sqrt, bias=self.colvec[:, CV_EPS:CV_EPS + 1], scale=1.0 / D),
               pst + [self.colvec_t], rstd.tks)
        self.E("dve", lambda e: e.reciprocal(rstd.ap, rstd.ap), rstd.tks, rstd.tks)
        for dc in range(DC):
            self.E("dve", lambda e, dc=dc: e.scalar_tensor_tensor(
                self.h[:, dc, 1:TT + 1], self.xT[:, dc, :], self.colvec[:, gcol + dc:gcol + dc + 1], rstd.ap,
                ALU.mult, ALU.mult),
                [self.xT_t[dc], self.colvec_t] + rstd.tks, [self.h_t[dc]])

    def hv(self, dc, lo=0, hi=TT):
        return self.h[:, dc, 1 + lo:1 + hi]

    def ffn(self, li):
        self.ar_reset()
        wgv = self.wg[li].rearrange("(c p) f -> p c f", p=128)
        wuv = self.wu[li].rearrange("(c p) f -> p c f", p=128)
        wdv = self.wd[li].rearrange("(c p) n -> p c n", p=128)
        NG = FC // 2
        q = []

        def issue(g):
            q.append((self.gload(wgv[:, :, g * 256:(g + 1) * 256]), self.gload(wuv[:, :, g * 256:(g + 1) * 256])))
        issue(0)
        dq = [self.dload(wdv[:, :, 0:128])]
        self.rmsnorm(CV_FFN + li * 8)
        act = [self.ar_alloc([TT], BF16) for _ in range(FC)]
        silu = [self.ar_alloc([TT], F32) for _ in range(2)]
        for g in range(NG):
            if g + 1 < NG:
                issue(g + 1)
            (gb, gt), (ub, ut) = q.pop(0)
            for fc in range(2):
                f = g * 2 + fc
                psg, psgt = self.bank(3 + (f % 2))
                psu, psut = self.bank(5 + (f % 2))
                for dc in range(DC):
                    self.E("pe", lambda e, dc=dc, fc=fc, gb=gb, psg=psg: e.matmul(
                        psg[:], gb[:, dc, fc * 128:(fc + 1) * 128], self.hv(dc), start=(dc == 0), stop=(dc == DC - 1)),
                        [gt, self.h_t[dc]], psgt)
                for dc in range(DC):
                    self.E("pe", lambda e, dc=dc, fc=fc, ub=ub, psu=psu: e.matmul(
                        psu[:], ub[:, dc, fc * 128:(fc + 1) * 128], self.hv(dc), start=(dc == 0), stop=(dc == DC - 1)),
                        [ut, self.h_t[dc]], psut)
                st_ = silu[f % 2]
                self.E("act", lambda e, st_=st_, psg=psg: e.activation(st_.ap, psg[:], AF.Silu), psgt, st_.tks)
                self.E("dve", lambda e, st_=st_, psu=psu, f=f: e.tensor_tensor(act[f].ap, st_.ap, psu[:], ALU.mult),
                       st_.tks + psut, act[f].tks)
        for dc in range(DC):
            if dc + 1 < DC:
                dq.append(self.dload(wdv[:, :, (dc + 1) * 128:(dc + 2) * 128]))
            db, dt_ = dq.pop(0)
            psy, psyt = self.bank(7 if dc % 2 == 0 else 2)
            for f in range(FC):
                self.E("pe", lambda e, f=f, db=db, psy=psy: e.matmul(
                    psy[:], db[:, f, :], act[f].ap, start=(f == 0), stop=(f == FC - 1)),
                    [dt_] + act[f].tks, psyt)
            self.E("dve", lambda e, dc=dc, psy=psy: e.scalar_tensor_tensor(
                self.xT[:, dc, :], psy[:], 0.5, self.xT[:, dc, :], ALU.mult, ALU.add),
                psyt + [self.xT_t[dc]], [self.xT_t[dc]])

    def setup_ab(self):
        c = self.c
        self.w_in_d = self.dram_in("w_in", [D, 1280])
        self.w_out_d = self.dram_in("w_out", [D, D])
        self.pool_w_d = self.dram_in("pool_w", [4, 128, 128])
        self.pw = c.sb("pw", [128, 4, 128], BF16)
        self.pw_t = Tk("pw")
        self.abw_slot = c.slot("abw", group=True)
        self.kcar = c.sb("kcar", [128, 128], BF16)
        self.kcar_t = Tk("kcar")
        self.vcar = c.sb("vcar", [128, 128], BF16)
        self.vcar_t = Tk("vcar")
        self.pcar = c.sb("pcar", [128, 4, 16], F32)
        self.pcar_t = Tk("pcar")
        self.esk = c.sb("esink", [128, 512], F32)
        self.esb = c.sb("esb", [128, 8], F32)
        self.esk_t = Tk("esink")
        src4 = self.pool_w_d.rearrange("g c d -> c g d")
        c.emit("pool", lambda e: e.dma_start(out=self.pw[:], in_=src4), writes=[self.pw_t], slot=self.abw_slot)
        self.E("act", lambda e: e.activation(self.esb[:], self.colvec[:, CV_SINK:CV_SINK + 8], AF.Exp),
               [self.colvec_t], [self.esk_t])
        for hk in range(2):
            for g in range(4):
                hq = hk * 4 + g
                self.E("dve", lambda e, hk=hk, g=g, hq=hq: e.tensor_copy(
                    self.esk[hk * 64:(hk + 1) * 64, g * 128:(g + 1) * 128],
                    self.esb[hk * 64:(hk + 1) * 64, hq:hq + 1].to_broadcast([64, 128])),
                    [self.esk_t], [self.esk_t])
        self.E("dve", lambda e: e.memset(self.pcar[:], 0.0), [], [self.pcar_t])
        self.E("dve", lambda e: e.memset(self.kcar[:], 0.0), [], [self.kcar_t])
        self.E("dve", lambda e: e.memset(self.vcar[:], 0.0), [], [self.vcar_t])

    def ab(self, ti):
        self.ar_reset()
        bd = self.cbf[:, 256:384]
        ones = self.cbf[:, 0:128]
        winv = self.w_in_d.rearrange("(c p) n -> p c n", p=128)
        woutv = self.w_out_d.rearrange("(c p) n -> p c n", p=128)
        wq = self.lazy([winv[:, :, i * 256:(i + 1) * 256] for i in range(5)], 1)
        wq(0)
        self.rmsnorm(CV_AB)
        qn = [self.ar_alloc([TT], BF16) for _ in range(4)]
        kn = self.ar_alloc([128 + TT], BF16)
        vtm = self.ar_alloc([NB + 1, 128], BF16)
        qraw = [self.ar_alloc([TT], F32) for _ in range(2)]
        sqb = [self.ar_alloc([TT], BF16) for _ in range(2)]
        rstd2 = self.ar_alloc([TT], F32)
        W = 16 + TT
        pbuf = [self.ar_alloc([W], F32) for _ in range(4)]
        ptmp = [self.ar_alloc([W], F32) for _ in range(2)]
        pdiff = [self.ar_alloc([TT], BF16) for _ in range(4)]
        opool = [self.ar_alloc([TT], BF16) for _ in range(4)]
        oT = self.ar_alloc([4, TT], BF16)
        ebuf = [self.ar_alloc([512], F32) for _ in range(2)]
        pt = [[self.ar_alloc([512], BF16) for _ in range(2)] for _ in range(2)]
        rden = [self.ar_alloc([512], F32) for _ in range(2)]
        self.E("dve", lambda e: e.tensor_copy(kn.ap[:, 0:128], self.kcar[:]), [self.kcar_t], kn.tks)
        self.E("dve", lambda e: e.tensor_copy(vtm.ap[:, 0, :], self.vcar[:]), [self.vcar_t], vtm.tks)
        for gi in range(4):
            self.E("dve", lambda e, gi=gi: e.tensor_copy(pbuf[gi].ap[:, 0:16], self.pcar[:, gi, :]), [self.pcar_t], pbuf[gi].tks)
        for ci in range(5):
            wb, wt = wq(ci // 2)
            n0 = (ci % 2) * 128
            ps, pst = self.bank(ci % 2)
            for dc in range(DC):
                self.E("pe", lambda e, dc=dc, n0=n0, ps=ps, wb=wb: e.matmul(
                    ps[:], wb[:, dc, n0:n0 + 128], self.hv(dc), start=(dc == 0), stop=(dc == DC - 1)),
                    [wt, self.h_t[dc]], pst)
            qr, s_ = qraw[ci % 2], sqb[ci % 2]
            ps2, ps2t = self.bank(2)
            self.E("act", lambda e, qr=qr, ps=ps: e.copy(qr.ap, ps[:]), pst, qr.tks)
            self.E("act", lambda e, s_=s_, ps=ps: e.activation(s_.ap, ps[:], AF.Square), pst, s_.tks)
            self.E("pe", lambda e, s_=s_, ps2=ps2: e.matmul(ps2[:], bd, s_.ap, start=True, stop=True),
                   s_.tks + [self.cbf_t], ps2t)
            self.E("act", lambda e, ps2=ps2: e.activation(rstd2.ap, ps2[:], AF.Sqrt,
                                                          bias=self.colvec[:, CV_EPS:CV_EPS + 1], scale=1.0 / 64),
                   ps2t + [self.colvec_t], rstd2.tks)
            self.E("dve", lambda e: e.reciprocal(rstd2.ap, rstd2.ap), rstd2.tks, rstd2.tks)
            if ci < 4:
                self.E("dve", lambda e, ci=ci, qr=qr: e.scalar_tensor_tensor(
                    qn[ci].ap, qr.ap, self.colvec[:, CV_QN:CV_QN + 1], rstd2.ap, ALU.mult, ALU.mult),
                    qr.tks + rstd2.tks + [self.colvec_t], qn[ci].tks)
            else:
                self.E("dve", lambda e, qr=qr: e.scalar_tensor_tensor(
                    kn.ap[:, 128:128 + TT], qr.ap, self.colvec[:, CV_KN:CV_KN + 1], rstd2.ap, ALU.mult, ALU.mult),
                    qr.tks + rstd2.tks + [self.colvec_t], kn.tks)
        wb, wt = wq(2)
        ps, pst = self.bank(1)
        for b in range(NB):
            for dc in range(DC):
                self.E("pe", lambda e, dc=dc, b=b, ps=ps, wb=wb: e.matmul(
                    ps[:, b * 128:(b + 1) * 128], self.hv(dc, b * 128, (b + 1) * 128), wb[:, dc, 128:256],
                    start=(dc == 0), stop=(dc == DC - 1)),
                    [wt, self.h_t[dc]], pst)
        self.E("act", lambda e, ps=ps: e.copy(vtm.ap[:, 1:NB + 1, :], ps[:].rearrange("p (b n) -> p b n", b=NB)),
               pst, vtm.tks)
        for gi in range(4):
            wb, wt = wq(3 + gi // 2)
            n0 = (gi % 2) * 128
            ps, pst = self.bank(gi % 2)
            for dc in range(DC):
                self.E("pe", lambda e, dc=dc, n0=n0, ps=ps, wb=wb: e.matmul(
                    ps[:], wb[:, dc, n0:n0 + 128], self.hv(dc), start=(dc == 0), stop=(dc == DC - 1)),
                    [wt, self.h_t[dc]], pst)
            self.E("act", lambda e, gi=gi, ps=ps: e.copy(pbuf[gi].ap[:, 16:16 + TT], ps[:]), pst, pbuf[gi].tks)
        woq = self.lazy([woutv[:, :, i * 256:(i + 1) * 256] for i in range(4)], 1)
        woq(0)
        for gi, w in enumerate((2, 4, 8, 16)):
            cur = pbuf[gi]
            lo = 0
            k = 0
            step = 1
            while step < w:
                nxt = ptmp[k % 2]
                lo += step
                self.E("dve", lambda e, cur=cur, nxt=nxt, lo=lo, step=step: e.tensor_tensor(
                    nxt.ap[:, lo:W], cur.ap[:, lo:W], cur.ap[:, lo - step:W - step], ALU.add),
                    cur.tks, nxt.tks)
                cur = nxt
                step *= 2
                k += 1
            self.E("dve", lambda e, cur=cur, gi=gi, w=w: e.scalar_tensor_tensor(
                pdiff[gi].ap, cur.ap[:, 16:W], 1.0 / w, pbuf[gi].ap[:, 16:W], ALU.mult, ALU.subtract),
                cur.tks + pbuf[gi].tks, pdiff[gi].tks)
            if ti == 0:
                tmp = qraw[0]
                self.E("dve", lambda e, cur=cur, gi=gi, tmp=tmp: e.tensor_tensor(
                    tmp.ap[:, 0:16], cur.ap[:, 16:32], self.cmat[:, CM_INVC + gi * 16:CM_INVC + gi * 16 + 16], ALU.mult),
                    cur.tks + [self.cmat_t], tmp.tks)
                self.E("dve", lambda e, gi=gi, tmp=tmp: e.tensor_tensor(
                    pdiff[gi].ap[:, 0:16], tmp.ap[:, 0:16], pbuf[gi].ap[:, 16:32], ALU.subtract),
                    tmp.tks + pbuf[gi].tks, pdiff[gi].tks)
            self.E("dve", lambda e, gi=gi: e.tensor_copy(self.pcar[:, gi, :], pbuf[gi].ap[:, TT:TT + 16]),
                   pbuf[gi].tks, [self.pcar_t])
            ps, pst = self.bank(gi % 2)
            self.E("pe", lambda e, gi=gi, ps=ps: e.matmul(ps[:], self.pw[:, gi, :], pdiff[gi].ap, start=True, stop=True),
                   [self.pw_t] + pdiff[gi].tks, pst)
            self.E("act", lambda e, gi=gi, ps=ps: e.activation(opool[gi].ap, ps[:], AF.Copy,
                                                               scale=self.colvec[:, CV_PS + gi:CV_PS + gi + 1]),
                   pst + [self.colvec_t], opool[gi].tks)
        for b in range(NB):
            gblk = ti * NB + b
            for hk in range(2):
                p0 = hk * 64
                kbs = [0, 1] if gblk > 0 else [1]
                ptb = pt[hk]
                for kb in kbs:
                    ps, pst = self.bank(3 + 2 * hk + kb)
                    for g in range(4):
                        self.E("pe", lambda e, ps=ps, p0=p0, b=b, kb=kb, g=g: e.matmul(
                            ps[:, g * 128:(g + 1) * 128],
                            kn.ap[p0:p0 + 64, (b + kb) * 128:(b + kb + 1) * 128],
                            qn[g].ap[p0:p0 + 64, b * 128:(b + 1) * 128], start=True, stop=True),
                            kn.tks + qn[g].tks, pst)
                    eb = ebuf[kb]
                    self.E("act", lambda e, eb=eb, ps=ps: e.activation(eb.ap, ps[:], AF.Exp, scale=0.125), pst, eb.tks)
                    m0 = CM_AM + (hk * 2 + kb) * 512
                    self.E("dve", lambda e, eb=eb, ptb=ptb, kb=kb, m0=m0: e.tensor_tensor(
                        ptb[kb].ap, eb.ap, self.cmat[:, m0:m0 + 512], ALU.mult),
                        eb.tks + [self.cmat_t], ptb[kb].tks)
                pv, pvt = self.bank(7 if hk == 0 else 0)
                den, dent = self.bank(2 if hk == 0 else 1)
                for i, kb in enumerate(kbs):
                    self.E("pe", lambda e, i=i, kb=kb, b=b, ptb=ptb, pv=pv, n=len(kbs): e.matmul(
                        pv[:, :], vtm.ap[:, b + kb, :], ptb[kb].ap, start=(i == 0), stop=(i == n - 1)),
                        vtm.tks + ptb[kb].tks, pvt)
                for i, kb in enumerate(kbs):
                    self.E("pe", lambda e, i=i, kb=kb, ptb=ptb, den=den, n=len(kbs): e.matmul(
                        den[:, :], ones, ptb[kb].ap, start=(i == 0), stop=(i == n - 1)),
                        [self.cbf_t] + ptb[kb].tks, dent)
                rd = rden[hk]
                self.E("dve", lambda e, rd=rd, den=den, p0=p0: e.tensor_tensor(
                    rd.ap[p0:p0 + 64, :], den[p0:p0 + 64, :], self.esk[p0:p0 + 64, :], ALU.add),
                    dent + [self.esk_t], rd.tks)
                self.E("dve", lambda e, rd=rd, p0=p0: e.reciprocal(rd.ap[p0:p0 + 64, :], rd.ap[p0:p0 + 64, :]), rd.tks, rd.tks)
                self.E("dve", lambda e, rd=rd, pv=pv, p0=p0, b=b: e.tensor_tensor(
                    oT.ap[p0:p0 + 64, :, b * 128:(b + 1) * 128],
                    pv[p0:p0 + 64, :].rearrange("p (g q) -> p g q", g=4),
                    rd.ap[p0:p0 + 64, :].rearrange("p (g q) -> p g q", g=4), ALU.mult),
                    pvt + rd.tks, oT.tks)
        self.E("dve", lambda e: e.tensor_copy(self.kcar[:], kn.ap[:, TT:TT + 128]), kn.tks, [self.kcar_t])
        self.E("dve", lambda e: e.tensor_copy(self.vcar[:], vtm.ap[:, NB, :]), vtm.tks, [self.vcar_t])
        for dc in range(DC):
            wb, wt = woq(dc // 2)
            n0 = (dc % 2) * 128
            ps, pst = self.bank(3 + dc % 2)
            for ch in range(4):
                self.E("pe", lambda e, ch=ch, ps=ps, wb=wb, n0=n0: e.matmul(
                    ps[:], wb[:, ch, n0:n0 + 128], oT.ap[:, ch, :], start=(ch == 0), stop=False),
                    [wt] + oT.tks, pst)
            for gi in range(4):
                self.E("pe", lambda e, gi=gi, ps=ps, wb=wb, n0=n0: e.matmul(
                    ps[:], wb[:, 4 + gi, n0:n0 + 128], opool[gi].ap, start=False, stop=(gi == 3)),
                    [wt] + opool[gi].tks, pst)
            self.E("dve", lambda e, dc=dc, ps=ps: e.tensor_tensor(self.xT[:, dc, :], ps[:], self.xT[:, dc, :], ALU.add),
                   pst + [self.xT_t[dc]], [self.xT_t[dc]])

    def setup_rwkv(self):
        c = self.c
        self.cw = {nm: self.dram_in(nm, [D, D]) for nm in ("c_w_r", "c_w_k", "c_w_v", "c_w_o")}
        w1_d = self.dram_in("c_w1", [D, 64])
        a1_d = self.dram_in("c_a1", [D, 64])
        g1_d = self.dram_in("c_g1", [D, 128])
        w2_d = self.dram_in("c_w2e", [65, D])
        a2_d = self.dram_in("c_a2e", [65, D])
        g2_d = self.dram_in("c_g2", [128, D])
        rows_d = self.dram_in("c_rows", [3, D])
        self.w1b = c.sb("w1b", [128, DC, 64], BF16)
        self.a1b = c.sb("a1b", [128, DC, 64], BF16)
        self.g1b = c.sb("g1b", [128, DC, 128], BF16)
        self.w2b = c.sb("w2b", [65, D], BF16)
        self.a2b = c.sb("a2b", [65, D], BF16)
        self.g2b = c.sb("g2b", [128, D], BF16)
        self.rowb = c.sb("rowb", [128, 3, D], F32)
        self.cw_t = Tk("cw")
        self.cw_slot = c.slot("cw", group=True)
        self.spad = c.sb("spad", [128, 16, 64], F32)
        self.sbf = c.sb("sbf", [128, 16, 64], BF16)
        self.spad_t = [Tk(f"spad{i}") for i in range(16)]
        self.sbf_t = [Tk(f"sbf{i}") for i in range(16)]
        self.xp0 = c.sb("xp0", [128, 4, 256], BF16)
        self.xp0_t = [Tk(f"xp0{i}") for i in range(4)]
        sl = self.cw_slot
        lds = [(self.w1b[:], w1_d.rearrange("(c p) n -> p c n", p=128)),
               (self.a1b[:], a1_d.rearrange("(c p) n -> p c n", p=128)),
               (self.g1b[:], g1_d.rearrange("(c p) n -> p c n", p=128)),
               (self.w2b[:], w2_d[:, :]), (self.a2b[:], a2_d[:, :]), (self.g2b[:], g2_d[:, :])]
        for n_, (dst, src) in enumerate(lds):
            c.emit("pool", lambda e, dst=dst, src=src: e.dma_start(out=dst, in_=src),
                   writes=[self.cw_t if n_ == len(lds) - 1 else Tk("x")], slot=sl)
        self.rowb_t = Tk("rowb")
        sl2 = c.slot("cwrow", group=True)
        for i in range(3):
            c.emit("sp", lambda e, i=i: e.dma_start(out=self.rowb[:, i:i + 1, :], in_=rows_d[i:i + 1, :].partition_broadcast(128)),
                   writes=[self.rowb_t if i == 2 else Tk("x")], slot=sl2)
        self.E("dve", lambda e: e.memset(self.spad[:], 0.0), [], self.spad_t)
        self.E("dve", lambda e: e.memset(self.sbf[:], 0.0), [], self.sbf_t)
        for hh in range(4):
            self.E("dve", lambda e, hh=hh: e.tensor_copy(self.xp0[:, hh, 128:256], self.cbf[:, 128:256]),
                   [self.cbf_t], [self.xp0_t[hh]])

    def rwkv(self, ti):
        self.ar_reset()
        cv = self.colvec
        identb = self.cbf[:, 128:256]
        cm = self.cmat
        cwt = [self.cw_t]

        def wview(nm, q):
            return self.cw[nm].rearrange("(c p) n -> p c n", p=128)[:, :, q * 256:(q + 1) * 256]
        self.rmsnorm(CV_C)
        rb = [self.ar_alloc([D], BF16) for _ in range(NB)]
        kb_ = [self.ar_alloc([D], BF16) for _ in range(NB)]
        vb = [self.ar_alloc([D], BF16) for _ in range(NB)]
        t1e = self.ar_alloc([TT], BF16)
        a1e = self.ar_alloc([TT], BF16)
        g1T = self.ar_alloc([TT], BF16)
        yT = [self.ar_alloc([TT], BF16) for _ in range(DC)]
        mark = self.ar_ptr
        xx = [self.ar_alloc([TT], BF16) for _ in range(DC)]
        xm = [self.ar_alloc([TT], BF16) for _ in range(DC)]
        for dc in range(DC):
            self.E("dve", lambda e, dc=dc: e.tensor_tensor(xx[dc].ap, self.h[:, dc, 0:TT], self.h[:, dc, 1:TT + 1], ALU.subtract),
                   [self.h_t[dc]], xx[dc].tks)
        self.E("dve", lambda e: e.tensor_copy(self.h[:, :, 0:1], self.h[:, :, TT:TT + 1]), self.h_t, self.h_t)

        def mix(i):
            for dc in range(DC):
                self.E("dve", lambda e, dc=dc, i=i: e.scalar_tensor_tensor(
                    xm[dc].ap, xx[dc].ap, cv[:, CV_MU + i * 8 + dc:CV_MU + i * 8 + dc + 1], self.hv(dc), ALU.mult, ALU.add),
                    xx[dc].tks + [self.h_t[dc], self.colvec_t], xm[dc].tks)
        mix(1)
        ps, pst = self.bank(0)
        for dc in range(DC):
            self.E("pe", lambda e, dc=dc, ps=ps: e.matmul(ps[0:64, :], self.w1b[:, dc, :], xm[dc].ap, start=(dc == 0), stop=(dc == DC - 1)),
                   cwt + xm[dc].tks, pst)
        self.E("act", lambda e, ps=ps: e.activation(t1e.ap[0:64, :], ps[0:64, :], AF.Tanh), pst, t1e.tks)
        self.E("dve", lambda e: e.memset(t1e.ap[64:65, :], 1.0), [], t1e.tks)
        mix(4)
        ps, pst = self.bank(1)
        for dc in range(DC):
            self.E("pe", lambda e, dc=dc, ps=ps: e.matmul(ps[0:64, :], self.a1b[:, dc, :], xm[dc].ap, start=(dc == 0), stop=(dc == DC - 1)),
                   cwt + xm[dc].tks, pst)
        self.E("act", lambda e, ps=ps: e.copy(a1e.ap[0:64, :], ps[0:64, :]), pst, a1e.tks)
        self.E("dve", lambda e: e.memset(a1e.ap[64:65, :], 1.0), [], a1e.tks)
        mix(5)
        ps, pst = self.bank(0)
        for dc in range(DC):
            self.E("pe", lambda e, dc=dc, ps=ps: e.matmul(ps[:, :], self.g1b[:, dc, :], xm[dc].ap, start=(dc == 0), stop=(dc == DC - 1)),
                   cwt + xm[dc].tks, pst)
        self.E("act", lambda e, ps=ps: e.activation(g1T.ap, ps[:, :], AF.Sigmoid), pst, g1T.tks)
        hcnt = 0
        for (i, nm, dst) in ((0, "c_w_r", rb), (2, "c_w_k", kb_), (3, "c_w_v", vb)):
            mix(i)
            wl = self.lazy([wview(nm, q_) for q_ in range(4)], 1)
            for q in range(4):
                wb, wt = wl(q)
                for cb in range(NB):
                    ph, pht = self.bk(hcnt % 8, 0, 256)
                    hcnt += 1
                    for dc in range(DC):
                        self.E("pe", lambda e, dc=dc, cb=cb, ph=ph, wb=wb: e.matmul(
                            ph, xm[dc].ap[:, cb * 128:(cb + 1) * 128], wb[:, dc, :], start=(dc == 0), stop=(dc == DC - 1)),
                            [wt] + xm[dc].tks, pht)
                    if hcnt % 2 == 0:
                        self.E("act", lambda e, cb=cb, q=q, ph=ph, dst=dst: e.copy(dst[cb].ap[:, q * 256:(q + 1) * 256], ph), pht, dst[cb].tks)
                    else:
                        self.E("dve", lambda e, cb=cb, q=q, ph=ph, dst=dst: e.tensor_copy(dst[cb].ap[:, q * 256:(q + 1) * 256], ph), pht, dst[cb].tks)
        if RWDBG <= 1:
            return
        for cb in range(NB):
            self.ar_ptr = mark
            self.rwkv_chunk(ti, cb, rb[cb], kb_[cb], vb[cb], t1e, a1e, g1T, yT)
        wl = self.lazy([wview("c_w_o", q_) for q_ in range(4)], 1)
        for q in range(4):
            wb, wt = wl(q)
            for nn in range(2):
                dco = q * 2 + nn
                ps, pst = self.bank(dco % 2)
                for blk in range(DC):
                    self.E("pe", lambda e, blk=blk, nn=nn, ps=ps, wb=wb: e.matmul(
                        ps[:], wb[:, blk, nn * 128:(nn + 1) * 128], yT[blk].ap, start=(blk == 0), stop=(blk == DC - 1)),
                        [wt] + yT[blk].tks, pst)
                self.E("dve", lambda e, dco=dco, ps=ps: e.tensor_tensor(self.xT[:, dco, :], ps[:], self.xT[:, dco, :], ALU.add),
                       pst + [self.xT_t[dco]], [self.xT_t[dco]])

    def rwkv_chunk(self, ti, cb, rb, kb_, vb, t1e, a1e, g1T, yT):
        cv, cm = self.colvec, self.cmat
        identb = self.cbf[:, 128:256]
        cwt = [self.cw_t]
        cs = slice(cb * 128, (cb + 1) * 128)
        A = self.ar_alloc
        sg = A([D], F32)
        ex = A([D], F32)
        en = A([D], F32)
        kkb = A([D], F32)
        km = A([D], F32)
        bb = A([D], F32)
        a_ = A([D], BF16)
        at, rt, kt, bt, kh, bh, bv, zb = [A([D], BF16) for _ in range(8)]
        sm = A([8, 16], F32)
        fmT = A([DC, 4, 128], BF16)
        xpb = [[A([256], BF16) for _ in range(2)] for _ in range(4)]
        xtb = [[A([128], BF16) for _ in range(2)] for _ in range(4)]
        xta = [A([256], BF16) for _ in range(4)]
        arb = [A([128], BF16) for _ in range(4)]
        ark = [A([128], BF16) for _ in range(4)]
        tinv = [A([128], BF16) for _ in range(4)]
        pm = [A([256], BF16) for _ in range(4)]
        ut = [A([64], BF16) for _ in range(4)]
        tmpf = [A([128], F32) for _ in range(2)]
        y = sg
        v3 = lambda b_: b_.ap.rearrange("p (h j) -> p h j", h=16)
        bc = lambda i: sm.ap[:, i, :].unsqueeze(2).to_broadcast([128, 16, 64])
        for hf in range(2):
            hs_ = slice(hf * 512, (hf + 1) * 512)
            ps, pst = self.bank(hf)
            self.E("pe", lambda e, ps=ps, hs_=hs_: e.matmul(ps[:], t1e.ap[0:65, cs], self.w2b[0:65, hs_], start=True, stop=True),
                   t1e.tks + cwt, pst)
            self.E("act", lambda e, ps=ps, hs_=hs_: e.activation(sg.ap[:, hs_], ps[:], AF.Sigmoid), pst, sg.tks)
            ps, pst = self.bank(2 + hf)
            self.E("pe", lambda e, ps=ps, hs_=hs_: e.matmul(ps[:], a1e.ap[0:65, cs], self.a2b[0:65, hs_], start=True, stop=True),
                   a1e.tks + cwt, pst)
            self.E("act", lambda e, ps=ps, hs_=hs_: e.activation(a_.ap[:, hs_], ps[:], AF.Sigmoid), pst, a_.tks)
        if RWDBG <= 2:
            return
        tri = self.cbf[:, 384:512]
        onesb = self.cbf[:, 0:128]
        sgh, sgl = kh, bh
        self.E("act", lambda e: e.copy(sgh.ap, sg.ap), sg.tks, sgh.tks)
        self.E("dve", lambda e: e.tensor_tensor(sgl.ap, sg.ap, sgh.ap, ALU.subtract), sg.tks + sgh.tks, sgl.tks)
        pcs = []
        for hf in range(2):
            hs_ = slice(hf * 512, (hf + 1) * 512)
            pc, pct = self.bank(4 + hf)
            self.E("pe", lambda e, pc=pc, hs_=hs_: e.matmul(pc[:], tri, sgh.ap[:, hs_], start=True, stop=False), sgh.tks + [self.cbf_t], pct)
            self.E("pe", lambda e, pc=pc, hs_=hs_: e.matmul(pc[:], tri, sgl.ap[:, hs_], start=False, stop=True), sgl.tks + [self.cbf_t], pct)
            pC, pCt = self.bank(6 + hf)
            self.E("pe", lambda e, pC=pC, hs_=hs_: e.matmul(pC[:], onesb, sgh.ap[:, hs_], start=True, stop=False), sgh.tks + [self.cbf_t], pCt)
            self.E("pe", lambda e, pC=pC, hs_=hs_: e.matmul(pC[:], onesb, sgl.ap[:, hs_], start=False, stop=True), sgl.tks + [self.cbf_t], pCt)
            pcs.append((pc, pct, pC, pCt))
        pw_, pwt = self.bk(0, 0, 128)
        for blk in range(DC):
            self.E("pe", lambda e, blk=blk: e.matmul(pw_[:, blk * 16:(blk + 1) * 16], sgh.ap[:, blk * 128:(blk + 1) * 128],
                                                     onesb[:, 0:16], start=True, stop=False), sgh.tks + [self.cbf_t], pwt)
            self.E("pe", lambda e, blk=blk: e.matmul(pw_[:, blk * 16:(blk + 1) * 16], sgl.ap[:, blk * 128:(blk + 1) * 128],
                                                     onesb[:, 0:16], start=False, stop=True), sgl.tks + [self.cbf_t], pwt)
        self.E("act", lambda e: e.activation(sm.ap[:, 7, 0:8], pw_.rearrange("p (b n) -> p b n", n=16)[:, :, 0], AF.Exp, scale=NEG_E),
               pwt, sm.tks)
        for hf in range(2):
            hs_ = slice(hf * 512, (hf + 1) * 512)
            pc, pct, pC, pCt = pcs[hf]
            self.E("act", lambda e, pc=pc, hs_=hs_: e.activation(ex.ap[:, hs_], pc[:], AF.Exp, scale=NEG_E), pct, ex.tks)
            self.E("act", lambda e, pc=pc, hs_=hs_: e.activation(en.ap[:, hs_], pc[:], AF.Exp, scale=-NEG_E), pct, en.tks)
            self.E("dve", lambda e, pc=pc, hs_=hs_: e.scalar_tensor_tensor(
                sg.ap[:, hs_], sg.ap[:, hs_], -1.0, pc[:], ALU.mult, ALU.add), sg.tks + pct, sg.tks)
        self.E("act", lambda e: e.activation(sg.ap, sg.ap, AF.Exp, scale=NEG_E), sg.tks, sg.tks)
        if RWDBG <= 3:
            return
        self.E("dve", lambda e: e.tensor_tensor(kkb.ap, kb_.ap, self.rowb[:, 0, :], ALU.mult), kb_.tks + [self.rowb_t], kkb.tks)
        self.E("dve", lambda e: e.tensor_tensor(bb.ap, kkb.ap, kkb.ap, ALU.mult), kkb.tks, bb.tks)
        self.E("dve", lambda e: e.tensor_reduce(sm.ap[:, 0, :], v3(bb), AX.X, ALU.add), bb.tks, sm.tks)
        self.E("dve", lambda e: e.tensor_scalar(sm.ap[:, 0, :], sm.ap[:, 0, :], 1e-24, None, ALU.max), sm.tks, sm.tks)
        self.E("act", lambda e: e.activation(sm.ap[:, 0, :], sm.ap[:, 0, :], AF.Sqrt), sm.tks, sm.tks)
        self.E("dve", lambda e: e.reciprocal(sm.ap[:, 0, :], sm.ap[:, 0, :]), sm.tks, sm.tks)
        self.E("dve", lambda e: e.tensor_tensor(v3(kkb), v3(kkb), bc(0), ALU.mult), kkb.tks + sm.tks, kkb.tks)
        self.E("dve", lambda e: e.scalar_tensor_tensor(km.ap, a_.ap, 1.0, self.rowb[:, 1, :], ALU.subtract, ALU.mult),
               a_.tks + [self.rowb_t], km.tks)
        self.E("dve", lambda e: e.scalar_tensor_tensor(km.ap, km.ap, 1.0, kb_.ap, ALU.add, ALU.mult), km.tks + kb_.tks, km.tks)
        self.E("dve", lambda e: e.tensor_tensor(bb.ap, kkb.ap, a_.ap, ALU.mult), kkb.tks + a_.tks, bb.tks)
        self.E("dve", lambda e: e.scalar_tensor_tensor(at.ap, kkb.ap, -1.0, sg.ap, ALU.mult, ALU.mult), kkb.tks + sg.tks, at.tks)
        self.E("dve", lambda e: e.tensor_tensor(rt.ap, rb.ap, ex.ap, ALU.mult), rb.tks + ex.tks, rt.tks)
        self.E("dve", lambda e: e.tensor_tensor(kt.ap, km.ap, en.ap, ALU.mult), km.tks + en.tks, kt.tks)
        self.E("dve", lambda e: e.tensor_tensor(bt.ap, bb.ap, en.ap, ALU.mult), bb.tks + en.tks, bt.tks)
        for hf in range(2):
            hs_ = slice(hf * 512, (hf + 1) * 512)
            pc, pct, pC, pCt = pcs[hf]
            self.E("act", lambda e, pC=pC, hs_=hs_: e.activation(ex.ap[:, hs_], pC[:], AF.Exp, scale=NEG_E), pCt, ex.tks)
        self.E("dve", lambda e: e.tensor_tensor(kh.ap, kt.ap, ex.ap, ALU.mult), kt.tks + ex.tks, kh.tks)
        self.E("dve", lambda e: e.tensor_tensor(bh.ap, bt.ap, ex.ap, ALU.mult), bt.tks + ex.tks, bh.tks)
        self.E("dve", lambda e: e.tensor_tensor(bb.ap, rb.ap, km.ap, ALU.mult), rb.tks + km.tks, bb.tks)
        self.E("dve", lambda e: e.tensor_tensor(bb.ap, bb.ap, self.rowb[:, 2, :], ALU.mult), bb.tks + [self.rowb_t], bb.tks)
        self.E("dve", lambda e: e.tensor_reduce(sm.ap[:, 1, :], v3(bb), AX.X, ALU.add), bb.tks, sm.tks)
        self.E("dve", lambda e: e.tensor_tensor(v3(bv), v3(vb), bc(1), ALU.mult), vb.tks + sm.tks, bv.tks)
        if RWDBG <= 4:
            return
        for blk in range(DC):
            ph, pht = self.bank(blk % 8)
            for j, src in enumerate((at, rt, bt, kt)):
                self.E("pe", lambda e, j=j, src=src, blk=blk, ph=ph: e.matmul(
                    ph[:, j * 128:(j + 1) * 128], src.ap[:, blk * 128:(blk + 1) * 128], identb, start=True, stop=True),
                    src.tks + [self.cbf_t], pht)
            if blk % 2 == 0:
                self.E("act", lambda e, blk=blk, ph=ph: e.copy(fmT.ap[:, blk, :, :], ph[:].rearrange("p (j t) -> p j t", j=4)), pht, fmT.tks)
            else:
                self.E("dve", lambda e, blk=blk, ph=ph: e.tensor_copy(fmT.ap[:, blk, :, :], ph[:].rearrange("p (j t) -> p j t", j=4)), pht, fmT.tks)
        if RWDBG <= 5:
            return
        flip = [0]

        def evac(out, in_, reads, writes):
            flip[0] ^= 1
            if flip[0]:
                self.E("act", lambda e: e.copy(out, in_), reads, writes)
            else:
                self.E("dve", lambda e: e.tensor_copy(out, in_), reads, writes)
        for grp in range(4):
            hs = [grp * 4 + hh for hh in range(4)]
            for hh, h in enumerate(hs):
                blk, p0 = h // 2, (h % 2) * 64
                f = lambda j, n=1, blk=blk, p0=p0: fmT.ap[p0:p0 + 64, blk, j:j + n, :]
                pA, pAt = self.bk(hh * 2, 0, 256)
                pB, pBt = self.bk(hh * 2 + 1, 0, 256)
                pC_, pCt_ = self.bk(hh * 2, 256, 512)
                self.E("pe", lambda e, pA=pA, f=f: e.matmul(pA.rearrange("p (a t) -> p a t", a=2), f(2)[:, 0, :], f(0, 2), start=True, stop=True),
                       fmT.tks, pAt)
                self.E("pe", lambda e, pB=pB, f=f: e.matmul(pB.rearrange("p (a t) -> p a t", a=2), f(0)[:, 0, :], f(2, 2), start=True, stop=True),
                       fmT.tks, pBt)
                self.E("pe", lambda e, pC_=pC_, f=f: e.matmul(pC_[:, 0:128], f(3)[:, 0, :], f(1)[:, 0, :], start=True, stop=True),
                       fmT.tks, pCt_)
                self.E("dve", lambda e, pA=pA, hh=hh: e.tensor_tensor(self.xp0[:, hh, 0:128], pA[:, 0:128], cm[:, CM_SU:CM_SU + 128], ALU.mult),
                       pAt + [self.cmat_t], [self.xp0_t[hh]])
                self.E("dve", lambda e, pA=pA, hh=hh: e.tensor_tensor(arb[hh].ap, pA[:, 128:256], cm[:, CM_UI:CM_UI + 128], ALU.mult),
                       pAt + [self.cmat_t], arb[hh].tks)
                self.E("dve", lambda e, pB=pB, hh=hh: e.tensor_tensor(xta[hh].ap, pB, cm[:, CM_SL:CM_SL + 256], ALU.mult),
                       pBt + [self.cmat_t], xta[hh].tks)
                self.E("dve", lambda e, pC_=pC_, hh=hh: e.tensor_tensor(ark[hh].ap, pC_[:, 0:128], cm[:, CM_UI:CM_UI + 128], ALU.mult),
                       pCt_ + [self.cmat_t], ark[hh].tks)
            if RWDBG <= 5.2:
                return
            curXP = [(self.xp0[:, hh, :], [self.xp0_t[hh]]) for hh in range(4)]
            curXT = [(xta[hh].ap[:, 0:128], xta[hh].tks) for hh in range(4)]
            for k in range(int(os.environ.get('LVK', '6'))):
                for hh in range(4):
                    XP, XPt = curXP[hh]
                    XT, XTt = curXT[hh]
                    pA, pAt = self.bk(hh * 2 + (k % 2), 0, 256)
                    pC_, pCt_ = self.bk(hh * 2 + (k % 2), 256, 512)
                    self.E("pe", lambda e, pA=pA, XP=XP, XT=XT: e.matmul(pA[:, 0:128], XT, XP[:, 0:128], start=True, stop=True), XPt + XTt, pAt)
                    LV = int(os.environ.get("LV", "0"))
                    self.E("pe", lambda e, pA=pA, XP=XP, XT=XT, LV=LV: e.matmul(pA[:, 128:256], XT, XP[:, 128:256], start=True, stop=(LV == 1)), XPt + XTt, pAt)
                    if LV != 1:
                        self.E("pe", lambda e, pA=pA, XP=XP: e.matmul(pA[:, 128:256], identb, XP[:, 128:256], start=False, stop=True), XPt + [self.cbf_t], pAt)
                    if LV != 2:
                        self.E("pe", lambda e, pC_=pC_, XP=XP, XT=XT: e.matmul(pC_[:, 0:128], XP[:, 0:128], XT, start=True, stop=True), XPt + XTt, pCt_)
                    else:
                        self.E("pe", lambda e, pC_=pC_, XP=XP, XT=XT: e.matmul(pC_[:, 0:128], XT, XP[:, 0:128], start=True, stop=True), XPt + XTt, pCt_)
                    nXP, nXT = xpb[hh][k % 2], xtb[hh][k % 2]
                    evac(nXP.ap, pA, pAt, nXP.tks)
                    evac(nXT.ap, pC_[:, 0:128], pCt_, nXT.tks)
                    curXP[hh] = (nXP.ap, nXP.tks)
                    curXT[hh] = (nXT.ap, nXT.tks)
            for hh in range(4):
                XP, XPt = curXP[hh]
                XT, XTt = curXT[hh]
                pA, pAt = self.bk(hh * 2, 0, 256)
                self.E("pe", lambda e, pA=pA, XP=XP, XT=XT: e.matmul(pA[:, 0:128], XT, XP[:, 128:256], start=True, stop=False), XPt + XTt, pAt)
                self.E("pe", lambda e, pA=pA, XP=XP: e.matmul(pA[:, 0:128], identb, XP[:, 128:256], start=False, stop=True), XPt + [self.cbf_t], pAt)
                evac(tinv[hh].ap, pA[:, 0:128], pAt, tinv[hh].tks)
            if RWDBG <= 5.5:
                return
            for hh, h in enumerate(hs):
                blk = h // 2
                pA, pAt = self.bk(hh * 2 + 1, 0, 256)
                self.E("pe", lambda e, pA=pA, hh=hh, blk=blk: e.matmul(pA[:, 0:128], at.ap[:, blk * 128:(blk + 1) * 128], tinv[hh].ap, start=True, stop=True),
                       at.tks + tinv[hh].tks, pAt)
                self.E("pe", lambda e, pA=pA, hh=hh: e.matmul(pA[:, 128:256], xta[hh].ap[:, 128:256], tinv[hh].ap, start=True, stop=True),
                       xta[hh].tks + tinv[hh].tks, pAt)
                evac(pm[hh].ap, pA, pAt, pm[hh].tks)
            if RWDBG <= 5.6:
                return
            for hh, h in enumerate(hs):
                pU, pUt = self.bk(hh * 2, 256, 512)
                self.E("pe", lambda e, pU=pU, hh=hh, h=h: e.matmul(pU[:, 0:64], pm[hh].ap[:, 0:128], self.sbf[:, h, :], start=True, stop=False),
                       pm[hh].tks + [self.sbf_t[h]], pUt)
                self.E("pe", lambda e, pU=pU, hh=hh, h=h: e.matmul(pU[:, 0:64], pm[hh].ap[:, 128:256], vb.ap[:, h * 64:(h + 1) * 64], start=False, stop=True),
                       pm[hh].tks + vb.tks, pUt)
                evac(ut[hh].ap, pU[:, 0:64], pUt, ut[hh].tks)
            if RWDBG <= 5.7:
                return
            for hh, h in enumerate(hs):
                blk, p0 = h // 2, (h % 2) * 64
                pY, pYt = self.bk(hh * 2 + 1, 256, 512)
                self.E("pe", lambda e, pY=pY, h=h, blk=blk: e.matmul(pY[:, 0:64], fmT.ap[:, blk, 1, :], self.sbf[:, h, :], start=True, stop=False),
                       fmT.tks + [self.sbf_t[h]], pYt)
                self.E("pe", lambda e, pY=pY, hh=hh: e.matmul(pY[:, 0:64], arb[hh].ap, ut[hh].ap, start=False, stop=False),
                       arb[hh].tks + ut[hh].tks, pYt)
                self.E("pe", lambda e, pY=pY, hh=hh, h=h: e.matmul(pY[:, 0:64], ark[hh].ap, vb.ap[:, h * 64:(h + 1) * 64], start=False, stop=True),
                       ark[hh].tks + vb.tks, pYt)
                evac(y.ap[:, h * 64:(h + 1) * 64], pY[:, 0:64], pYt, y.tks)
                pS, pSt = self.bk(hh * 2, 0, 256)
                self.E("pe", lambda e, pS=pS, hh=hh, blk=blk: e.matmul(pS[:, 0:64], bh.ap[:, blk * 128:(blk + 1) * 128], ut[hh].ap, start=True, stop=False),
                       bh.tks + ut[hh].tks, pSt)
                self.E("pe", lambda e, pS=pS, h=h, blk=blk: e.matmul(pS[:, 0:64], kh.ap[:, blk * 128:(blk + 1) * 128], vb.ap[:, h * 64:(h + 1) * 64], start=False, stop=True),
                       kh.tks + vb.tks, pSt)
                self.E("dve", lambda e, pS=pS, h=h, blk=blk, p0=p0: e.scalar_tensor_tensor(
                    self.spad[p0:p0 + 64, h, :], self.spad[p0:p0 + 64, h, :], sm.ap[p0:p0 + 64, 7, blk:blk + 1], pS[p0:p0 + 64, 0:64],
                    ALU.mult, ALU.add), [self.spad_t[h]] + sm.tks + pSt, [self.spad_t[h]])
                self.E("act", lambda e, h=h, p0=p0: e.copy(self.sbf[p0:p0 + 64, h, :], self.spad[p0:p0 + 64, h, :]),
                       [self.spad_t[h]], [self.sbf_t[h]])
        if RWDBG <= 6:
            return
        self.E("dve", lambda e: e.tensor_tensor(bb.ap, y.ap, y.ap, ALU.mult), y.tks, bb.tks)
        self.E("dve", lambda e: e.tensor_reduce(sm.ap[:, 2, :], v3(y), AX.X, ALU.add), y.tks, sm.tks)
        self.E("dve", lambda e: e.tensor_reduce(sm.ap[:, 3, :], v3(bb), AX.X, ALU.add), bb.tks, sm.tks)
        self.E("dve", lambda e: e.tensor_scalar(sm.ap[:, 2, :], sm.ap[:, 2, :], 1.0 / 64, None, ALU.mult), sm.tks, sm.tks)
        self.E("dve", lambda e: e.tensor_tensor(sm.ap[:, 4, :], sm.ap[:, 2, :], sm.ap[:, 2, :], ALU.mult), sm.tks, sm.tks)
        self.E("dve", lambda e: e.scalar_tensor_tensor(sm.ap[:, 3, :], sm.ap[:, 3, :], 1.0 / 64, sm.ap[:, 4, :], ALU.mult, ALU.subtract),
               sm.tks, sm.tks)
        self.E("act", lambda e: e.activation(sm.ap[:, 3, :], sm.ap[:, 3, :], AF.Sqrt, bias=cv[:, CV_GNEPS:CV_GNEPS + 1]),
               sm.tks + [self.colvec_t], sm.tks)
        self.E("dve", lambda e: e.reciprocal(sm.ap[:, 3, :], sm.ap[:, 3, :]), sm.tks, sm.tks)
        self.E("dve", lambda e: e.tensor_tensor(v3(y), v3(y), bc(2), ALU.subtract), y.tks + sm.tks, y.tks)
        self.E("dve", lambda e: e.tensor_tensor(v3(zb), v3(y), bc(3), ALU.mult), y.tks + sm.tks, zb.tks)
        for blk in range(DC):
            phb, pht = self.bk(blk % 4, 0, 256)
            self.E("pe", lambda e, blk=blk, phb=phb: e.matmul(phb[:, 0:128], zb.ap[:, blk * 128:(blk + 1) * 128], identb, start=True, stop=True),
                   zb.tks + [self.cbf_t], pht)
            self.E("pe", lambda e, blk=blk, phb=phb: e.matmul(phb[:, 128:256], bv.ap[:, blk * 128:(blk + 1) * 128], identb, start=True, stop=True),
                   bv.tks + [self.cbf_t], pht)
            pg, pgt = self.bk(4 + blk % 4, 0, 256)
            self.E("pe", lambda e, blk=blk, pg=pg: e.matmul(pg[:, 0:128], self.g2b[:, blk * 128:(blk + 1) * 128], g1T.ap[:, cs], start=True, stop=True),
                   [self.cw_t] + g1T.tks, pgt)
            tf = tmpf[blk % 2]
            self.E("dve", lambda e, blk=blk, phb=phb, tf=tf: e.tensor_scalar(
                tf.ap, phb[:, 0:128], cv[:, CV_LNW + blk:CV_LNW + blk + 1], cv[:, CV_LNB + blk:CV_LNB + blk + 1], ALU.mult, ALU.add),
                pht + [self.colvec_t], tf.tks)
            self.E("dve", lambda e, phb=phb, tf=tf: e.tensor_tensor(tf.ap, tf.ap, phb[:, 128:256], ALU.add), pht + tf.tks, tf.tks)
            self.E("dve", lambda e, blk=blk, pg=pg, tf=tf: e.tensor_tensor(yT[blk].ap[:, cs], tf.ap, pg[:, 0:128], ALU.mult),
                   tf.tks + pgt, yT[blk].tks)


    def run_stage(self, ti, s):
        if s.startswith("ffn"):
            self.ffn(int(s[3:]))
        elif s == "ab":
            self.ab(ti)
        elif s == "rwkv":
            self.rwkv(ti)
        elif s == "dbgnorm":
            self.ar_reset()
            self.rmsnorm(0)
            for dc in range(DC):
                self.E("dve", lambda e, dc=dc: e.tensor_copy(self.xT[:, dc, :], self.hv(dc)),
                       [self.h_t[dc]], [self.xT_t[dc]])


_CACHE = {}


def get_program(T, stages):
    key = (T, tuple(stages))
    if key not in _CACHE:
        b = Builder(T, list(stages))
        nc = b.build()
        _CACHE[key] = (nc, b)
    return _CACHE[key]


ALL_STAGES = ["ffn0", "ab", "ffn1", "ffn2", "rwkv", "ffn3"]


def make_in_maps(inp, T, stages, ncores):
    x = np.asarray(inp["x"], np.float32)
    f32 = lambda a: np.ascontiguousarray(np.asarray(a, np.float32))
    common = {"colvec": host_colvec(inp), "cmat": host_consts()}
    if any(s_.startswith("ffn") for s_ in stages):
        common["wg"] = f32(inp["ffn_w_gate"]).reshape(4, D, DFF)
        common["wu"] = f32(inp["ffn_w_up"]).reshape(4, D, DFF)
        common["wd"] = f32(inp["ffn_w_down"]).reshape(4, DFF, D)
    if "ab" in stages:
        w_in = f32(inp["ab_w_in"]).reshape(D, 1280)
        qcols = w_in[:, :512].reshape(D, 2, 4, 64).transpose(0, 2, 1, 3).reshape(D, 512)
        common["w_in"] = np.ascontiguousarray(np.concatenate([qcols, w_in[:, 512:]], axis=1))
        w_out = f32(inp["ab_w_out"]).reshape(D, D)
        arows = w_out[:512].reshape(2, 4, 64, D).transpose(1, 0, 2, 3).reshape(512, D)
        common["w_out"] = np.ascontiguousarray(np.concatenate([arows, w_out[512:]], axis=0))
        common["pool_w"] = f32(inp["pool_w"]).reshape(4, 128, 128)
    if "rwkv" in stages:
        for nm in ("c_w_r", "c_w_k", "c_w_v", "c_w_o"):
            common[nm] = f32(inp[nm]).reshape(D, D)
        common["c_w1"] = f32(inp["c_w1"]).reshape(D, 64)
        common["c_a1"] = f32(inp["c_a1"]).reshape(D, 64)
        common["c_g1"] = f32(inp["c_g1"]).reshape(D, 128)
        common["c_w2e"] = np.ascontiguousarray(np.concatenate([f32(inp["c_w2"]).reshape(64, D), f32(inp["c_w0"]).reshape(1, D)], axis=0))
        common["c_a2e"] = np.ascontiguousarray(np.concatenate([f32(inp["c_a2"]).reshape(64, D), f32(inp["c_a0"]).reshape(1, D)], axis=0))
        common["c_g2"] = f32(inp["c_g2"]).reshape(128, D)
        common["c_rows"] = np.ascontiguousarray(np.stack([f32(inp["c_k_k"]).reshape(D), f32(inp["c_k_a"]).reshape(D),
                                                          f32(inp["c_r_k"]).reshape(D)], axis=0))
    in_maps = []
    for ci in range(ncores):
        m = dict(common)
        m["x"] = np.ascontiguousarray(x[ci, :T])
        in_maps.append(m)
    return in_maps


def run(inp, T=SEQ, stages=ALL_STAGES, ncores=8, trace=False):
    nc, b = get_program(T, stages)
    in_maps = make_in_maps(inp, T, stages, ncores)
    res = run_bass_kernel_spmd(nc, in_maps, core_ids=list(range(ncores)), trace=trace)
    out = np.stack([np.asarray(r["y"]) for r in res.results], axis=0)
    return out, res


def kernel(**inputs):
    out, _ = run(inputs)
    return out.astype(np.float32)
```

```python
import contextlib
import os
import numpy as np
import concourse.bass as bass
import concourse.mybir as mybir
from concourse.bass_utils import run_bass_kernel_spmd

F32 = mybir.dt.float32
BF16 = mybir.dt.bfloat16
ALU = mybir.AluOpType
AF = mybir.ActivationFunctionType
AX = mybir.AxisListType

D = 1024
DC = 8
DFF = 2816
FC = 22
TT = 512
NB = TT // 128
SEQ = 8192
RMS_EPS = 1e-6
GN_EPS = 64e-5
NEG_E = -float(np.exp(-0.5))
ENGS = ["pe", "act", "dve", "pool", "sp"]
SEM_CH = int(os.environ.get("SEM_CH", "12000"))
ARENA_KB = 106
RWDBG = float(os.environ.get('RWDBG', '9'))
NG_RING = 4


class Tk:
    __slots__ = ("name", "w", "r", "excl")

    def __init__(self, name, excl=False):
        self.name = name
        self.w = {}
        self.r = {}
        self.excl = excl


class Buf:
    __slots__ = ("ap", "tks")

    def __init__(self, ap, tks):
        self.ap = ap
        self.tks = tks


class Slot:
    def __init__(self, key, group=False):
        self.key = key
        self.count = 0
        self.sem = None
        self.group = group


class Op:
    __slots__ = ("eng", "fn", "deps", "tok", "slot", "snap", "waits", "signal", "sig")

    def __init__(self, eng, fn, deps, tok, slot):
        self.eng = eng
        self.fn = fn
        self.deps = deps
        self.tok = tok
        self.slot = slot
        self.snap = None
        self.waits = ()
        self.signal = False
        self.sig = None


class Ctx:
    def __init__(self, nc):
        self.nc = nc
        self.oplist = []
        self.eng_ops = {e: [] for e in ENGS}
        self.tokmap = {}
        self.slots = []
        self.es = contextlib.ExitStack()
        self.out_slots = []

    def sb(self, name, shape, dt):
        return self.es.enter_context(self.nc.sbuf_tensor("sb_" + name, list(shape), dt))

    def ps(self, name, shape, dt=F32):
        return self.es.enter_context(self.nc.psum_tensor("ps_" + name, list(shape), dt))

    def slot(self, name, group=False):
        s = Slot(("dma", name), group)
        self.slots.append(s)
        return s

    def emit(self, eng, fn, reads=(), writes=(), slot=None):
        if any(t.excl for t in reads):
            writes = list(writes) + [t for t in reads if t.excl]
            reads = [t for t in reads if not t.excl]
        deps = {}
        for t in reads:
            for k, v in t.w.items():
                if deps.get(k, -1) < v:
                    deps[k] = v
        for t in writes:
            for k, v in t.w.items():
                if deps.get(k, -1) < v:
                    deps[k] = v
            for k, v in t.r.items():
                if deps.get(k, -1) < v:
                    deps[k] = v
        if eng == "pe":
            deps.pop("pe", None)
        if slot is not None:
            slot.count += 1
            tok = (slot.key, slot.count)
        else:
            tok = (eng, len(self.eng_ops[eng]))
        op = Op(eng, fn, deps, tok, slot)
        self.oplist.append(op)
        self.eng_ops[eng].append(op)
        self.tokmap[tok] = op
        k, v = tok
        for t in reads:
            if t.r.get(k, -1) < v:
                t.r[k] = v
        for t in writes:
            t.w = {k: v}
            t.r = {}
        return tok

    def finalize(self):
        nc = self.nc
        known = {e: {} for e in ENGS}
        for op in self.oplist:
            kn = known[op.eng]
            waits = []
            copied = False
            for k, v in op.deps.items():
                if kn.get(k, -1) >= v:
                    continue
                if not copied:
                    kn = dict(kn)
                    known[op.eng] = kn
                    copied = True
                waits.append((k, v))
                prod = self.tokmap[(k, v)]
                prod.signal = True
                for kk, vv in prod.snap.items():
                    if kn.get(kk, -1) < vv:
                        kn[kk] = vv
                kn[k] = v
            op.waits = waits
            op.snap = kn
        self.eng_sems = {e: [] for e in ENGS}
        for e in ENGS:
            n = 0
            for op in self.eng_ops[e]:
                if op.slot is None and op.signal:
                    op.sig = n
                    n += 1
            nch = (n + SEM_CH - 1) // SEM_CH
            for c in range(nch):
                self.eng_sems[e].append(self.es.enter_context(nc.semaphore(f"s_{e}_{c}")))
        for s in self.slots:
            if s.count > 0:
                s.sem = self.es.enter_context(nc.semaphore("d_" + s.key[1]))

        def resolve(k, v):
            if isinstance(k, tuple):
                s = self.tokmap[(k, v)].slot
                return s.sem, 16 * (s.count if s.group else v)
            sig = self.tokmap[(k, v)].sig
            return self.eng_sems[k][sig // SEM_CH], sig % SEM_CH + 1

        ctx = self

        def run_engine(ename):
            def body(e):
                for op in ctx.eng_ops[ename]:
                    ws = [resolve(k, v) for (k, v) in op.waits]
                    if op.slot is not None:
                        for (s, v) in ws:
                            e.wait_ge(s, v)
                        ins = op.fn(e)
                        ins.then_inc(op.slot.sem, 16)
                    else:
                        for (s, v) in ws[:-1]:
                            e.wait_ge(s, v)
                        ins = op.fn(e)
                        if ws:
                            ins._wait_ge(*ws[-1])
                        if op.signal:
                            ins.then_inc(ctx.eng_sems[ename][op.sig // SEM_CH], 1)
                if ename == "sp":
                    for s in ctx.out_slots:
                        if s.count:
                            e.wait_ge(s.sem, 16 * s.count)
            return body

        with nc.Block() as block:
            block.tensor(run_engine("pe"))
            block.scalar(run_engine("act"))
            block.vector(run_engine("dve"))
            block.gpsimd(run_engine("pool"))
            block.sync(run_engine("sp"))
        self.stats = {e: len(self.eng_ops[e]) for e in ENGS}
        self.stats["waits"] = sum(len(op.waits) for op in self.oplist)
        self.stats["sems"] = sum(len(v) for v in self.eng_sems.values()) + sum(1 for s in self.slots if s.count)


CV_FFN = 0
CV_AB = 32
CV_C = 40
CV_QN = 48
CV_KN = 49
CV_PS = 50
CV_MU = 54
CV_SINK = 102
CV_LNW = 110
CV_LNB = 118
CV_GNEPS = 126
CV_EPS = 127
NCOL = 128
CM_IDENT = 0
CM_ONES = 128
CM_BD = 256
CM_AM = 384
CM_INVC = CM_AM + 2048
CM_TRIS = CM_INVC + 64
CM_ONESS = CM_TRIS + 128
CM_SU = CM_ONESS + 128
CM_UI = CM_SU + 128
CM_SL = CM_UI + 256
NCMAT = CM_SL + 256


def host_consts():
    cm = np.zeros((128, NCMAT), np.float32)
    cm[:, CM_IDENT:CM_IDENT + 128] = np.eye(128, dtype=np.float32)
    cm[:, CM_ONES:CM_ONES + 128] = 1.0
    cm[0:64, CM_BD:CM_BD + 64] = 1.0
    cm[64:128, CM_BD + 64:CM_BD + 128] = 1.0
    kk = np.arange(128)[:, None].astype(np.float64)
    qq = np.arange(128)[None, :].astype(np.float64)
    am = np.zeros((128, 2, 2, 4, 128), np.float64)
    for hk in range(2):
        for g in range(4):
            hq = hk * 4 + g
            slope = 2.0 ** (-8.0 * (hq + 1) / 8.0)
            dist_cur = qq - kk
            am[:, hk, 1, g, :] = np.where(dist_cur >= 0, np.exp(-slope * dist_cur), 0.0)
            dist_prev = qq - kk + 128
            am[:, hk, 0, g, :] = np.where(dist_prev < 128, np.exp(-slope * dist_prev), 0.0)
    cm[:, CM_AM:CM_AM + 2048] = am.reshape(128, 2048).astype(np.float32)
    invc = np.zeros((4, 16), np.float64)
    for gi, w in enumerate((2, 4, 8, 16)):
        invc[gi] = 1.0 / np.minimum(np.arange(1, 17), w)
    cm[:, CM_INVC:CM_INVC + 64] = invc.reshape(1, 64).astype(np.float32)
    s_ = np.arange(128)
    cm[:, CM_TRIS:CM_TRIS + 128] = NEG_E * (s_[:, None] <= s_[None, :]).astype(np.float32)
    cm[:, CM_ONESS:CM_ONESS + 128] = NEG_E
    cm[:, CM_SU:CM_SU + 128] = (s_[:, None] < s_[None, :]).astype(np.float32)
    cm[:, CM_UI:CM_UI + 128] = (s_[:, None] <= s_[None, :]).astype(np.float32)
    cm[:, CM_UI + 128:CM_UI + 256] = (s_[:, None] <= s_[None, :]).astype(np.float32)
    sl = (s_[:, None] > s_[None, :]).astype(np.float32)
    cm[:, CM_SL:CM_SL + 128] = sl
    cm[:, CM_SL + 128:CM_SL + 256] = sl
    return cm


def host_colvec(inp):
    cv = np.zeros((128, NCOL), np.float32)
    cv[:, CV_EPS] = RMS_EPS
    cv[:, CV_GNEPS] = GN_EPS
    fn = np.asarray(inp["ffn_norm"], np.float32).reshape(4, DC, 128)
    for li in range(4):
        cv[:, CV_FFN + li * 8:CV_FFN + li * 8 + 8] = fn[li].T
    cv[:, CV_AB:CV_AB + 8] = np.asarray(inp["ab_norm"], np.float32).reshape(DC, 128).T
    cv[:, CV_C:CV_C + 8] = np.asarray(inp["c_norm"], np.float32).reshape(DC, 128).T
    cv[:, CV_QN] = np.tile(np.asarray(inp["q_norm"], np.float32).reshape(64), 2)
    cv[:, CV_KN] = np.tile(np.asarray(inp["k_norm"], np.float32).reshape(64), 2)
    cv[:, CV_PS:CV_PS + 4] = np.asarray(inp["pool_scale"], np.float32).reshape(4, 128).T
    cv[:, CV_SINK:CV_SINK + 8] = np.asarray(inp["attn_sinks"], np.float32).reshape(1, 8)
    mu = np.asarray(inp["c_mu"], np.float32).reshape(6, DC, 128)
    for i in range(6):
        cv[:, CV_MU + i * 8:CV_MU + i * 8 + 8] = mu[i].T
    cv[:, CV_LNW:CV_LNW + 8] = np.asarray(inp["c_lnx_w"], np.float32).reshape(DC, 128).T
    cv[:, CV_LNB:CV_LNB + 8] = np.asarray(inp["c_lnx_b"], np.float32).reshape(DC, 128).T
    return cv


class Builder:
    def __init__(self, T, stages):
        self.T = T
        self.NT = T // TT
        self.stages = stages
        nc = bass.Bass("TRN2", target_bir_lowering=False)
        self.nc = nc
        self.c = Ctx(nc)

    def dram_in(self, name, shape):
        return self.nc.dram_tensor(name, list(shape), F32, kind="ExternalInput").ap()

    def E(self, eng, fn, reads=(), writes=()):
        self.c.emit(eng, fn, reads=reads, writes=writes)

    def ar_reset(self):
        self.ar_ptr = 0

    def ar_alloc(self, free_shape, dt):
        n = 1
        for s_ in free_shape:
            n *= s_
        nbytes = n * (2 if dt == BF16 else 4)
        nb = (nbytes + 255) // 256
        b0 = self.ar_ptr
        self.ar_ptr += nb
        assert self.ar_ptr <= ARENA_KB * 4, f"arena overflow {self.ar_ptr}"
        self.ar_peak = max(getattr(self, "ar_peak", 0), self.ar_ptr)
        ap = self.arena[:, b0 * 64:b0 * 64 + (nbytes + 3) // 4]
        if dt == BF16:
            ap = ap.bitcast(BF16)
        if len(free_shape) == 2:
            ap = ap.rearrange("p (a b) -> p a b", a=free_shape[0])
        elif len(free_shape) == 3:
            ap = ap.rearrange("p (a b c) -> p a b c", a=free_shape[0], b=free_shape[1])
        return Buf(ap, self.ar_t[b0:b0 + nb])

    def bank(self, b):
        return self.psb[b], [self.psh_t[b]]

    def bk(self, b, lo, hi):
        return self.psb[b][:, lo:hi], [self.psh_t[b]]

    def gload(self, src):
        i = self.gcount % NG_RING
        self.gcount += 1
        buf, t, sl = self.gring[i], self.gring_t[i], self.gring_s[i]
        assert (not t.w) or t.r, "G ring buffer overwritten before it was consumed"
        self.c.emit("pool", lambda e: e.dma_start(out=buf[:].rearrange("p (c n) -> p c n", c=DC), in_=src),
                    writes=[t], slot=sl)
        return buf[:].rearrange("p (c n) -> p c n", c=DC), t

    def dload(self, src):
        i = self.dcount % 2
        self.dcount += 1
        buf, t, sl = self.dring[i], self.dring_t[i], self.dring_s[i]
        assert (not t.w) or t.r, "D ring buffer overwritten before it was consumed"
        self.c.emit("pool", lambda e: e.dma_start(out=buf[:].rearrange("p (c n) -> p c n", c=FC), in_=src),
                    writes=[t], slot=sl)
        return buf[:].rearrange("p (c n) -> p c n", c=FC), t

    def lazy(self, srcs, ahead):
        items = [None] * len(srcs)
        state = {"next": 0}

        def get(i):
            while state["next"] <= min(i + ahead, len(srcs) - 1):
                j = state["next"]
                items[j] = self.gload(srcs[j])
                state["next"] += 1
            return items[i]
        return get

    def build(self):
        nc, c = self.nc, self.c
        T = self.T
        st = self.stages
        self.x = self.dram_in("x", [T, D])
        self.y = nc.dram_tensor("y", [T, D], F32, kind="ExternalOutput").ap()
        self.colvec_d = self.dram_in("colvec", [128, NCOL])
        self.cmat_d = self.dram_in("cmat", [128, NCMAT])
        need_ffn = any(s.startswith("ffn") for s in st)
        if need_ffn:
            self.wg = self.dram_in("wg", [4, D, DFF])
            self.wu = self.dram_in("wu", [4, D, DFF])
            self.wd = self.dram_in("wd", [4, DFF, D])
        self.alloc_common()
        self.setup_consts()
        if "ab" in st:
            self.setup_ab()
        if "rwkv" in st:
            self.setup_rwkv()
        for ti in range(self.NT):
            self.load_x(ti)
            for s in st:
                self.run_stage(ti, s)
            self.store_x(ti)
        c.finalize()
        return nc

    def alloc_common(self):
        c = self.c
        self.colvec = c.sb("colvec", [128, NCOL], F32)
        self.colvec_t = Tk("colvec")
        self.cmat = c.sb("cmat", [128, NCMAT], F32)
        self.cmat_t = Tk("cmat")
        self.cbf = c.sb("cbf", [128, 512], BF16)
        self.cbf_t = Tk("cbf")
        self.xT = c.sb("xT", [128, DC, TT], F32)
        self.xT_t = [Tk(f"xT{i}") for i in range(DC)]
        self.xin_slot = c.slot("xin")
        self.xout_slot = c.slot("xout")
        c.out_slots.append(self.xout_slot)
        self.h = c.sb("h", [128, DC, TT + 1], BF16)
        self.h_t = [Tk(f"h{i}") for i in range(DC)]
        self.arena = c.sb("arena", [128, ARENA_KB * 256], F32)
        self.ar_t = [Tk(f"ar{i}") for i in range(ARENA_KB * 4)]
        self.ar_ptr = 0
        self.psb = [c.ps(f"psb{i}", [128, 512], F32) for i in range(8)]
        self.psh_t = [Tk(f"psb{i}", excl=True) for i in range(8)]
        self.const_slot = c.slot("const", group=True)
        self.gring = [c.sb(f"gring{i}", [128, 2048], BF16) for i in range(NG_RING)]
        self.gring_t = [Tk(f"gring{i}") for i in range(NG_RING)]
        self.gring_s = [c.slot(f"gring{i}") for i in range(NG_RING)]
        self.dring = [c.sb(f"dring{i}", [128, FC * 128], BF16) for i in range(2)]
        self.dring_t = [Tk(f"dring{i}") for i in range(2)]
        self.dring_s = [c.slot(f"dring{i}") for i in range(2)]
        self.gcount = 0
        self.dcount = 0

    def setup_consts(self):
        c = self.c
        cv, cm = self.colvec, self.cmat
        c.emit("sp", lambda e: e.dma_start(out=cv[:], in_=self.colvec_d[:, :]),
               writes=[self.colvec_t], slot=self.const_slot)
        c.emit("sp", lambda e: e.dma_start(out=cm[:], in_=self.cmat_d[:, :]),
               writes=[self.cmat_t], slot=self.const_slot)
        cb = self.cbf
        self.E("dve", lambda e: e.tensor_copy(cb[:, 0:128], cm[:, CM_ONES:CM_ONES + 128]),
               [self.cmat_t], [self.cbf_t])
        self.E("dve", lambda e: e.tensor_copy(cb[:, 128:256], cm[:, CM_IDENT:CM_IDENT + 128]),
               [self.cmat_t, self.cbf_t], [self.cbf_t])
        self.E("dve", lambda e: e.tensor_copy(cb[:, 256:384], cm[:, CM_BD:CM_BD + 128]),
               [self.cmat_t, self.cbf_t], [self.cbf_t])
        self.E("dve", lambda e: e.tensor_copy(cb[:, 384:512], cm[:, CM_UI:CM_UI + 128]),
               [self.cmat_t, self.cbf_t], [self.cbf_t])
        self.E("dve", lambda e: e.memset(self.h[:], 0.0), [], self.h_t)

    def load_x(self, ti):
        c = self.c
        self.ar_reset()
        xin = self.ar_alloc([NB, D], F32)
        src = self.x[ti * TT:(ti + 1) * TT, :].rearrange("(b p) d -> p b d", p=128)
        c.emit("sp", lambda e: e.dma_start(out=xin.ap, in_=src), writes=xin.tks, slot=self.xin_slot)
        ident = self.cmat[:, CM_IDENT:CM_IDENT + 128]
        for dc in range(DC):
            ps, pst = self.bank(dc % 2)
            for b in range(NB):
                self.E("pe", lambda e, b=b, dc=dc, ps=ps: e.transpose(
                    ps[:, b * 128:(b + 1) * 128], xin.ap[:, b, dc * 128:(dc + 1) * 128], ident),
                    xin.tks + [self.cmat_t], pst)
            if dc % 2 == 0:
                self.E("act", lambda e, dc=dc, ps=ps: e.copy(self.xT[:, dc, :], ps[:]), pst, [self.xT_t[dc]])
            else:
                self.E("dve", lambda e, dc=dc, ps=ps: e.tensor_copy(self.xT[:, dc, :], ps[:]), pst, [self.xT_t[dc]])

    def store_x(self, ti):
        c = self.c
        self.ar_reset()
        xo = self.ar_alloc([NB, D], F32)
        ident = self.cmat[:, CM_IDENT:CM_IDENT + 128]
        k = 0
        for b in range(NB):
            for half in range(2):
                ps, pst = self.bank(k % 2)
                k += 1
                for q in range(4):
                    dc = half * 4 + q
                    self.E("pe", lambda e, b=b, dc=dc, q=q, ps=ps: e.transpose(
                        ps[:, q * 128:(q + 1) * 128], self.xT[:, dc, b * 128:(b + 1) * 128], ident),
                        [self.xT_t[dc], self.cmat_t], pst)
                if k % 2 == 0:
                    self.E("act", lambda e, b=b, half=half, ps=ps: e.copy(xo.ap[:, b, half * 512:(half + 1) * 512], ps[:]),
                           pst, xo.tks)
                else:
                    self.E("dve", lambda e, b=b, half=half, ps=ps: e.tensor_copy(xo.ap[:, b, half * 512:(half + 1) * 512], ps[:]),
                           pst, xo.tks)
        dst = self.y[ti * TT:(ti + 1) * TT, :].rearrange("(b p) d -> p b d", p=128)
        c.emit("sp", lambda e: e.dma_start(out=dst, in_=xo.ap), reads=xo.tks, slot=self.xout_slot)

    def rmsnorm(self, gcol):
        ones = self.cbf[:, 0:128]
        sq = [self.ar_alloc([TT], BF16) for _ in range(2)]
        rstd = self.ar_alloc([TT], F32)
        ps, pst = self.bank(2)
        for dc in range(DC):
            s_ = sq[dc % 2]
            self.E("act", lambda e, dc=dc, s_=s_: e.activation(s_.ap, self.xT[:, dc, :], AF.Square),
                   [self.xT_t[dc]], s_.tks)
            self.E("pe", lambda e, dc=dc, s_=s_: e.matmul(ps[:], ones, s_.ap, start=(dc == 0), stop=(dc == DC - 1)),
                   s_.tks + [self.cbf_t], pst)
        self.E("act", lambda e: e.activation(rstd.ap, ps[:], AF.Sqrt, bias=self.colvec[:, CV_EPS:CV_EPS + 1], scale=1.0 / D),
               pst + [self.colvec_t], rstd.tks)
        self.E("dve", lambda e: e.reciprocal(rstd.ap, rstd.ap), rstd.tks, rstd.tks)
        for dc in range(DC):
            self.E("dve", lambda e, dc=dc: e.scalar_tensor_tensor(
                self.h[:, dc, 1:TT + 1], self.xT[:, dc, :], self.colvec[:, gcol + dc:gcol + dc + 1], rstd.ap,
                ALU.mult, ALU.mult),
                [self.xT_t[dc], self.colvec_t] + rstd.tks, [self.h_t[dc]])

    def hv(self, dc, lo=0, hi=TT):
        return self.h[:, dc, 1 + lo:1 + hi]

    def ffn(self, li):
        self.ar_reset()
        wgv = self.wg[li].rearrange("(c p) f -> p c f", p=128)
        wuv = self.wu[li].rearrange("(c p) f -> p c f", p=128)
        wdv = self.wd[li].rearrange("(c p) n -> p c n", p=128)
        NG = FC // 2
        q = []

        def issue(g):
            q.append((self.gload(wgv[:, :, g * 256:(g + 1) * 256]), self.gload(wuv[:, :, g * 256:(g + 1) * 256])))
        issue(0)
        dq = [self.dload(wdv[:, :, 0:128])]
        self.rmsnorm(CV_FFN + li * 8)
        act = [self.ar_alloc([TT], BF16) for _ in range(FC)]
        silu = [self.ar_alloc([TT], F32) for _ in range(2)]
        for g in range(NG):
            if g + 1 < NG:
                issue(g + 1)
            (gb, gt), (ub, ut) = q.pop(0)
            for fc in range(2):
                f = g * 2 + fc
                psg, psgt = self.bank(3 + (f % 2))
                psu, psut = self.bank(5 + (f % 2))
                for dc in range(DC):
                    self.E("pe", lambda e, dc=dc, fc=fc, gb=gb, psg=psg: e.matmul(
                        psg[:], gb[:, dc, fc * 128:(fc + 1) * 128], self.hv(dc), start=(dc == 0), stop=(dc == DC - 1)),
                        [gt, self.h_t[dc]], psgt)
                for dc in range(DC):
                    self.E("pe", lambda e, dc=dc, fc=fc, ub=ub, psu=psu: e.matmul(
                        psu[:], ub[:, dc, fc * 128:(fc + 1) * 128], self.hv(dc), start=(dc == 0), stop=(dc == DC - 1)),
                        [ut, self.h_t[dc]], psut)
                st_ = silu[f % 2]
                self.E("act", lambda e, st_=st_, psg=psg: e.activation(st_.ap, psg[:], AF.Silu), psgt, st_.tks)
                self.E("dve", lambda e, st_=st_, psu=psu, f=f: e.tensor_tensor(act[f].ap, st_.ap, psu[:], ALU.mult),
                       st_.tks + psut, act[f].tks)
        for dc in range(DC):
            if dc + 1 < DC:
                dq.append(self.dload(wdv[:, :, (dc + 1) * 128:(dc + 2) * 128]))
            db, dt_ = dq.pop(0)
            psy, psyt = self.bank(7 if dc % 2 == 0 else 2)
            for f in range(FC):
                self.E("pe", lambda e, f=f, db=db, psy=psy: e.matmul(
                    psy[:], db[:, f, :], act[f].ap, start=(f == 0), stop=(f == FC - 1)),
                    [dt_] + act[f].tks, psyt)
            self.E("dve", lambda e, dc=dc, psy=psy: e.scalar_tensor_tensor(
                self.xT[:, dc, :], psy[:], 0.5, self.xT[:, dc, :], ALU.mult, ALU.add),
                psyt + [self.xT_t[dc]], [self.xT_t[dc]])

    def setup_ab(self):
        c = self.c
        self.w_in_d = self.dram_in("w_in", [D, 1280])
        self.w_out_d = self.dram_in("w_out", [D, D])
        self.pool_w_d = self.dram_in("pool_w", [4, 128, 128])
        self.pw = c.sb("pw", [128, 4, 128], BF16)
        self.pw_t = Tk("pw")
        self.abw_slot = c.slot("abw", group=True)
        self.kcar = c.sb("kcar", [128, 128], BF16)
        self.kcar_t = Tk("kcar")
        self.vcar = c.sb("vcar", [128, 128], BF16)
        self.vcar_t = Tk("vcar")
        self.pcar = c.sb("pcar", [128, 4, 16], F32)
        self.pcar_t = Tk("pcar")
        self.esk = c.sb("esink", [128, 512], F32)
        self.esb = c.sb("esb", [128, 8], F32)
        self.esk_t = Tk("esink")
        src4 = self.pool_w_d.rearrange("g c d -> c g d")
        c.emit("pool", lambda e: e.dma_start(out=self.pw[:], in_=src4), writes=[self.pw_t], slot=self.abw_slot)
        self.E("act", lambda e: e.activation(self.esb[:], self.colvec[:, CV_SINK:CV_SINK + 8], AF.Exp),
               [self.colvec_t], [self.esk_t])
        for hk in range(2):
            for g in range(4):
                hq = hk * 4 + g
                self.E("dve", lambda e, hk=hk, g=g, hq=hq: e.tensor_copy(
                    self.esk[hk * 64:(hk + 1) * 64, g * 128:(g + 1) * 128],
                    self.esb[hk * 64:(hk + 1) * 64, hq:hq + 1].to_broadcast([64, 128])),
                    [self.esk_t], [self.esk_t])
        self.E("dve", lambda e: e.memset(self.pcar[:], 0.0), [], [self.pcar_t])
        self.E("dve", lambda e: e.memset(self.kcar[:], 0.0), [], [self.kcar_t])
        self.E("dve", lambda e: e.memset(self.vcar[:], 0.0), [], [self.vcar_t])

    def ab(self, ti):
        self.ar_reset()
        bd = self.cbf[:, 256:384]
        ones = self.cbf[:, 0:128]
        winv = self.w_in_d.rearrange("(c p) n -> p c n", p=128)
        woutv = self.w_out_d.rearrange("(c p) n -> p c n", p=128)
        wq = self.lazy([winv[:, :, i * 256:(i + 1) * 256] for i in range(5)], 1)
        wq(0)
        self.rmsnorm(CV_AB)
        qn = [self.ar_alloc([TT], BF16) for _ in range(4)]
        kn = self.ar_alloc([128 + TT], BF16)
        vtm = self.ar_alloc([NB + 1, 128], BF16)
        qraw = [self.ar_alloc([TT], F32) for _ in range(2)]
        sqb = [self.ar_alloc([TT], BF16) for _ in range(2)]
        rstd2 = self.ar_alloc([TT], F32)
        W = 16 + TT
        pbuf = [self.ar_alloc([W], F32) for _ in range(4)]
        ptmp = [self.ar_alloc([W], F32) for _ in range(2)]
        pdiff = [self.ar_alloc([TT], BF16) for _ in range(4)]
        opool = [self.ar_alloc([TT], BF16) for _ in range(4)]
        oT = self.ar_alloc([4, TT], BF16)
        ebuf = [self.ar_alloc([512], F32) for _ in range(2)]
        pt = [[self.ar_alloc([512], BF16) for _ in range(2)] for _ in range(2)]
        rden = [self.ar_alloc([512], F32) for _ in range(2)]
        self.E("dve", lambda e: e.tensor_copy(kn.ap[:, 0:128], self.kcar[:]), [self.kcar_t], kn.tks)
        self.E("dve", lambda e: e.tensor_copy(vtm.ap[:, 0, :], self.vcar[:]), [self.vcar_t], vtm.tks)
        for gi in range(4):
            self.E("dve", lambda e, gi=gi: e.tensor_copy(pbuf[gi].ap[:, 0:16], self.pcar[:, gi, :]), [self.pcar_t], pbuf[gi].tks)
        for ci in range(5):
            wb, wt = wq(ci // 2)
            n0 = (ci % 2) * 128
            ps, pst = self.bank(ci % 2)
            for dc in range(DC):
                self.E("pe", lambda e, dc=dc, n0=n0, ps=ps, wb=wb: e.matmul(
                    ps[:], wb[:, dc, n0:n0 + 128], self.hv(dc), start=(dc == 0), stop=(dc == DC - 1)),
                    [wt, self.h_t[dc]], pst)
            qr, s_ = qraw[ci % 2], sqb[ci % 2]
            ps2, ps2t = self.bank(2)
            self.E("act", lambda e, qr=qr, ps=ps: e.copy(qr.ap, ps[:]), pst, qr.tks)
            self.E("act", lambda e, s_=s_, ps=ps: e.activation(s_.ap, ps[:], AF.Square), pst, s_.tks)
            self.E("pe", lambda e, s_=s_, ps2=ps2: e.matmul(ps2[:], bd, s_.ap, start=True, stop=True),
                   s_.tks + [self.cbf_t], ps2t)
            self.E("act", lambda e, ps2=ps2: e.activation(rstd2.ap, ps2[:], AF.Sqrt,
                                                          bias=self.colvec[:, CV_EPS:CV_EPS + 1], scale=1.0 / 64),
                   ps2t + [self.colvec_t], rstd2.tks)
            self.E("dve", lambda e: e.reciprocal(rstd2.ap, rstd2.ap), rstd2.tks, rstd2.tks)
            if ci < 4:
                self.E("dve", lambda e, ci=ci, qr=qr: e.scalar_tensor_tensor(
                    qn[ci].ap, qr.ap, self.colvec[:, CV_QN:CV_QN + 1], rstd2.ap, ALU.mult, ALU.mult),
                    qr.tks + rstd2.tks + [self.colvec_t], qn[ci].tks)
            else:
                self.E("dve", lambda e, qr=qr: e.scalar_tensor_tensor(
                    kn.ap[:, 128:128 + TT], qr.ap, self.colvec[:, CV_KN:CV_KN + 1], rstd2.ap, ALU.mult, ALU.mult),
                    qr.tks + rstd2.tks + [self.colvec_t], kn.tks)
        wb, wt = wq(2)
        ps, pst = self.bank(1)
        for b in range(NB):
            for dc in range(DC):
                self.E("pe", lambda e, dc=dc, b=b, ps=ps, wb=wb: e.matmul(
                    ps[:, b * 128:(b + 1) * 128], self.hv(dc, b * 128, (b + 1) * 128), wb[:, dc, 128:256],
                    start=(dc == 0), stop=(dc == DC - 1)),
                    [wt, self.h_t[dc]], pst)
        self.E("act", lambda e, ps=ps: e.copy(vtm.ap[:, 1:NB + 1, :], ps[:].rearrange("p (b n) -> p b n", b=NB)),
               pst, vtm.tks)
        for gi in range(4):
            wb, wt = wq(3 + gi // 2)
            n0 = (gi % 2) * 128
            ps, pst = self.bank(gi % 2)
            for dc in range(DC):
                self.E("pe", lambda e, dc=dc, n0=n0, ps=ps, wb=wb: e.matmul(
                    ps[:], wb[:, dc, n0:n0 + 128], self.hv(dc), start=(dc == 0), stop=(dc == DC - 1)),
                    [wt, self.h_t[dc]], pst)
            self.E("act", lambda e, gi=gi, ps=ps: e.copy(pbuf[gi].ap[:, 16:16 + TT], ps[:]), pst, pbuf[gi].tks)
        woq = self.lazy([woutv[:, :, i * 256:(i + 1) * 256] for i in range(4)], 1)
        woq(0)
        for gi, w in enumerate((2, 4, 8, 16)):
            cur = pbuf[gi]
            lo = 0
            k = 0
            step = 1
            while step < w:
                nxt = ptmp[k % 2]
                lo += step
                self.E("dve", lambda e, cur=cur, nxt=nxt, lo=lo, step=step: e.tensor_tensor(
                    nxt.ap[:, lo:W], cur.ap[:, lo:W], cur.ap[:, lo - step:W - step], ALU.add),
                    cur.tks, nxt.tks)
                cur = nxt
                step *= 2
                k += 1
            self.E("dve", lambda e, cur=cur, gi=gi, w=w: e.scalar_tensor_tensor(
                pdiff[gi].ap, cur.ap[:, 16:W], 1.0 / w, pbuf[gi].ap[:, 16:W], ALU.mult, ALU.subtract),
                cur.tks + pbuf[gi].tks, pdiff[gi].tks)
            if ti == 0:
                tmp = qraw[0]
                self.E("dve", lambda e, cur=cur, gi=gi, tmp=tmp: e.tensor_tensor(
                    tmp.ap[:, 0:16], cur.ap[:, 16:32], self.cmat[:, CM_INVC + gi * 16:CM_INVC + gi * 16 + 16], ALU.mult),
                    cur.tks + [self.cmat_t], tmp.tks)
                self.E("dve", lambda e, gi=gi, tmp=tmp: e.tensor_tensor(
                    pdiff[gi].ap[:, 0:16], tmp.ap[:, 0:16], pbuf[gi].ap[:, 16:32], ALU.subtract),
                    tmp.tks + pbuf[gi].tks, pdiff[gi].tks)
            self.E("dve", lambda e, gi=gi: e.tensor_copy(self.pcar[:, gi, :], pbuf[gi].ap[:, TT:TT + 16]),
                   pbuf[gi].tks, [self.pcar_t])
            ps, pst = self.bank(gi % 2)
            self.E("pe", lambda e, gi=gi, ps=ps: e.matmul(ps[:], self.pw[:, gi, :], pdiff[gi].ap, start=True, stop=True),
                   [self.pw_t] + pdiff[gi].tks, pst)
            self.E("act", lambda e, gi=gi, ps=ps: e.activation(opool[gi].ap, ps[:], AF.Copy,
                                                               scale=self.colvec[:, CV_PS + gi:CV_PS + gi + 1]),
                   pst + [self.colvec_t], opool[gi].tks)
        for b in range(NB):
            gblk = ti * NB + b
            for hk in range(2):
                p0 = hk * 64
                kbs = [0, 1] if gblk > 0 else [1]
                ptb = pt[hk]
                for kb in kbs:
                    ps, pst = self.bank(3 + 2 * hk + kb)
                    for g in range(4):
                        self.E("pe", lambda e, ps=ps, p0=p0, b=b, kb=kb, g=g: e.matmul(
                            ps[:, g * 128:(g + 1) * 128],
                            kn.ap[p0:p0 + 64, (b + kb) * 128:(b + kb + 1) * 128],
                            qn[g].ap[p0:p0 + 64, b * 128:(b + 1) * 128], start=True, stop=True),
                            kn.tks + qn[g].tks, pst)
                    eb = ebuf[kb]
                    self.E("act", lambda e, eb=eb, ps=ps: e.activation(eb.ap, ps[:], AF.Exp, scale=0.125), pst, eb.tks)
                    m0 = CM_AM + (hk * 2 + kb) * 512
                    self.E("dve", lambda e, eb=eb, ptb=ptb, kb=kb, m0=m0: e.tensor_tensor(
                        ptb[kb].ap, eb.ap, self.cmat[:, m0:m0 + 512], ALU.mult),
                        eb.tks + [self.cmat_t], ptb[kb].tks)
                pv, pvt = self.bank(7 if hk == 0 else 0)
                den, dent = self.bank(2 if hk == 0 else 1)
                for i, kb in enumerate(kbs):
                    self.E("pe", lambda e, i=i, kb=kb, b=b, ptb=ptb, pv=pv, n=len(kbs): e.matmul(
                        pv[:, :], vtm.ap[:, b + kb, :], ptb[kb].ap, start=(i == 0), stop=(i == n - 1)),
                        vtm.tks + ptb[kb].tks, pvt)
                for i, kb in enumerate(kbs):
                    self.E("pe", lambda e, i=i, kb=kb, ptb=ptb, den=den, n=len(kbs): e.matmul(
                        den[:, :], ones, ptb[kb].ap, start=(i == 0), stop=(i == n - 1)),
                        [self.cbf_t] + ptb[kb].tks, dent)
                rd = rden[hk]
                self.E("dve", lambda e, rd=rd, den=den, p0=p0: e.tensor_tensor(
                    rd.ap[p0:p0 + 64, :], den[p0:p0 + 64, :], self.esk[p0:p0 + 64, :], ALU.add),
                    dent + [self.esk_t], rd.tks)
                self.E("dve", lambda e, rd=rd, p0=p0: e.reciprocal(rd.ap[p0:p0 + 64, :], rd.ap[p0:p0 + 64, :]), rd.tks, rd.tks)
                self.E("dve", lambda e, rd=rd, pv=pv, p0=p0, b=b: e.tensor_tensor(
                    oT.ap[p0:p0 + 64, :, b * 128:(b + 1) * 128],
                    pv[p0:p0 + 64, :].rearrange("p (g q) -> p g q", g=4),
                    rd.ap[p0:p0 + 64, :].rearrange("p (g q) -> p g q", g=4), ALU.mult),
                    pvt + rd.tks, oT.tks)
        self.E("dve", lambda e: e.tensor_copy(self.kcar[:], kn.ap[:, TT:TT + 128]), kn.tks, [self.kcar_t])
        self.E("dve", lambda e: e.tensor_copy(self.vcar[:], vtm.ap[:, NB, :]), vtm.tks, [self.vcar_t])
        for dc in range(DC):
            wb, wt = woq(dc // 2)
            n0 = (dc % 2) * 128
            ps, pst = self.bank(3 + dc % 2)
            for ch in range(4):
                self.E("pe", lambda e, ch=ch, ps=ps, wb=wb, n0=n0: e.matmul(
                    ps[:], wb[:, ch, n0:n0 + 128], oT.ap[:, ch, :], start=(ch == 0), stop=False),
                    [wt] + oT.tks, pst)
            for gi in range(4):
                self.E("pe", lambda e, gi=gi, ps=ps, wb=wb, n0=n0: e.matmul(
                    ps[:], wb[:, 4 + gi, n0:n0 + 128], opool[gi].ap, start=False, stop=(gi == 3)),
                    [wt] + opool[gi].tks, pst)
            self.E("dve", lambda e, dc=dc, ps=ps: e.tensor_tensor(self.xT[:, dc, :], ps[:], self.xT[:, dc, :], ALU.add),
                   pst + [self.xT_t[dc]], [self.xT_t[dc]])

    def setup_rwkv(self):
        c = self.c
        self.cw = {nm: self.dram_in(nm, [D, D]) for nm in ("c_w_r", "c_w_k", "c_w_v", "c_w_o")}
        w1_d = self.dram_in("c_w1", [D, 64])
        a1_d = self.dram_in("c_a1", [D, 64])
        g1_d = self.dram_in("c_g1", [D, 128])
        w2_d = self.dram_in("c_w2e", [65, D])
        a2_d = self.dram_in("c_a2e", [65, D])
        g2_d = self.dram_in("c_g2", [128, D])
        rows_d = self.dram_in("c_rows", [3, D])
        self.w1b = c.sb("w1b", [128, DC, 64], BF16)
        self.a1b = c.sb("a1b", [128, DC, 64], BF16)
        self.g1b = c.sb("g1b", [128, DC, 128], BF16)
        self.w2b = c.sb("w2b", [65, D], BF16)
        self.a2b = c.sb("a2b", [65, D], BF16)
        self.g2b = c.sb("g2b", [128, D], BF16)
        self.rowb = c.sb("rowb", [128, 3, D], F32)
        self.cw_t = Tk("cw")
        self.cw_slot = c.slot("cw", group=True)
        self.spad = c.sb("spad", [128, 16, 64], F32)
        self.sbf = c.sb("sbf", [128, 16, 64], BF16)
        self.spad_t = [Tk(f"spad{i}") for i in range(16)]
        self.sbf_t = [Tk(f"sbf{i}") for i in range(16)]
        sl = self.cw_slot
        lds = [(self.w1b[:], w1_d.rearrange("(c p) n -> p c n", p=128)),
               (self.a1b[:], a1_d.rearrange("(c p) n -> p c n", p=128)),
               (self.g1b[:], g1_d.rearrange("(c p) n -> p c n", p=128)),
               (self.w2b[:], w2_d[:, :]), (self.a2b[:], a2_d[:, :]), (self.g2b[:], g2_d[:, :])]
        for n_, (dst, src) in enumerate(lds):
            c.emit("pool", lambda e, dst=dst, src=src: e.dma_start(out=dst, in_=src),
                   writes=[self.cw_t if n_ == len(lds) - 1 else Tk("x")], slot=sl)
        self.rowb_t = Tk("rowb")
        sl2 = c.slot("cwrow", group=True)
        for i in range(3):
            c.emit("sp", lambda e, i=i: e.dma_start(out=self.rowb[:, i:i + 1, :], in_=rows_d[i:i + 1, :].partition_broadcast(128)),
                   writes=[self.rowb_t if i == 2 else Tk("x")], slot=sl2)
        self.E("dve", lambda e: e.memset(self.spad[:], 0.0), [], self.spad_t)
        self.E("dve", lambda e: e.memset(self.sbf[:], 0.0), [], self.sbf_t)

    def rwkv(self, ti):
        self.ar_reset()
        cv = self.colvec
        identb = self.cbf[:, 128:256]
        cm = self.cmat
        cwt = [self.cw_t]

        def wview(nm, q):
            return self.cw[nm].rearrange("(c p) n -> p c n", p=128)[:, :, q * 256:(q + 1) * 256]
        self.rmsnorm(CV_C)
        rb = [self.ar_alloc([D], BF16) for _ in range(NB)]
        kb_ = [self.ar_alloc([D], BF16) for _ in range(NB)]
        vb = [self.ar_alloc([D], BF16) for _ in range(NB)]
        t1e = self.ar_alloc([TT], BF16)
        a1e = self.ar_alloc([TT], BF16)
        g1T = self.ar_alloc([TT], BF16)
        yT = [self.ar_alloc([TT], BF16) for _ in range(DC)]
        mark = self.ar_ptr
        xx = [self.ar_alloc([TT], BF16) for _ in range(DC)]
        xm = [self.ar_alloc([TT], BF16) for _ in range(DC)]
        for dc in range(DC):
            self.E("dve", lambda e, dc=dc: e.tensor_tensor(xx[dc].ap, self.h[:, dc, 0:TT], self.h[:, dc, 1:TT + 1], ALU.subtract),
                   [self.h_t[dc]], xx[dc].tks)
        self.E("dve", lambda e: e.tensor_copy(self.h[:, :, 0:1], self.h[:, :, TT:TT + 1]), self.h_t, self.h_t)

        def mix(i):
            for dc in range(DC):
                self.E("dve", lambda e, dc=dc, i=i: e.scalar_tensor_tensor(
                    xm[dc].ap, xx[dc].ap, cv[:, CV_MU + i * 8 + dc:CV_MU + i * 8 + dc + 1], self.hv(dc), ALU.mult, ALU.add),
                    xx[dc].tks + [self.h_t[dc], self.colvec_t], xm[dc].tks)
        mix(1)
        ps, pst = self.bank(0)
        for dc in range(DC):
            self.E("pe", lambda e, dc=dc, ps=ps: e.matmul(ps[0:64, :], self.w1b[:, dc, :], xm[dc].ap, start=(dc == 0), stop=(dc == DC - 1)),
                   cwt + xm[dc].tks, pst)
        self.E("act", lambda e, ps=ps: e.activation(t1e.ap[0:64, :], ps[0:64, :], AF.Tanh), pst, t1e.tks)
        self.E("dve", lambda e: e.memset(t1e.ap[64:65, :], 1.0), [], t1e.tks)
        mix(4)
        ps, pst = self.bank(1)
        for dc in range(DC):
            self.E("pe", lambda e, dc=dc, ps=ps: e.matmul(ps[0:64, :], self.a1b[:, dc, :], xm[dc].ap, start=(dc == 0), stop=(dc == DC - 1)),
                   cwt + xm[dc].tks, pst)
        self.E("act", lambda e, ps=ps: e.copy(a1e.ap[0:64, :], ps[0:64, :]), pst, a1e.tks)
        self.E("dve", lambda e: e.memset(a1e.ap[64:65, :], 1.0), [], a1e.tks)
        mix(5)
        ps, pst = self.bank(0)
        for dc in range(DC):
            self.E("pe", lambda e, dc=dc, ps=ps: e.matmul(ps[:, :], self.g1b[:, dc, :], xm[dc].ap, start=(dc == 0), stop=(dc == DC - 1)),
                   cwt + xm[dc].tks, pst)
        self.E("act", lambda e, ps=ps: e.activation(g1T.ap, ps[:, :], AF.Sigmoid), pst, g1T.tks)
        hcnt = 0
        for (i, nm, dst) in ((0, "c_w_r", rb), (2, "c_w_k", kb_), (3, "c_w_v", vb)):
            mix(i)
            wl = self.lazy([wview(nm, q_) for q_ in range(4)], 1)
            for q in range(4):
                wb, wt = wl(q)
                for cb in range(NB):
                    ph, pht = self.bk(hcnt % 8, 0, 256)
                    hcnt += 1
                    for dc in range(DC):
                        self.E("pe", lambda e, dc=dc, cb=cb, ph=ph, wb=wb: e.matmul(
                            ph, xm[dc].ap[:, cb * 128:(cb + 1) * 128], wb[:, dc, :], start=(dc == 0), stop=(dc == DC - 1)),
                            [wt] + xm[dc].tks, pht)
                    if hcnt % 2 == 0:
                        self.E("act", lambda e, cb=cb, q=q, ph=ph, dst=dst: e.copy(dst[cb].ap[:, q * 256:(q + 1) * 256], ph), pht, dst[cb].tks)
                    else:
                        self.E("dve", lambda e, cb=cb, q=q, ph=ph, dst=dst: e.tensor_copy(dst[cb].ap[:, q * 256:(q + 1) * 256], ph), pht, dst[cb].tks)
        if RWDBG <= 1:
            return
        for cb in range(NB):
            self.ar_ptr = mark
            self.rwkv_chunk(ti, cb, rb[cb], kb_[cb], vb[cb], t1e, a1e, g1T, yT)
        wl = self.lazy([wview("c_w_o", q_) for q_ in range(4)], 1)
        for q in range(4):
            wb, wt = wl(q)
            for nn in range(2):
                dco = q * 2 + nn
                ps, pst = self.bank(dco % 2)
                for blk in range(DC):
                    self.E("pe", lambda e, blk=blk, nn=nn, ps=ps, wb=wb: e.matmul(
                        ps[:], wb[:, blk, nn * 128:(nn + 1) * 128], yT[blk].ap, start=(blk == 0), stop=(blk == DC - 1)),
                        [wt] + yT[blk].tks, pst)
                self.E("dve", lambda e, dco=dco, ps=ps: e.tensor_tensor(self.xT[:, dco, :], ps[:], self.xT[:, dco, :], ALU.add),
                       pst + [self.xT_t[dco]], [self.xT_t[dco]])

    def rwkv_chunk(self, ti, cb, rb, kb_, vb, t1e, a1e, g1T, yT):
        cv, cm = self.colvec, self.cmat
        identb = self.cbf[:, 128:256]
        cwt = [self.cw_t]
        cs = slice(cb * 128, (cb + 1) * 128)
        A = self.ar_alloc
        sg = A([D], F32)
        bb = A([D], F32)
        a_ = A([D], BF16)
        at, rt, kt, bt, kh, bh, bv, zb = [A([D], BF16) for _ in range(8)]
        sm = A([8, 16], F32)
        sm2 = A([1, 16], F32)
        fmT = A([DC, 4, 128], BF16)
        tmpf = [A([128], F32) for _ in range(2)]
        mark2 = self.ar_ptr
        ex = A([D], F32)
        en = A([D], F32)
        kkb = A([D], F32)
        km = A([D], F32)
        y = sg
        v3 = lambda b_: b_.ap.rearrange("p (h j) -> p h j", h=16)
        bc = lambda i: sm.ap[:, i, :].unsqueeze(2).to_broadcast([128, 16, 64])
        for hf in range(2):
            hs_ = slice(hf * 512, (hf + 1) * 512)
            ps, pst = self.bank(hf)
            self.E("pe", lambda e, ps=ps, hs_=hs_: e.matmul(ps[:], t1e.ap[0:65, cs], self.w2b[0:65, hs_], start=True, stop=True),
                   t1e.tks + cwt, pst)
            self.E("act", lambda e, ps=ps, hs_=hs_: e.activation(sg.ap[:, hs_], ps[:], AF.Sigmoid), pst, sg.tks)
            ps, pst = self.bank(2 + hf)
            self.E("pe", lambda e, ps=ps, hs_=hs_: e.matmul(ps[:], a1e.ap[0:65, cs], self.a2b[0:65, hs_], start=True, stop=True),
                   a1e.tks + cwt, pst)
            self.E("act", lambda e, ps=ps, hs_=hs_: e.activation(a_.ap[:, hs_], ps[:], AF.Sigmoid), pst, a_.tks)
        if RWDBG <= 2:
            return
        tri = self.cbf[:, 384:512]
        onesb = self.cbf[:, 0:128]
        sgh, sgl = kh, bh
        self.E("act", lambda e: e.copy(sgh.ap, sg.ap), sg.tks, sgh.tks)
        self.E("dve", lambda e: e.tensor_tensor(sgl.ap, sg.ap, sgh.ap, ALU.subtract), sg.tks + sgh.tks, sgl.tks)
        pcs = []
        for hf in range(2):
            hs_ = slice(hf * 512, (hf + 1) * 512)
            pc, pct = self.bank(4 + hf)
            self.E("pe", lambda e, pc=pc, hs_=hs_: e.matmul(pc[:], tri, sgh.ap[:, hs_], start=True, stop=False), sgh.tks + [self.cbf_t], pct)
            self.E("pe", lambda e, pc=pc, hs_=hs_: e.matmul(pc[:], tri, sgl.ap[:, hs_], start=False, stop=True), sgl.tks + [self.cbf_t], pct)
            pC, pCt = self.bank(6 + hf)
            self.E("pe", lambda e, pC=pC, hs_=hs_: e.matmul(pC[:], onesb, sgh.ap[:, hs_], start=True, stop=False), sgh.tks + [self.cbf_t], pCt)
            self.E("pe", lambda e, pC=pC, hs_=hs_: e.matmul(pC[:], onesb, sgl.ap[:, hs_], start=False, stop=True), sgl.tks + [self.cbf_t], pCt)
            pcs.append((pc, pct, pC, pCt))
        pw_, pwt = self.bk(0, 0, 128)
        for blk in range(DC):
            self.E("pe", lambda e, blk=blk: e.matmul(pw_[:, blk * 16:(blk + 1) * 16], sgh.ap[:, blk * 128:(blk + 1) * 128],
                                                     onesb[:, 0:16], start=True, stop=False), sgh.tks + [self.cbf_t], pwt)
            self.E("pe", lambda e, blk=blk: e.matmul(pw_[:, blk * 16:(blk + 1) * 16], sgl.ap[:, blk * 128:(blk + 1) * 128],
                                                     onesb[:, 0:16], start=False, stop=True), sgl.tks + [self.cbf_t], pwt)
        self.E("act", lambda e: e.activation(sm.ap[:, 7, 0:8], pw_.rearrange("p (b n) -> p b n", n=16)[:, :, 0], AF.Exp, scale=NEG_E),
               pwt, sm.tks)
        for hf in range(2):
            hs_ = slice(hf * 512, (hf + 1) * 512)
            pc, pct, pC, pCt = pcs[hf]
            self.E("act", lambda e, pc=pc, hs_=hs_: e.activation(ex.ap[:, hs_], pc[:], AF.Exp, scale=NEG_E), pct, ex.tks)
            self.E("act", lambda e, pc=pc, hs_=hs_: e.activation(en.ap[:, hs_], pc[:], AF.Exp, scale=-NEG_E), pct, en.tks)
            self.E("dve", lambda e, pc=pc, hs_=hs_: e.scalar_tensor_tensor(
                sg.ap[:, hs_], sg.ap[:, hs_], -1.0, pc[:], ALU.mult, ALU.add), sg.tks + pct, sg.tks)
        self.E("act", lambda e: e.activation(sg.ap, sg.ap, AF.Exp, scale=NEG_E), sg.tks, sg.tks)
        if RWDBG <= 3:
            return
        self.E("dve", lambda e: e.tensor_tensor(kkb.ap, kb_.ap, self.rowb[:, 0, :], ALU.mult), kb_.tks + [self.rowb_t], kkb.tks)
        self.E("dve", lambda e: e.tensor_tensor(bb.ap, kkb.ap, kkb.ap, ALU.mult), kkb.tks, bb.tks)
        self.E("dve", lambda e: e.tensor_reduce(sm.ap[:, 0, :], v3(bb), AX.X, ALU.add), bb.tks, sm.tks)
        self.E("dve", lambda e: e.tensor_scalar(sm.ap[:, 0, :], sm.ap[:, 0, :], 1e-24, None, ALU.max), sm.tks, sm.tks)
        self.E("act", lambda e: e.activation(sm.ap[:, 0, :], sm.ap[:, 0, :], AF.Sqrt), sm.tks, sm.tks)
        self.E("dve", lambda e: e.reciprocal(sm.ap[:, 0, :], sm.ap[:, 0, :]), sm.tks, sm.tks)
        self.E("dve", lambda e: e.tensor_tensor(v3(kkb), v3(kkb), bc(0), ALU.mult), kkb.tks + sm.tks, kkb.tks)
        self.E("dve", lambda e: e.scalar_tensor_tensor(km.ap, a_.ap, 1.0, self.rowb[:, 1, :], ALU.subtract, ALU.mult),
               a_.tks + [self.rowb_t], km.tks)
        self.E("dve", lambda e: e.scalar_tensor_tensor(km.ap, km.ap, 1.0, kb_.ap, ALU.add, ALU.mult), km.tks + kb_.tks, km.tks)
        self.E("dve", lambda e: e.tensor_tensor(bb.ap, kkb.ap, a_.ap, ALU.mult), kkb.tks + a_.tks, bb.tks)
        self.E("dve", lambda e: e.scalar_tensor_tensor(at.ap, kkb.ap, -1.0, sg.ap, ALU.mult, ALU.mult), kkb.tks + sg.tks, at.tks)
        self.E("dve", lambda e: e.tensor_tensor(rt.ap, rb.ap, ex.ap, ALU.mult), rb.tks + ex.tks, rt.tks)
        self.E("dve", lambda e: e.tensor_tensor(kt.ap, km.ap, en.ap, ALU.mult), km.tks + en.tks, kt.tks)
        self.E("dve", lambda e: e.tensor_tensor(bt.ap, bb.ap, en.ap, ALU.mult), bb.tks + en.tks, bt.tks)
        for hf in range(2):
            hs_ = slice(hf * 512, (hf + 1) * 512)
            pc, pct, pC, pCt = pcs[hf]
            self.E("act", lambda e, pC=pC, hs_=hs_: e.activation(ex.ap[:, hs_], pC[:], AF.Exp, scale=NEG_E), pCt, ex.tks)
        self.E("dve", lambda e: e.tensor_tensor(kh.ap, kt.ap, ex.ap, ALU.mult), kt.tks + ex.tks, kh.tks)
        self.E("dve", lambda e: e.tensor_tensor(bh.ap, bt.ap, ex.ap, ALU.mult), bt.tks + ex.tks, bh.tks)
        self.E("pool", lambda e: e.tensor_tensor(bb.ap, rb.ap, km.ap, ALU.mult), rb.tks + km.tks, bb.tks)
        self.E("pool", lambda e: e.tensor_tensor(bb.ap, bb.ap, self.rowb[:, 2, :], ALU.mult), bb.tks + [self.rowb_t], bb.tks)
        self.E("dve", lambda e: e.tensor_reduce(sm2.ap[:, 0, :], v3(bb), AX.X, ALU.add), bb.tks, sm2.tks)
        self.E("pool", lambda e: e.tensor_tensor(v3(bv), v3(vb), sm2.ap[:, 0, :].unsqueeze(2).to_broadcast([128, 16, 64]), ALU.mult),
               vb.tks + sm2.tks, bv.tks)
        if RWDBG <= 4:
            return
        for j, src in enumerate((at, rt, bt, kt)):
            for hf in range(2):
                ph, pht = self.bank((2 * j + hf) % 8)
                for b4 in range(4):
                    blk = hf * 4 + b4
                    self.E("pe", lambda e, src=src, blk=blk, b4=b4, ph=ph: e.matmul(
                        ph[:, b4 * 128:(b4 + 1) * 128], src.ap[:, blk * 128:(blk + 1) * 128], identb, start=True, stop=True),
                        src.tks + [self.cbf_t], pht)
                if (2 * j + hf) % 2 == 0:
                    self.E("act", lambda e, j=j, hf=hf, ph=ph: e.copy(fmT.ap[:, hf * 4:hf * 4 + 4, j, :], ph[:].rearrange("p (b t) -> p b t", b=4)), pht, fmT.tks)
                else:
                    self.E("dve", lambda e, j=j, hf=hf, ph=ph: e.tensor_copy(fmT.ap[:, hf * 4:hf * 4 + 4, j, :], ph[:].rearrange("p (b t) -> p b t", b=4)), pht, fmT.tks)
        if RWDBG <= 5:
            return
        self.ar_ptr = mark2
        Wb = [A([768], BF16) for _ in range(8)]
        tinv = [A([128], BF16) for _ in range(8)]
        pm = [A([256], BF16) for _ in range(8)]
        ut = [A([64], BF16) for _ in range(8)]

        def evac(hh, out, in_, reads, writes):
            if hh % 2 == 0:
                self.E("act", lambda e: e.copy(out, in_), reads, writes)
            else:
                self.E("dve", lambda e: e.tensor_copy(out, in_), reads, writes)
        for grp in range(2):
            hs = [grp * 8 + hh for hh in range(8)]
            for hh, h in enumerate(hs):
                blk, p0 = h // 2, (h % 2) * 64
                f = lambda j, n=1, blk=blk, p0=p0: fmT.ap[p0:p0 + 64, blk, j:j + n, :]
                W = Wb[hh]
                pb, pbt = self.bank(hh)
                self.E("pe", lambda e, pb=pb, f=f: e.matmul(pb[:, 0:256].rearrange("p (a t) -> p a t", a=2), f(2)[:, 0, :], f(0, 2), start=True, stop=True),
                       fmT.tks, pbt)
                self.E("pe", lambda e, pb=pb, f=f: e.matmul(pb[:, 256:384], f(3)[:, 0, :], f(1)[:, 0, :], start=True, stop=True),
                       fmT.tks, pbt)
                self.E("dve", lambda e, pb=pb, W=W: e.tensor_tensor(W.ap[:, 0:128], pb[:, 0:128], cm[:, CM_SU:CM_SU + 128], ALU.mult),
                       pbt + [self.cmat_t], W.tks)
                self.E("dve", lambda e, pb=pb, W=W: e.tensor_tensor(W.ap[:, 512:768], pb[:, 128:384], cm[:, CM_UI:CM_UI + 256], ALU.mult),
                       pbt + [self.cmat_t], W.tks)
                self.E("act", lambda e, W=W: e.copy(W.ap[:, 128:256], identb), [self.cbf_t], W.tks)
            for hh, h in enumerate(hs):
                blk, p0 = h // 2, (h % 2) * 64
                f = lambda j, n=1, blk=blk, p0=p0: fmT.ap[p0:p0 + 64, blk, j:j + n, :]
                W = Wb[hh]
                pb, pbt = self.bank(hh)
                self.E("pe", lambda e, pb=pb, f=f: e.matmul(pb[:, 0:256].rearrange("p (a t) -> p a t", a=2), f(0)[:, 0, :], f(2, 2), start=True, stop=True),
                       fmT.tks, pbt)
                self.E("dve", lambda e, pb=pb, W=W: e.tensor_tensor(W.ap[:, 256:512], pb[:, 0:256], cm[:, CM_SL:CM_SL + 256], ALU.mult),
                       pbt + [self.cmat_t], W.tks)
            for k in range(6):
                for hh in range(8):
                    W = Wb[hh]
                    X, P, XT = W.ap[:, 0:128], W.ap[:, 128:256], W.ap[:, 256:384]
                    pb, pbt = self.bank(hh)
                    self.E("pe", lambda e, pb=pb, X=X, XT=XT: e.matmul(pb[:, 0:128], XT, X, start=True, stop=True), W.tks, pbt)
                    self.E("pe", lambda e, pb=pb, P=P, XT=XT: e.matmul(pb[:, 128:256], XT, P, start=True, stop=False), W.tks, pbt)
                    self.E("pe", lambda e, pb=pb, P=P: e.matmul(pb[:, 128:256], identb, P, start=False, stop=True), W.tks + [self.cbf_t], pbt)
                    self.E("pe", lambda e, pb=pb, X=X, XT=XT: e.matmul(pb[:, 256:384], X, XT, start=True, stop=True), W.tks, pbt)
                    evac(hh, W.ap[:, 0:384], pb[:, 0:384], pbt, W.tks)
            for hh in range(8):
                W = Wb[hh]
                P, XT = W.ap[:, 128:256], W.ap[:, 256:384]
                pb, pbt = self.bank(hh)
                self.E("pe", lambda e, pb=pb, P=P, XT=XT: e.matmul(pb[:, 0:128], XT, P, start=True, stop=False), W.tks, pbt)
                self.E("pe", lambda e, pb=pb, P=P: e.matmul(pb[:, 0:128], identb, P, start=False, stop=True), W.tks + [self.cbf_t], pbt)
                evac(hh, tinv[hh].ap, pb[:, 0:128], pbt, tinv[hh].tks)
            for hh, h in enumerate(hs):
                blk = h // 2
                W = Wb[hh]
                pb, pbt = self.bank(hh)
                self.E("pe", lambda e, pb=pb, hh=hh, blk=blk: e.matmul(pb[:, 0:128], at.ap[:, blk * 128:(blk + 1) * 128], tinv[hh].ap, start=True, stop=True),
                       at.tks + tinv[hh].tks, pbt)
                self.E("pe", lambda e, pb=pb, hh=hh, W=W: e.matmul(pb[:, 128:256], W.ap[:, 384:512], tinv[hh].ap, start=True, stop=True),
                       W.tks + tinv[hh].tks, pbt)
                evac(hh, pm[hh].ap, pb[:, 0:256], pbt, pm[hh].tks)
            for hh, h in enumerate(hs):
                pb, pbt = self.bank(hh)
                self.E("pe", lambda e, pb=pb, hh=hh, h=h: e.matmul(pb[:, 0:64], pm[hh].ap[:, 0:128], self.sbf[:, h, :], start=True, stop=False),
                       pm[hh].tks + [self.sbf_t[h]], pbt)
                self.E("pe", lambda e, pb=pb, hh=hh, h=h: e.matmul(pb[:, 0:64], pm[hh].ap[:, 128:256], vb.ap[:, h * 64:(h + 1) * 64], start=False, stop=True),
                       pm[hh].tks + vb.tks, pbt)
                evac(hh, ut[hh].ap, pb[:, 0:64], pbt, ut[hh].tks)
            for hh, h in enumerate(hs):
                blk, p0 = h // 2, (h % 2) * 64
                W = Wb[hh]
                pb, pbt = self.bank(hh)
                self.E("pe", lambda e, pb=pb, h=h, blk=blk: e.matmul(pb[:, 0:64], fmT.ap[:, blk, 1, :], self.sbf[:, h, :], start=True, stop=False),
                       fmT.tks + [self.sbf_t[h]], pbt)
                self.E("pe", lambda e, pb=pb, hh=hh, W=W: e.matmul(pb[:, 0:64], W.ap[:, 512:640], ut[hh].ap, start=False, stop=False),
                       W.tks + ut[hh].tks, pbt)
                self.E("pe", lambda e, pb=pb, W=W, h=h: e.matmul(pb[:, 0:64], W.ap[:, 640:768], vb.ap[:, h * 64:(h + 1) * 64], start=False, stop=True),
                       W.tks + vb.tks, pbt)
                self.E("pe", lambda e, pb=pb, hh=hh, blk=blk: e.matmul(pb[:, 64:128], bh.ap[:, blk * 128:(blk + 1) * 128], ut[hh].ap, start=True, stop=False),
                       bh.tks + ut[hh].tks, pbt)
                self.E("pe", lambda e, pb=pb, h=h, blk=blk: e.matmul(pb[:, 64:128], kh.ap[:, blk * 128:(blk + 1) * 128], vb.ap[:, h * 64:(h + 1) * 64], start=False, stop=True),
                       kh.tks + vb.tks, pbt)
                evac(hh, y.ap[:, h * 64:(h + 1) * 64], pb[:, 0:64], pbt, y.tks)
                self.E("dve", lambda e, pb=pb, h=h, blk=blk, p0=p0: e.scalar_tensor_tensor(
                    self.spad[p0:p0 + 64, h, :], self.spad[p0:p0 + 64, h, :], sm.ap[p0:p0 + 64, 7, blk:blk + 1], pb[p0:p0 + 64, 64:128],
                    ALU.mult, ALU.add), [self.spad_t[h]] + sm.tks + pbt, [self.spad_t[h]])
                self.E("act", lambda e, h=h, p0=p0: e.copy(self.sbf[p0:p0 + 64, h, :], self.spad[p0:p0 + 64, h, :]),
                       [self.spad_t[h]], [self.sbf_t[h]])
        if RWDBG <= 6:
            return
        self.E("dve", lambda e: e.tensor_tensor(bb.ap, y.ap, y.ap, ALU.mult), y.tks, bb.tks)
        self.E("dve", lambda e: e.tensor_reduce(sm.ap[:, 2, :], v3(y), AX.X, ALU.add), y.tks, sm.tks)
        self.E("dve", lambda e: e.tensor_reduce(sm.ap[:, 3, :], v3(bb), AX.X, ALU.add), bb.tks, sm.tks)
        self.E("dve", lambda e: e.tensor_scalar(sm.ap[:, 2, :], sm.ap[:, 2, :], 1.0 / 64, None, ALU.mult), sm.tks, sm.tks)
        self.E("dve", lambda e: e.tensor_tensor(sm.ap[:, 4, :], sm.ap[:, 2, :], sm.ap[:, 2, :], ALU.mult), sm.tks, sm.tks)
        self.E("dve", lambda e: e.scalar_tensor_tensor(sm.ap[:, 3, :], sm.ap[:, 3, :], 1.0 / 64, sm.ap[:, 4, :], ALU.mult, ALU.subtract),
               sm.tks, sm.tks)
        self.E("act", lambda e: e.activation(sm.ap[:, 3, :], sm.ap[:, 3, :], AF.Sqrt, bias=cv[:, CV_GNEPS:CV_GNEPS + 1]),
               sm.tks + [self.colvec_t], sm.tks)
        self.E("dve", lambda e: e.reciprocal(sm.ap[:, 3, :], sm.ap[:, 3, :]), sm.tks, sm.tks)
        self.E("dve", lambda e: e.tensor_tensor(v3(y), v3(y), bc(2), ALU.subtract), y.tks + sm.tks, y.tks)
        self.E("dve", lambda e: e.tensor_tensor(v3(zb), v3(y), bc(3), ALU.mult), y.tks + sm.tks, zb.tks)
        for blk in range(DC):
            phb, pht = self.bk(blk % 4, 0, 256)
            self.E("pe", lambda e, blk=blk, phb=phb: e.matmul(phb[:, 0:128], zb.ap[:, blk * 128:(blk + 1) * 128], identb, start=True, stop=True),
                   zb.tks + [self.cbf_t], pht)
            self.E("pe", lambda e, blk=blk, phb=phb: e.matmul(phb[:, 128:256], bv.ap[:, blk * 128:(blk + 1) * 128], identb, start=True, stop=True),
                   bv.tks + [self.cbf_t], pht)
            pg, pgt = self.bk(4 + blk % 4, 0, 256)
            self.E("pe", lambda e, blk=blk, pg=pg: e.matmul(pg[:, 0:128], self.g2b[:, blk * 128:(blk + 1) * 128], g1T.ap[:, cs], start=True, stop=True),
                   [self.cw_t] + g1T.tks, pgt)
            tf = tmpf[blk % 2]
            self.E("dve", lambda e, blk=blk, phb=phb, tf=tf: e.tensor_scalar(
                tf.ap, phb[:, 0:128], cv[:, CV_LNW + blk:CV_LNW + blk + 1], cv[:, CV_LNB + blk:CV_LNB + blk + 1], ALU.mult, ALU.add),
                pht + [self.colvec_t], tf.tks)
            self.E("dve", lambda e, phb=phb, tf=tf: e.tensor_tensor(tf.ap, tf.ap, phb[:, 128:256], ALU.add), pht + tf.tks, tf.tks)
            self.E("dve", lambda e, blk=blk, pg=pg, tf=tf: e.tensor_tensor(yT[blk].ap[:, cs], tf.ap, pg[:, 0:128], ALU.mult),
                   tf.tks + pgt, yT[blk].tks)


    def run_stage(self, ti, s):
        if s.startswith("ffn"):
            self.ffn(int(s[3:]))
        elif s == "ab":
            self.ab(ti)
        elif s == "rwkv":
            self.rwkv(ti)
        elif s == "dbgnorm":
            self.ar_reset()
            self.rmsnorm(0)
            for dc in range(DC):
                self.E("dve", lambda e, dc=dc: e.tensor_copy(self.xT[:, dc, :], self.hv(dc)),
                       [self.h_t[dc]], [self.xT_t[dc]])


_CACHE = {}


def get_program(T, stages):
    key = (T, tuple(stages))
    if key not in _CACHE:
        b = Builder(T, list(stages))
        nc = b.build()
        _CACHE[key] = (nc, b)
    return _CACHE[key]


ALL_STAGES = ["ffn0", "ab", "ffn1", "ffn2", "rwkv", "ffn3"]


def make_in_maps(inp, T, stages, ncores):
    x = np.asarray(inp["x"], np.float32)
    f32 = lambda a: np.ascontiguousarray(np.asarray(a, np.float32))
    common = {"colvec": host_colvec(inp), "cmat": host_consts()}
    if any(s_.startswith("ffn") for s_ in stages):
        common["wg"] = f32(inp["ffn_w_gate"]).reshape(4, D, DFF)
        common["wu"] = f32(inp["ffn_w_up"]).reshape(4, D, DFF)
        common["wd"] = f32(inp["ffn_w_down"]).reshape(4, DFF, D)
    if "ab" in stages:
        w_in = f32(inp["ab_w_in"]).reshape(D, 1280)
        qcols = w_in[:, :512].reshape(D, 2, 4, 64).transpose(0, 2, 1, 3).reshape(D, 512)
        common["w_in"] = np.ascontiguousarray(np.concatenate([qcols, w_in[:, 512:]], axis=1))
        w_out = f32(inp["ab_w_out"]).reshape(D, D)
        arows = w_out[:512].reshape(2, 4, 64, D).transpose(1, 0, 2, 3).reshape(512, D)
        common["w_out"] = np.ascontiguousarray(np.concatenate([arows, w_out[512:]], axis=0))
        common["pool_w"] = f32(inp["pool_w"]).reshape(4, 128, 128)
    if "rwkv" in stages:
        for nm in ("c_w_r", "c_w_k", "c_w_v", "c_w_o"):
            common[nm] = f32(inp[nm]).reshape(D, D)
        common["c_w1"] = f32(inp["c_w1"]).reshape(D, 64)
        common["c_a1"] = f32(inp["c_a1"]).reshape(D, 64)
        common["c_g1"] = f32(inp["c_g1"]).reshape(D, 128)
        common["c_w2e"] = np.ascontiguousarray(np.concatenate([f32(inp["c_w2"]).reshape(64, D), f32(inp["c_w0"]).reshape(1, D)], axis=0))
        common["c_a2e"] = np.ascontiguousarray(np.concatenate([f32(inp["c_a2"]).reshape(64, D), f32(inp["c_a0"]).reshape(1, D)], axis=0))
        common["c_g2"] = f32(inp["c_g2"]).reshape(128, D)
        common["c_rows"] = np.ascontiguousarray(np.stack([f32(inp["c_k_k"]).reshape(D), f32(inp["c_k_a"]).reshape(D),
                                                          f32(inp["c_r_k"]).reshape(D)], axis=0))
    in_maps = []
    for ci in range(ncores):
        m = dict(common)
        m["x"] = np.ascontiguousarray(x[ci, :T])
        in_maps.append(m)
    return in_maps


def run(inp, T=SEQ, stages=ALL_STAGES, ncores=8, trace=False):
    nc, b = get_program(T, stages)
    in_maps = make_in_maps(inp, T, stages, ncores)
    res = run_bass_kernel_spmd(nc, in_maps, core_ids=list(range(ncores)), trace=trace)
    out = np.stack([np.asarray(r["y"]) for r in res.results], axis=0)
    return out, res


def kernel(**inputs):
    out, _ = run(inputs)
    return out.astype(np.float32)
```

```python
import contextlib
import os
import numpy as np
import concourse.bass as bass
import concourse.mybir as mybir
from concourse.bass_utils import run_bass_kernel_spmd

F32 = mybir.dt.float32
BF16 = mybir.dt.bfloat16
ALU = mybir.AluOpType
AF = mybir.ActivationFunctionType
AX = mybir.AxisListType

D = 1024
DC = 8
DFF = 2816
FC = 22
TT = 512
NB = TT // 128
SEQ = 8192
RMS_EPS = 1e-6
GN_EPS = 64e-5
NEG_E = -float(np.exp(-0.5))
ENGS = ["pe", "act", "dve", "pool", "sp"]
SEM_CH = int(os.environ.get("SEM_CH", "12000"))
ARENA_KB = 98
RWDBG = float(os.environ.get('RWDBG', '9'))
NG_RING = 6


class Tk:
    __slots__ = ("name", "w", "r", "excl")

    def __init__(self, name, excl=False):
        self.name = name
        self.w = {}
        self.r = {}
        self.excl = excl


class Buf:
    __slots__ = ("ap", "tks")

    def __init__(self, ap, tks):
        self.ap = ap
        self.tks = tks


class Slot:
    def __init__(self, key, group=False):
        self.key = key
        self.count = 0
        self.sem = None
        self.group = group


class Op:
    __slots__ = ("eng", "fn", "deps", "tok", "slot", "snap", "waits", "signal", "sig")

    def __init__(self, eng, fn, deps, tok, slot):
        self.eng = eng
        self.fn = fn
        self.deps = deps
        self.tok = tok
        self.slot = slot
        self.snap = None
        self.waits = ()
        self.signal = False
        self.sig = None


class Ctx:
    def __init__(self, nc):
        self.nc = nc
        self.oplist = []
        self.eng_ops = {e: [] for e in ENGS}
        self.tokmap = {}
        self.slots = []
        self.es = contextlib.ExitStack()
        self.out_slots = []

    def sb(self, name, shape, dt):
        return self.es.enter_context(self.nc.sbuf_tensor("sb_" + name, list(shape), dt))

    def ps(self, name, shape, dt=F32):
        return self.es.enter_context(self.nc.psum_tensor("ps_" + name, list(shape), dt))

    def slot(self, name, group=False):
        s = Slot(("dma", name), group)
        self.slots.append(s)
        return s

    def emit(self, eng, fn, reads=(), writes=(), slot=None):
        if any(t.excl for t in reads):
            writes = list(writes) + [t for t in reads if t.excl]
            reads = [t for t in reads if not t.excl]
        deps = {}
        for t in reads:
            for k, v in t.w.items():
                if deps.get(k, -1) < v:
                    deps[k] = v
        for t in writes:
            for k, v in t.w.items():
                if deps.get(k, -1) < v:
                    deps[k] = v
            for k, v in t.r.items():
                if deps.get(k, -1) < v:
                    deps[k] = v
        if eng == "pe":
            deps.pop("pe", None)
        if slot is not None:
            slot.count += 1
            tok = (slot.key, slot.count)
        else:
            tok = (eng, len(self.eng_ops[eng]))
        op = Op(eng, fn, deps, tok, slot)
        self.oplist.append(op)
        self.eng_ops[eng].append(op)
        self.tokmap[tok] = op
        k, v = tok
        for t in reads:
            if t.r.get(k, -1) < v:
                t.r[k] = v
        for t in writes:
            t.w = {k: v}
            t.r = {}
        return tok

    def finalize(self):
        nc = self.nc
        known = {e: {} for e in ENGS}
        for op in self.oplist:
            kn = known[op.eng]
            waits = []
            copied = False
            for k, v in op.deps.items():
                if kn.get(k, -1) >= v:
                    continue
                if not copied:
                    kn = dict(kn)
                    known[op.eng] = kn
                    copied = True
                waits.append((k, v))
                prod = self.tokmap[(k, v)]
                prod.signal = True
                for kk, vv in prod.snap.items():
                    if kn.get(kk, -1) < vv:
                        kn[kk] = vv
                kn[k] = v
            op.waits = waits
            op.snap = kn
        self.eng_sems = {e: [] for e in ENGS}
        for e in ENGS:
            n = 0
            for op in self.eng_ops[e]:
                if op.slot is None and op.signal:
                    op.sig = n
                    n += 1
            nch = (n + SEM_CH - 1) // SEM_CH
            for c in range(nch):
                self.eng_sems[e].append(self.es.enter_context(nc.semaphore(f"s_{e}_{c}")))
        for s in self.slots:
            if s.count > 0:
                s.sem = self.es.enter_context(nc.semaphore("d_" + s.key[1]))

        def resolve(k, v):
            if isinstance(k, tuple):
                s = self.tokmap[(k, v)].slot
                return s.sem, 16 * (s.count if s.group else v)
            sig = self.tokmap[(k, v)].sig
            return self.eng_sems[k][sig // SEM_CH], sig % SEM_CH + 1

        ctx = self

        def run_engine(ename):
            def body(e):
                for op in ctx.eng_ops[ename]:
                    ws = [resolve(k, v) for (k, v) in op.waits]
                    if op.slot is not None:
                        for (s, v) in ws:
                            e.wait_ge(s, v)
                        ins = op.fn(e)
                        ins.then_inc(op.slot.sem, 16)
                    else:
                        for (s, v) in ws[:-1]:
                            e.wait_ge(s, v)
                        ins = op.fn(e)
                        if ws:
                            ins._wait_ge(*ws[-1])
                        if op.signal:
                            ins.then_inc(ctx.eng_sems[ename][op.sig // SEM_CH], 1)
                if ename == "sp":
                    for s in ctx.out_slots:
                        if s.count:
                            e.wait_ge(s.sem, 16 * s.count)
            return body

        with nc.Block() as block:
            block.tensor(run_engine("pe"))
            block.scalar(run_engine("act"))
            block.vector(run_engine("dve"))
            block.gpsimd(run_engine("pool"))
            block.sync(run_engine("sp"))
        self.stats = {e: len(self.eng_ops[e]) for e in ENGS}
        self.stats["waits"] = sum(len(op.waits) for op in self.oplist)
        self.stats["sems"] = sum(len(v) for v in self.eng_sems.values()) + sum(1 for s in self.slots if s.count)


CV_FFN = 0
CV_AB = 32
CV_C = 40
CV_QN = 48
CV_KN = 49
CV_PS = 50
CV_MU = 54
CV_SINK = 102
CV_LNW = 110
CV_LNB = 118
CV_GNEPS = 126
CV_EPS = 127
NCOL = 128
CM_IDENT = 0
CM_ONES = 128
CM_BD = 256
CM_AM = 384
CM_INVC = CM_AM + 2048
CM_TRIS = CM_INVC + 64
CM_ONESS = CM_TRIS + 128
CM_SU = CM_ONESS + 128
CM_UI = CM_SU + 128
CM_SL = CM_UI + 256
NCMAT = CM_SL + 256


def host_consts():
    cm = np.zeros((128, NCMAT), np.float32)
    cm[:, CM_IDENT:CM_IDENT + 128] = np.eye(128, dtype=np.float32)
    cm[:, CM_ONES:CM_ONES + 128] = 1.0
    cm[0:64, CM_BD:CM_BD + 64] = 1.0
    cm[64:128, CM_BD + 64:CM_BD + 128] = 1.0
    kk = np.arange(128)[:, None].astype(np.float64)
    qq = np.arange(128)[None, :].astype(np.float64)
    am = np.zeros((128, 2, 2, 4, 128), np.float64)
    for hk in range(2):
        for g in range(4):
            hq = hk * 4 + g
            slope = 2.0 ** (-8.0 * (hq + 1) / 8.0)
            dist_cur = qq - kk
            am[:, hk, 1, g, :] = np.where(dist_cur >= 0, np.exp(-slope * dist_cur), 0.0)
            dist_prev = qq - kk + 128
            am[:, hk, 0, g, :] = np.where(dist_prev < 128, np.exp(-slope * dist_prev), 0.0)
    cm[:, CM_AM:CM_AM + 2048] = am.reshape(128, 2048).astype(np.float32)
    invc = np.zeros((4, 16), np.float64)
    for gi, w in enumerate((2, 4, 8, 16)):
        invc[gi] = 1.0 / np.minimum(np.arange(1, 17), w)
    cm[:, CM_INVC:CM_INVC + 64] = invc.reshape(1, 64).astype(np.float32)
    s_ = np.arange(128)
    cm[:, CM_TRIS:CM_TRIS + 128] = NEG_E * (s_[:, None] <= s_[None, :]).astype(np.float32)
    cm[:, CM_ONESS:CM_ONESS + 128] = NEG_E
    cm[:, CM_SU:CM_SU + 128] = (s_[:, None] < s_[None, :]).astype(np.float32)
    cm[:, CM_UI:CM_UI + 128] = (s_[:, None] <= s_[None, :]).astype(np.float32)
    cm[:, CM_UI + 128:CM_UI + 256] = (s_[:, None] <= s_[None, :]).astype(np.float32)
    sl = (s_[:, None] > s_[None, :]).astype(np.float32)
    cm[:, CM_SL:CM_SL + 128] = sl
    cm[:, CM_SL + 128:CM_SL + 256] = sl
    return cm


def host_colvec(inp):
    cv = np.zeros((128, NCOL), np.float32)
    cv[:, CV_EPS] = RMS_EPS
    cv[:, CV_GNEPS] = GN_EPS
    fn = np.asarray(inp["ffn_norm"], np.float32).reshape(4, DC, 128)
    for li in range(4):
        cv[:, CV_FFN + li * 8:CV_FFN + li * 8 + 8] = fn[li].T
    cv[:, CV_AB:CV_AB + 8] = np.asarray(inp["ab_norm"], np.float32).reshape(DC, 128).T
    cv[:, CV_C:CV_C + 8] = np.asarray(inp["c_norm"], np.float32).reshape(DC, 128).T
    cv[:, CV_QN] = np.tile(np.asarray(inp["q_norm"], np.float32).reshape(64), 2)
    cv[:, CV_KN] = np.tile(np.asarray(inp["k_norm"], np.float32).reshape(64), 2)
    cv[:, CV_PS:CV_PS + 4] = np.asarray(inp["pool_scale"], np.float32).reshape(4, 128).T
    cv[:, CV_SINK:CV_SINK + 8] = np.asarray(inp["attn_sinks"], np.float32).reshape(1, 8)
    mu = np.asarray(inp["c_mu"], np.float32).reshape(6, DC, 128)
    for i in range(6):
        cv[:, CV_MU + i * 8:CV_MU + i * 8 + 8] = mu[i].T
    cv[:, CV_LNW:CV_LNW + 8] = np.asarray(inp["c_lnx_w"], np.float32).reshape(DC, 128).T
    cv[:, CV_LNB:CV_LNB + 8] = np.asarray(inp["c_lnx_b"], np.float32).reshape(DC, 128).T
    return cv


class Builder:
    def __init__(self, T, stages):
        self.T = T
        self.NT = T // TT
        self.stages = stages
        nc = bass.Bass("TRN2", target_bir_lowering=False)
        self.nc = nc
        self.c = Ctx(nc)

    def dram_in(self, name, shape):
        return self.nc.dram_tensor(name, list(shape), F32, kind="ExternalInput").ap()

    def E(self, eng, fn, reads=(), writes=()):
        self.c.emit(eng, fn, reads=reads, writes=writes)

    def ar_reset(self):
        self.ar_ptr = 0

    def ar_alloc(self, free_shape, dt):
        n = 1
        for s_ in free_shape:
            n *= s_
        nbytes = n * (2 if dt == BF16 else 4)
        nb = (nbytes + 255) // 256
        b0 = self.ar_ptr
        self.ar_ptr += nb
        assert self.ar_ptr <= ARENA_KB * 4, f"arena overflow {self.ar_ptr}"
        self.ar_peak = max(getattr(self, "ar_peak", 0), self.ar_ptr)
        ap = self.arena[:, b0 * 64:b0 * 64 + (nbytes + 3) // 4]
        if dt == BF16:
            ap = ap.bitcast(BF16)
        if len(free_shape) == 2:
            ap = ap.rearrange("p (a b) -> p a b", a=free_shape[0])
        elif len(free_shape) == 3:
            ap = ap.rearrange("p (a b c) -> p a b c", a=free_shape[0], b=free_shape[1])
        return Buf(ap, self.ar_t[b0:b0 + nb])

    def bank(self, b):
        return self.psb[b], [self.psh_t[b]]

    def bk(self, b, lo, hi):
        return self.psb[b][:, lo:hi], [self.psh_t[b]]

    def gload(self, src):
        i = self.gcount % NG_RING
        self.gcount += 1
        buf, t, sl = self.gring[i], self.gring_t[i], self.gring_s[i]
        assert (not t.w) or t.r, "G ring buffer overwritten before it was consumed"
        self.c.emit("pool", lambda e: e.dma_start(out=buf[:].rearrange("p (c n) -> p c n", c=DC), in_=src),
                    writes=[t], slot=sl)
        return buf[:].rearrange("p (c n) -> p c n", c=DC), t

    def dload(self, src):
        i = self.dcount % 2
        self.dcount += 1
        buf, t, sl = self.dring[i], self.dring_t[i], self.dring_s[i]
        assert (not t.w) or t.r, "D ring buffer overwritten before it was consumed"
        self.c.emit("pool", lambda e: e.dma_start(out=buf[:].rearrange("p (c n) -> p c n", c=FC), in_=src),
                    writes=[t], slot=sl)
        return buf[:].rearrange("p (c n) -> p c n", c=FC), t

    def lazy(self, srcs, ahead):
        items = [None] * len(srcs)
        state = {"next": 0}

        def get(i):
            while state["next"] <= min(i + ahead, len(srcs) - 1):
                j = state["next"]
                items[j] = self.gload(srcs[j])
                state["next"] += 1
            return items[i]
        return get

    def build(self):
        nc, c = self.nc, self.c
        T = self.T
        st = self.stages
        self.x = self.dram_in("x", [T, D])
        self.y = nc.dram_tensor("y", [T, D], F32, kind="ExternalOutput").ap()
        self.colvec_d = self.dram_in("colvec", [128, NCOL])
        self.cmat_d = self.dram_in("cmat", [128, NCMAT])
        need_ffn = any(s.startswith("ffn") for s in st)
        if need_ffn:
            self.wg = self.dram_in("wg", [4, D, DFF])
            self.wu = self.dram_in("wu", [4, D, DFF])
            self.wd = self.dram_in("wd", [4, DFF, D])
        self.alloc_common()
        self.setup_consts()
        if "ab" in st:
            self.setup_ab()
        if "rwkv" in st:
            self.setup_rwkv()
        for ti in range(self.NT):
            self.load_x(ti)
            for s in st:
                self.run_stage(ti, s)
            self.store_x(ti)
        c.finalize()
        return nc

    def alloc_common(self):
        c = self.c
        self.colvec = c.sb("colvec", [128, NCOL], F32)
        self.colvec_t = Tk("colvec")
        self.cmat = c.sb("cmat", [128, NCMAT], F32)
        self.cmat_t = Tk("cmat")
        self.cbf = c.sb("cbf", [128, 512], BF16)
        self.cbf_t = Tk("cbf")
        self.xT = c.sb("xT", [128, DC, TT], F32)
        self.xT_t = [Tk(f"xT{i}") for i in range(DC)]
        self.xin_slot = c.slot("xin")
        self.xout_slot = c.slot("xout")
        c.out_slots.append(self.xout_slot)
        self.h = c.sb("h", [128, DC, TT + 1], BF16)
        self.h_t = [Tk(f"h{i}") for i in range(DC)]
        self.arena = c.sb("arena", [128, ARENA_KB * 256], F32)
        self.ar_t = [Tk(f"ar{i}") for i in range(ARENA_KB * 4)]
        self.ar_ptr = 0
        self.psb = [c.ps(f"psb{i}", [128, 512], F32) for i in range(8)]
        self.psh_t = [Tk(f"psb{i}", excl=True) for i in range(8)]
        self.const_slot = c.slot("const", group=True)
        self.gring = [c.sb(f"gring{i}", [128, 2048], BF16) for i in range(NG_RING)]
        self.gring_t = [Tk(f"gring{i}") for i in range(NG_RING)]
        self.gring_s = [c.slot(f"gring{i}") for i in range(NG_RING)]
        self.dring = [c.sb(f"dring{i}", [128, FC * 128], BF16) for i in range(2)]
        self.dring_t = [Tk(f"dring{i}") for i in range(2)]
        self.dring_s = [c.slot(f"dring{i}") for i in range(2)]
        self.gcount = 0
        self.dcount = 0

    def setup_consts(self):
        c = self.c
        cv, cm = self.colvec, self.cmat
        c.emit("sp", lambda e: e.dma_start(out=cv[:], in_=self.colvec_d[:, :]),
               writes=[self.colvec_t], slot=self.const_slot)
        c.emit("sp", lambda e: e.dma_start(out=cm[:], in_=self.cmat_d[:, :]),
               writes=[self.cmat_t], slot=self.const_slot)
        cb = self.cbf
        self.E("dve", lambda e: e.tensor_copy(cb[:, 0:128], cm[:, CM_ONES:CM_ONES + 128]),
               [self.cmat_t], [self.cbf_t])
        self.E("dve", lambda e: e.tensor_copy(cb[:, 128:256], cm[:, CM_IDENT:CM_IDENT + 128]),
               [self.cmat_t, self.cbf_t], [self.cbf_t])
        self.E("dve", lambda e: e.tensor_copy(cb[:, 256:384], cm[:, CM_BD:CM_BD + 128]),
               [self.cmat_t, self.cbf_t], [self.cbf_t])
        self.E("dve", lambda e: e.tensor_copy(cb[:, 384:512], cm[:, CM_UI:CM_UI + 128]),
               [self.cmat_t, self.cbf_t], [self.cbf_t])
        self.E("dve", lambda e: e.memset(self.h[:], 0.0), [], self.h_t)

    def load_x(self, ti):
        c = self.c
        self.ar_reset()
        xin = self.ar_alloc([NB, D], F32)
        src = self.x[ti * TT:(ti + 1) * TT, :].rearrange("(b p) d -> p b d", p=128)
        c.emit("sp", lambda e: e.dma_start(out=xin.ap, in_=src), writes=xin.tks, slot=self.xin_slot)
        ident = self.cmat[:, CM_IDENT:CM_IDENT + 128]
        for dc in range(DC):
            ps, pst = self.bank(dc % 2)
            for b in range(NB):
                self.E("pe", lambda e, b=b, dc=dc, ps=ps: e.transpose(
                    ps[:, b * 128:(b + 1) * 128], xin.ap[:, b, dc * 128:(dc + 1) * 128], ident),
                    xin.tks + [self.cmat_t], pst)
            if dc % 2 == 0:
                self.E("act", lambda e, dc=dc, ps=ps: e.copy(self.xT[:, dc, :], ps[:]), pst, [self.xT_t[dc]])
            else:
                self.E("dve", lambda e, dc=dc, ps=ps: e.tensor_copy(self.xT[:, dc, :], ps[:]), pst, [self.xT_t[dc]])

    def store_x(self, ti):
        c = self.c
        self.ar_reset()
        xo = self.ar_alloc([NB, D], F32)
        ident = self.cmat[:, CM_IDENT:CM_IDENT + 128]
        k = 0
        for b in range(NB):
            for half in range(2):
                ps, pst = self.bank(k % 2)
                k += 1
                for q in range(4):
                    dc = half * 4 + q
                    self.E("pe", lambda e, b=b, dc=dc, q=q, ps=ps: e.transpose(
                        ps[:, q * 128:(q + 1) * 128], self.xT[:, dc, b * 128:(b + 1) * 128], ident),
                        [self.xT_t[dc], self.cmat_t], pst)
                if k % 2 == 0:
                    self.E("act", lambda e, b=b, half=half, ps=ps: e.copy(xo.ap[:, b, half * 512:(half + 1) * 512], ps[:]),
                           pst, xo.tks)
                else:
                    self.E("dve", lambda e, b=b, half=half, ps=ps: e.tensor_copy(xo.ap[:, b, half * 512:(half + 1) * 512], ps[:]),
                           pst, xo.tks)
        dst = self.y[ti * TT:(ti + 1) * TT, :].rearrange("(b p) d -> p b d", p=128)
        c.emit("sp", lambda e: e.dma_start(out=dst, in_=xo.ap), reads=xo.tks, slot=self.xout_slot)

    def rmsnorm(self, gcol):
        ones = self.cbf[:, 0:128]
        sq = [self.ar_alloc([TT], BF16) for _ in range(2)]
        rstd = self.ar_alloc([TT], F32)
        ps, pst = self.bank(2)
        for dc in range(DC):
            s_ = sq[dc % 2]
            self.E("act", lambda e, dc=dc, s_=s_: e.activation(s_.ap, self.xT[:, dc, :], AF.Square),
                   [self.xT_t[dc]], s_.tks)
            self.E("pe", lambda e, dc=dc, s_=s_: e.matmul(ps[:], ones, s_.ap, start=(dc == 0), stop=(dc == DC - 1)),
                   s_.tks + [self.cbf_t], pst)
        self.E("act", lambda e: e.activation(rstd.ap, ps[:], AF.Sqrt, bias=self.colvec[:, CV_EPS:CV_EPS + 1], scale=1.0 / D),
               pst + [self.colvec_t], rstd.tks)
        self.E("dve", lambda e: e.reciprocal(rstd.ap, rstd.ap), rstd.tks, rstd.tks)
        for dc in range(DC):
            self.E("dve", lambda e, dc=dc: e.scalar_tensor_tensor(
                self.h[:, dc, 1:TT + 1], self.xT[:, dc, :], self.colvec[:, gcol + dc:gcol + dc + 1], rstd.ap,
                ALU.mult, ALU.mult),
                [self.xT_t[dc], self.colvec_t] + rstd.tks, [self.h_t[dc]])

    def hv(self, dc, lo=0, hi=TT):
        return self.h[:, dc, 1 + lo:1 + hi]

    def ffn(self, li):
        self.ar_reset()
        wgv = self.wg[li].rearrange("(c p) f -> p c f", p=128)
        wuv = self.wu[li].rearrange("(c p) f -> p c f", p=128)
        wdv = self.wd[li].rearrange("(c p) n -> p c n", p=128)
        NG = FC // 2
        q = []

        def issue(g):
            q.append((self.gload(wgv[:, :, g * 256:(g + 1) * 256]), self.gload(wuv[:, :, g * 256:(g + 1) * 256])))
        issue(0)
        issue(1)
        dq = [self.dload(wdv[:, :, 0:128])]
        self.rmsnorm(CV_FFN + li * 8)
        act = [self.ar_alloc([TT], BF16) for _ in range(FC)]
        silu = [self.ar_alloc([TT], F32) for _ in range(2)]
        for g in range(NG):
            if g + 2 < NG:
                issue(g + 2)
            (gb, gt), (ub, ut) = q.pop(0)
            for fc in range(2):
                f = g * 2 + fc
                psg, psgt = self.bank(3 + (f % 2))
                psu, psut = self.bank(5 + (f % 2))
                for dc in range(DC):
                    self.E("pe", lambda e, dc=dc, fc=fc, gb=gb, psg=psg: e.matmul(
                        psg[:], gb[:, dc, fc * 128:(fc + 1) * 128], self.hv(dc), start=(dc == 0), stop=(dc == DC - 1)),
                        [gt, self.h_t[dc]], psgt)
                for dc in range(DC):
                    self.E("pe", lambda e, dc=dc, fc=fc, ub=ub, psu=psu: e.matmul(
                        psu[:], ub[:, dc, fc * 128:(fc + 1) * 128], self.hv(dc), start=(dc == 0), stop=(dc == DC - 1)),
                        [ut, self.h_t[dc]], psut)
                st_ = silu[f % 2]
                self.E("act", lambda e, st_=st_, psg=psg: e.activation(st_.ap, psg[:], AF.Silu), psgt, st_.tks)
                self.E("dve", lambda e, st_=st_, psu=psu, f=f: e.tensor_tensor(act[f].ap, st_.ap, psu[:], ALU.mult),
                       st_.tks + psut, act[f].tks)
        for dc in range(DC):
            if dc + 1 < DC:
                dq.append(self.dload(wdv[:, :, (dc + 1) * 128:(dc + 2) * 128]))
            db, dt_ = dq.pop(0)
            psy, psyt = self.bank(7 if dc % 2 == 0 else 2)
            for f in range(FC):
                self.E("pe", lambda e, f=f, db=db, psy=psy: e.matmul(
                    psy[:], db[:, f, :], act[f].ap, start=(f == 0), stop=(f == FC - 1)),
                    [dt_] + act[f].tks, psyt)
            self.E("dve", lambda e, dc=dc, psy=psy: e.scalar_tensor_tensor(
                self.xT[:, dc, :], psy[:], 0.5, self.xT[:, dc, :], ALU.mult, ALU.add),
                psyt + [self.xT_t[dc]], [self.xT_t[dc]])

    def setup_ab(self):
        c = self.c
        self.w_in_d = self.dram_in("w_in", [D, 1280])
        self.w_out_d = self.dram_in("w_out", [D, D])
        self.pool_w_d = self.dram_in("pool_w", [4, 128, 128])
        self.pw = c.sb("pw", [128, 4, 128], BF16)
        self.pw_t = Tk("pw")
        self.abw_slot = c.slot("abw", group=True)
        self.kcar = c.sb("kcar", [128, 128], BF16)
        self.kcar_t = Tk("kcar")
        self.vcar = c.sb("vcar", [128, 128], BF16)
        self.vcar_t = Tk("vcar")
        self.pcar = c.sb("pcar", [128, 4, 16], F32)
        self.pcar_t = Tk("pcar")
        self.esk = c.sb("esink", [128, 512], F32)
        self.esb = c.sb("esb", [128, 8], F32)
        self.esk_t = Tk("esink")
        src4 = self.pool_w_d.rearrange("g c d -> c g d")
        c.emit("pool", lambda e: e.dma_start(out=self.pw[:], in_=src4), writes=[self.pw_t], slot=self.abw_slot)
        self.E("act", lambda e: e.activation(self.esb[:], self.colvec[:, CV_SINK:CV_SINK + 8], AF.Exp),
               [self.colvec_t], [self.esk_t])
        for hk in range(2):
            for g in range(4):
                hq = hk * 4 + g
                self.E("dve", lambda e, hk=hk, g=g, hq=hq: e.tensor_copy(
                    self.esk[hk * 64:(hk + 1) * 64, g * 128:(g + 1) * 128],
                    self.esb[hk * 64:(hk + 1) * 64, hq:hq + 1].to_broadcast([64, 128])),
                    [self.esk_t], [self.esk_t])
        self.E("dve", lambda e: e.memset(self.pcar[:], 0.0), [], [self.pcar_t])
        self.E("dve", lambda e: e.memset(self.kcar[:], 0.0), [], [self.kcar_t])
        self.E("dve", lambda e: e.memset(self.vcar[:], 0.0), [], [self.vcar_t])

    def ab(self, ti):
        self.ar_reset()
        bd = self.cbf[:, 256:384]
        ones = self.cbf[:, 0:128]
        winv = self.w_in_d.rearrange("(c p) n -> p c n", p=128)
        woutv = self.w_out_d.rearrange("(c p) n -> p c n", p=128)
        wq = self.lazy([winv[:, :, i * 256:(i + 1) * 256] for i in range(5)], 1)
        wq(0)
        self.rmsnorm(CV_AB)
        qn = [self.ar_alloc([TT], BF16) for _ in range(4)]
        kn = self.ar_alloc([128 + TT], BF16)
        vtm = self.ar_alloc([NB + 1, 128], BF16)
        qraw = [self.ar_alloc([TT], F32) for _ in range(2)]
        sqb = [self.ar_alloc([TT], BF16) for _ in range(2)]
        rstd2 = self.ar_alloc([TT], F32)
        W = 16 + TT
        pbuf = [self.ar_alloc([W], F32) for _ in range(4)]
        ptmp = [self.ar_alloc([W], F32) for _ in range(2)]
        pdiff = [self.ar_alloc([TT], BF16) for _ in range(4)]
        opool = [self.ar_alloc([TT], BF16) for _ in range(4)]
        oT = self.ar_alloc([4, TT], BF16)
        ebuf = [self.ar_alloc([512], F32) for _ in range(2)]
        pt = [[self.ar_alloc([512], BF16) for _ in range(2)] for _ in range(2)]
        rden = [self.ar_alloc([512], F32) for _ in range(2)]
        self.E("dve", lambda e: e.tensor_copy(kn.ap[:, 0:128], self.kcar[:]), [self.kcar_t], kn.tks)
        self.E("dve", lambda e: e.tensor_copy(vtm.ap[:, 0, :], self.vcar[:]), [self.vcar_t], vtm.tks)
        for gi in range(4):
            self.E("dve", lambda e, gi=gi: e.tensor_copy(pbuf[gi].ap[:, 0:16], self.pcar[:, gi, :]), [self.pcar_t], pbuf[gi].tks)
        for ci in range(5):
            wb, wt = wq(ci // 2)
            n0 = (ci % 2) * 128
            ps, pst = self.bank(ci % 2)
            for dc in range(DC):
                self.E("pe", lambda e, dc=dc, n0=n0, ps=ps, wb=wb: e.matmul(
                    ps[:], wb[:, dc, n0:n0 + 128], self.hv(dc), start=(dc == 0), stop=(dc == DC - 1)),
                    [wt, self.h_t[dc]], pst)
            qr, s_ = qraw[ci % 2], sqb[ci % 2]
            ps2, ps2t = self.bank(2)
            self.E("act", lambda e, qr=qr, ps=ps: e.copy(qr.ap, ps[:]), pst, qr.tks)
            self.E("act", lambda e, s_=s_, ps=ps: e.activation(s_.ap, ps[:], AF.Square), pst, s_.tks)
            self.E("pe", lambda e, s_=s_, ps2=ps2: e.matmul(ps2[:], bd, s_.ap, start=True, stop=True),
                   s_.tks + [self.cbf_t], ps2t)
            self.E("act", lambda e, ps2=ps2: e.activation(rstd2.ap, ps2[:], AF.Sqrt,
                                                          bias=self.colvec[:, CV_EPS:CV_EPS + 1], scale=1.0 / 64),
                   ps2t + [self.colvec_t], rstd2.tks)
            self.E("dve", lambda e: e.reciprocal(rstd2.ap, rstd2.ap), rstd2.tks, rstd2.tks)
            if ci < 4:
                self.E("dve", lambda e, ci=ci, qr=qr: e.scalar_tensor_tensor(
                    qn[ci].ap, qr.ap, self.colvec[:, CV_QN:CV_QN + 1], rstd2.ap, ALU.mult, ALU.mult),
                    qr.tks + rstd2.tks + [self.colvec_t], qn[ci].tks)
            else:
                self.E("dve", lambda e, qr=qr: e.scalar_tensor_tensor(
                    kn.ap[:, 128:128 + TT], qr.ap, self.colvec[:, CV_KN:CV_KN + 1], rstd2.ap, ALU.mult, ALU.mult),
                    qr.tks + rstd2.tks + [self.colvec_t], kn.tks)
        wb, wt = wq(2)
        ps, pst = self.bank(1)
        for b in range(NB):
            for dc in range(DC):
                self.E("pe", lambda e, dc=dc, b=b, ps=ps, wb=wb: e.matmul(
                    ps[:, b * 128:(b + 1) * 128], self.hv(dc, b * 128, (b + 1) * 128), wb[:, dc, 128:256],
                    start=(dc == 0), stop=(dc == DC - 1)),
                    [wt, self.h_t[dc]], pst)
        self.E("act", lambda e, ps=ps: e.copy(vtm.ap[:, 1:NB + 1, :], ps[:].rearrange("p (b n) -> p b n", b=NB)),
               pst, vtm.tks)
        for gi in range(4):
            wb, wt = wq(3 + gi // 2)
            n0 = (gi % 2) * 128
            ps, pst = self.bank(gi % 2)
            for dc in range(DC):
                self.E("pe", lambda e, dc=dc, n0=n0, ps=ps, wb=wb: e.matmul(
                    ps[:], wb[:, dc, n0:n0 + 128], self.hv(dc), start=(dc == 0), stop=(dc == DC - 1)),
                    [wt, self.h_t[dc]], pst)
            self.E("act", lambda e, gi=gi, ps=ps: e.copy(pbuf[gi].ap[:, 16:16 + TT], ps[:]), pst, pbuf[gi].tks)
        woq = self.lazy([woutv[:, :, i * 256:(i + 1) * 256] for i in range(4)], 1)
        woq(0)
        for gi, w in enumerate((2, 4, 8, 16)):
            cur = pbuf[gi]
            lo = 0
            k = 0
            step = 1
            while step < w:
                nxt = ptmp[k % 2]
                lo += step
                self.E("dve", lambda e, cur=cur, nxt=nxt, lo=lo, step=step: e.tensor_tensor(
                    nxt.ap[:, lo:W], cur.ap[:, lo:W], cur.ap[:, lo - step:W - step], ALU.add),
                    cur.tks, nxt.tks)
                cur = nxt
                step *= 2
                k += 1
            self.E("dve", lambda e, cur=cur, gi=gi, w=w: e.scalar_tensor_tensor(
                pdiff[gi].ap, cur.ap[:, 16:W], 1.0 / w, pbuf[gi].ap[:, 16:W], ALU.mult, ALU.subtract),
                cur.tks + pbuf[gi].tks, pdiff[gi].tks)
            if ti == 0:
                tmp = qraw[0]
                self.E("dve", lambda e, cur=cur, gi=gi, tmp=tmp: e.tensor_tensor(
                    tmp.ap[:, 0:16], cur.ap[:, 16:32], self.cmat[:, CM_INVC + gi * 16:CM_INVC + gi * 16 + 16], ALU.mult),
                    cur.tks + [self.cmat_t], tmp.tks)
                self.E("dve", lambda e, gi=gi, tmp=tmp: e.tensor_tensor(
                    pdiff[gi].ap[:, 0:16], tmp.ap[:, 0:16], pbuf[gi].ap[:, 16:32], ALU.subtract),
                    tmp.tks + pbuf[gi].tks, pdiff[gi].tks)
            self.E("dve", lambda e, gi=gi: e.tensor_copy(self.pcar[:, gi, :], pbuf[gi].ap[:, TT:TT + 16]),
                   pbuf[gi].tks, [self.pcar_t])
            ps, pst = self.bank(gi % 2)
            self.E("pe", lambda e, gi=gi, ps=ps: e.matmul(ps[:], self.pw[:, gi, :], pdiff[gi].ap, start=True, stop=True),
                   [self.pw_t] + pdiff[gi].tks, pst)
            self.E("act", lambda e, gi=gi, ps=ps: e.activation(opool[gi].ap, ps[:], AF.Copy,
                                                               scale=self.colvec[:, CV_PS + gi:CV_PS + gi + 1]),
                   pst + [self.colvec_t], opool[gi].tks)
        for b in range(NB):
            gblk = ti * NB + b
            for hk in range(2):
                p0 = hk * 64
                kbs = [0, 1] if gblk > 0 else [1]
                ptb = pt[hk]
                for kb in kbs:
                    ps, pst = self.bank(3 + 2 * hk + kb)
                    for g in range(4):
                        self.E("pe", lambda e, ps=ps, p0=p0, b=b, kb=kb, g=g: e.matmul(
                            ps[:, g * 128:(g + 1) * 128],
                            kn.ap[p0:p0 + 64, (b + kb) * 128:(b + kb + 1) * 128],
                            qn[g].ap[p0:p0 + 64, b * 128:(b + 1) * 128], start=True, stop=True),
                            kn.tks + qn[g].tks, pst)
                    eb = ebuf[kb]
                    self.E("act", lambda e, eb=eb, ps=ps: e.activation(eb.ap, ps[:], AF.Exp, scale=0.125), pst, eb.tks)
                    m0 = CM_AM + (hk * 2 + kb) * 512
                    self.E("dve", lambda e, eb=eb, ptb=ptb, kb=kb, m0=m0: e.tensor_tensor(
                        ptb[kb].ap, eb.ap, self.cmat[:, m0:m0 + 512], ALU.mult),
                        eb.tks + [self.cmat_t], ptb[kb].tks)
                pv, pvt = self.bank(7 if hk == 0 else 0)
                den, dent = self.bank(2 if hk == 0 else 1)
                for i, kb in enumerate(kbs):
                    self.E("pe", lambda e, i=i, kb=kb, b=b, ptb=ptb, pv=pv, n=len(kbs): e.matmul(
                        pv[:, :], vtm.ap[:, b + kb, :], ptb[kb].ap, start=(i == 0), stop=(i == n - 1)),
                        vtm.tks + ptb[kb].tks, pvt)
                for i, kb in enumerate(kbs):
                    self.E("pe", lambda e, i=i, kb=kb, ptb=ptb, den=den, n=len(kbs): e.matmul(
                        den[:, :], ones, ptb[kb].ap, start=(i == 0), stop=(i == n - 1)),
                        [self.cbf_t] + ptb[kb].tks, dent)
                rd = rden[hk]
                self.E("dve", lambda e, rd=rd, den=den, p0=p0: e.tensor_tensor(
                    rd.ap[p0:p0 + 64, :], den[p0:p0 + 64, :], self.esk[p0:p0 + 64, :], ALU.add),
                    dent + [self.esk_t], rd.tks)
                self.E("dve", lambda e, rd=rd, p0=p0: e.reciprocal(rd.ap[p0:p0 + 64, :], rd.ap[p0:p0 + 64, :]), rd.tks, rd.tks)
                self.E("dve", lambda e, rd=rd, pv=pv, p0=p0, b=b: e.tensor_tensor(
                    oT.ap[p0:p0 + 64, :, b * 128:(b + 1) * 128],
                    pv[p0:p0 + 64, :].rearrange("p (g q) -> p g q", g=4),
                    rd.ap[p0:p0 + 64, :].rearrange("p (g q) -> p g q", g=4), ALU.mult),
                    pvt + rd.tks, oT.tks)
        self.E("dve", lambda e: e.tensor_copy(self.kcar[:], kn.ap[:, TT:TT + 128]), kn.tks, [self.kcar_t])
        self.E("dve", lambda e: e.tensor_copy(self.vcar[:], vtm.ap[:, NB, :]), vtm.tks, [self.vcar_t])
        for dc in range(DC):
            wb, wt = woq(dc // 2)
            n0 = (dc % 2) * 128
            ps, pst = self.bank(3 + dc % 2)
            for ch in range(4):
                self.E("pe", lambda e, ch=ch, ps=ps, wb=wb, n0=n0: e.matmul(
                    ps[:], wb[:, ch, n0:n0 + 128], oT.ap[:, ch, :], start=(ch == 0), stop=False),
                    [wt] + oT.tks, pst)
            for gi in range(4):
                self.E("pe", lambda e, gi=gi, ps=ps, wb=wb, n0=n0: e.matmul(
                    ps[:], wb[:, 4 + gi, n0:n0 + 128], opool[gi].ap, start=False, stop=(gi == 3)),
                    [wt] + opool[gi].tks, pst)
            self.E("dve", lambda e, dc=dc, ps=ps: e.tensor_tensor(self.xT[:, dc, :], ps[:], self.xT[:, dc, :], ALU.add),
                   pst + [self.xT_t[dc]], [self.xT_t[dc]])

    def setup_rwkv(self):
        c = self.c
        self.cw = {nm: self.dram_in(nm, [D, D]) for nm in ("c_w_r", "c_w_k", "c_w_v", "c_w_o")}
        w1_d = self.dram_in("c_w1", [D, 64])
        a1_d = self.dram_in("c_a1", [D, 64])
        g1_d = self.dram_in("c_g1", [D, 128])
        w2_d = self.dram_in("c_w2e", [65, D])
        a2_d = self.dram_in("c_a2e", [65, D])
        g2_d = self.dram_in("c_g2", [128, D])
        rows_d = self.dram_in("c_rows", [3, D])
        self.w1b = c.sb("w1b", [128, DC, 64], BF16)
        self.a1b = c.sb("a1b", [128, DC, 64], BF16)
        self.g1b = c.sb("g1b", [128, DC, 128], BF16)
        self.w2b = c.sb("w2b", [65, D], BF16)
        self.a2b = c.sb("a2b", [65, D], BF16)
        self.g2b = c.sb("g2b", [128, D], BF16)
        self.rowb = c.sb("rowb", [128, 3, D], F32)
        self.cw_t = Tk("cw")
        self.cw_slot = c.slot("cw", group=True)
        self.spad = c.sb("spad", [128, 16, 64], F32)
        self.sbf = c.sb("sbf", [128, 16, 64], BF16)
        self.spad_t = [Tk(f"spad{i}") for i in range(16)]
        self.sbf_t = [Tk(f"sbf{i}") for i in range(16)]
        sl = self.cw_slot
        lds = [(self.w1b[:], w1_d.rearrange("(c p) n -> p c n", p=128)),
               (self.a1b[:], a1_d.rearrange("(c p) n -> p c n", p=128)),
               (self.g1b[:], g1_d.rearrange("(c p) n -> p c n", p=128)),
               (self.w2b[:], w2_d[:, :]), (self.a2b[:], a2_d[:, :]), (self.g2b[:], g2_d[:, :])]
        for n_, (dst, src) in enumerate(lds):
            c.emit("pool", lambda e, dst=dst, src=src: e.dma_start(out=dst, in_=src),
                   writes=[self.cw_t if n_ == len(lds) - 1 else Tk("x")], slot=sl)
        self.rowb_t = Tk("rowb")
        sl2 = c.slot("cwrow", group=True)
        for i in range(3):
            c.emit("sp", lambda e, i=i: e.dma_start(out=self.rowb[:, i:i + 1, :], in_=rows_d[i:i + 1, :].partition_broadcast(128)),
                   writes=[self.rowb_t if i == 2 else Tk("x")], slot=sl2)
        self.E("dve", lambda e: e.memset(self.spad[:], 0.0), [], self.spad_t)
        self.E("dve", lambda e: e.memset(self.sbf[:], 0.0), [], self.sbf_t)

    def rwkv(self, ti):
        self.ar_reset()
        cv = self.colvec
        identb = self.cbf[:, 128:256]
        cm = self.cmat
        cwt = [self.cw_t]

        def wview(nm, q):
            return self.cw[nm].rearrange("(c p) n -> p c n", p=128)[:, :, q * 256:(q + 1) * 256]
        self.rmsnorm(CV_C)
        rb = [self.ar_alloc([D], BF16) for _ in range(NB)]
        kb_ = [self.ar_alloc([D], BF16) for _ in range(NB)]
        vb = [self.ar_alloc([D], BF16) for _ in range(NB)]
        t1e = self.ar_alloc([TT], BF16)
        a1e = self.ar_alloc([TT], BF16)
        g1T = self.ar_alloc([TT], BF16)
        yT = [self.ar_alloc([TT], BF16) for _ in range(DC)]
        mark = self.ar_ptr
        xx = [self.ar_alloc([TT], BF16) for _ in range(DC)]
        xm = [self.ar_alloc([TT], BF16) for _ in range(DC)]
        for dc in range(DC):
            self.E("dve", lambda e, dc=dc: e.tensor_tensor(xx[dc].ap, self.h[:, dc, 0:TT], self.h[:, dc, 1:TT + 1], ALU.subtract),
                   [self.h_t[dc]], xx[dc].tks)
        self.E("dve", lambda e: e.tensor_copy(self.h[:, :, 0:1], self.h[:, :, TT:TT + 1]), self.h_t, self.h_t)

        def mix(i):
            for dc in range(DC):
                self.E("dve", lambda e, dc=dc, i=i: e.scalar_tensor_tensor(
                    xm[dc].ap, xx[dc].ap, cv[:, CV_MU + i * 8 + dc:CV_MU + i * 8 + dc + 1], self.hv(dc), ALU.mult, ALU.add),
                    xx[dc].tks + [self.h_t[dc], self.colvec_t], xm[dc].tks)
        mix(1)
        ps, pst = self.bank(0)
        for dc in range(DC):
            self.E("pe", lambda e, dc=dc, ps=ps: e.matmul(ps[0:64, :], self.w1b[:, dc, :], xm[dc].ap, start=(dc == 0), stop=(dc == DC - 1)),
                   cwt + xm[dc].tks, pst)
        self.E("act", lambda e, ps=ps: e.activation(t1e.ap[0:64, :], ps[0:64, :], AF.Tanh), pst, t1e.tks)
        self.E("dve", lambda e: e.memset(t1e.ap[64:65, :], 1.0), [], t1e.tks)
        mix(4)
        ps, pst = self.bank(1)
        for dc in range(DC):
            self.E("pe", lambda e, dc=dc, ps=ps: e.matmul(ps[0:64, :], self.a1b[:, dc, :], xm[dc].ap, start=(dc == 0), stop=(dc == DC - 1)),
                   cwt + xm[dc].tks, pst)
        self.E("act", lambda e, ps=ps: e.copy(a1e.ap[0:64, :], ps[0:64, :]), pst, a1e.tks)
        self.E("dve", lambda e: e.memset(a1e.ap[64:65, :], 1.0), [], a1e.tks)
        mix(5)
        ps, pst = self.bank(0)
        for dc in range(DC):
            self.E("pe", lambda e, dc=dc, ps=ps: e.matmul(ps[:, :], self.g1b[:, dc, :], xm[dc].ap, start=(dc == 0), stop=(dc == DC - 1)),
                   cwt + xm[dc].tks, pst)
        self.E("act", lambda e, ps=ps: e.activation(g1T.ap, ps[:, :], AF.Sigmoid), pst, g1T.tks)
        hcnt = 0
        for (i, nm, dst) in ((0, "c_w_r", rb), (2, "c_w_k", kb_), (3, "c_w_v", vb)):
            mix(i)
            wl = self.lazy([wview(nm, q_) for q_ in range(4)], 1)
            for q in range(4):
                wb, wt = wl(q)
                for cb in range(NB):
                    ph, pht = self.bk(hcnt % 8, 0, 256)
                    hcnt += 1
                    for dc in range(DC):
                        self.E("pe", lambda e, dc=dc, cb=cb, ph=ph, wb=wb: e.matmul(
                            ph, xm[dc].ap[:, cb * 128:(cb + 1) * 128], wb[:, dc, :], start=(dc == 0), stop=(dc == DC - 1)),
                            [wt] + xm[dc].tks, pht)
                    if hcnt % 2 == 0:
                        self.E("act", lambda e, cb=cb, q=q, ph=ph, dst=dst: e.copy(dst[cb].ap[:, q * 256:(q + 1) * 256], ph), pht, dst[cb].tks)
                    else:
                        self.E("dve", lambda e, cb=cb, q=q, ph=ph, dst=dst: e.tensor_copy(dst[cb].ap[:, q * 256:(q + 1) * 256], ph), pht, dst[cb].tks)
        if RWDBG <= 1:
            return
        for cb in range(NB):
            self.ar_ptr = mark
            self.rwkv_chunk(ti, cb, rb[cb], kb_[cb], vb[cb], t1e, a1e, g1T, yT)
        wl = self.lazy([wview("c_w_o", q_) for q_ in range(4)], 1)
        for q in range(4):
            wb, wt = wl(q)
            for nn in range(2):
                dco = q * 2 + nn
                ps, pst = self.bank(dco % 2)
                for blk in range(DC):
                    self.E("pe", lambda e, blk=blk, nn=nn, ps=ps, wb=wb: e.matmul(
                        ps[:], wb[:, blk, nn * 128:(nn + 1) * 128], yT[blk].ap, start=(blk == 0), stop=(blk == DC - 1)),
                        [wt] + yT[blk].tks, pst)
                self.E("dve", lambda e, dco=dco, ps=ps: e.tensor_tensor(self.xT[:, dco, :], ps[:], self.xT[:, dco, :], ALU.add),
                       pst + [self.xT_t[dco]], [self.xT_t[dco]])

    def rwkv_chunk(self, ti, cb, rb, kb_, vb, t1e, a1e, g1T, yT):
        cv, cm = self.colvec, self.cmat
        identb = self.cbf[:, 128:256]
        cwt = [self.cw_t]
        cs = slice(cb * 128, (cb + 1) * 128)
        A = self.ar_alloc
        sg = A([D], F32)
        bb = A([D], F32)
        a_ = A([D], BF16)
        at, rt, kt, bt, kh, bh, bv, zb = [A([D], BF16) for _ in range(8)]
        sm = A([8, 16], F32)
        sm2 = A([1, 16], F32)
        fmT = A([DC, 4, 128], BF16)
        tmpf = [A([128], F32) for _ in range(2)]
        mark2 = self.ar_ptr
        ex = A([D], F32)
        en = A([D], F32)
        kkb = A([D], F32)
        km = A([D], F32)
        y = sg
        v3 = lambda b_: b_.ap.rearrange("p (h j) -> p h j", h=16)
        bc = lambda i: sm.ap[:, i, :].unsqueeze(2).to_broadcast([128, 16, 64])
        for hf in range(2):
            hs_ = slice(hf * 512, (hf + 1) * 512)
            ps, pst = self.bank(hf)
            self.E("pe", lambda e, ps=ps, hs_=hs_: e.matmul(ps[:], t1e.ap[0:65, cs], self.w2b[0:65, hs_], start=True, stop=True),
                   t1e.tks + cwt, pst)
            self.E("act", lambda e, ps=ps, hs_=hs_: e.activation(sg.ap[:, hs_], ps[:], AF.Sigmoid), pst, sg.tks)
            ps, pst = self.bank(2 + hf)
            self.E("pe", lambda e, ps=ps, hs_=hs_: e.matmul(ps[:], a1e.ap[0:65, cs], self.a2b[0:65, hs_], start=True, stop=True),
                   a1e.tks + cwt, pst)
            self.E("act", lambda e, ps=ps, hs_=hs_: e.activation(a_.ap[:, hs_], ps[:], AF.Sigmoid), pst, a_.tks)
        if RWDBG <= 2:
            return
        if RWDBG <= 3:
            return
        self.E("dve", lambda e: e.tensor_tensor(kkb.ap, kb_.ap, self.rowb[:, 0, :], ALU.mult), kb_.tks + [self.rowb_t], kkb.tks)
        self.E("dve", lambda e: e.tensor_tensor(bb.ap, kkb.ap, kkb.ap, ALU.mult), kkb.tks, bb.tks)
        self.E("dve", lambda e: e.tensor_reduce(sm.ap[:, 0, :], v3(bb), AX.X, ALU.add), bb.tks, sm.tks)
        self.E("dve", lambda e: e.tensor_scalar(sm.ap[:, 0, :], sm.ap[:, 0, :], 1e-24, None, ALU.max), sm.tks, sm.tks)
        self.E("act", lambda e: e.activation(sm.ap[:, 0, :], sm.ap[:, 0, :], AF.Sqrt), sm.tks, sm.tks)
        self.E("dve", lambda e: e.reciprocal(sm.ap[:, 0, :], sm.ap[:, 0, :]), sm.tks, sm.tks)
        self.E("dve", lambda e: e.tensor_tensor(v3(kkb), v3(kkb), bc(0), ALU.mult), kkb.tks + sm.tks, kkb.tks)
        self.E("dve", lambda e: e.scalar_tensor_tensor(km.ap, a_.ap, 1.0, self.rowb[:, 1, :], ALU.subtract, ALU.mult),
               a_.tks + [self.rowb_t], km.tks)
        self.E("dve", lambda e: e.scalar_tensor_tensor(km.ap, km.ap, 1.0, kb_.ap, ALU.add, ALU.mult), km.tks + kb_.tks, km.tks)
        self.E("dve", lambda e: e.tensor_tensor(bb.ap, kkb.ap, a_.ap, ALU.mult), kkb.tks + a_.tks, bb.tks)
        tri = self.cbf[:, 384:512]
        onesb = self.cbf[:, 0:128]
        sgh, sgl = kh, bh
        self.E("act", lambda e: e.copy(sgh.ap, sg.ap), sg.tks, sgh.tks)
        self.E("dve", lambda e: e.tensor_tensor(sgl.ap, sg.ap, sgh.ap, ALU.subtract), sg.tks + sgh.tks, sgl.tks)
        pcs = []
        for hf in range(2):
            hs_ = slice(hf * 512, (hf + 1) * 512)
            pc, pct = self.bank(4 + hf)
            self.E("pe", lambda e, pc=pc, hs_=hs_: e.matmul(pc[:], tri, sgh.ap[:, hs_], start=True, stop=False), sgh.tks + [self.cbf_t], pct)
            self.E("pe", lambda e, pc=pc, hs_=hs_: e.matmul(pc[:], tri, sgl.ap[:, hs_], start=False, stop=True), sgl.tks + [self.cbf_t], pct)
            pC, pCt = self.bank(6 + hf)
            self.E("pe", lambda e, pC=pC, hs_=hs_: e.matmul(pC[:], onesb, sgh.ap[:, hs_], start=True, stop=False), sgh.tks + [self.cbf_t], pCt)
            self.E("pe", lambda e, pC=pC, hs_=hs_: e.matmul(pC[:], onesb, sgl.ap[:, hs_], start=False, stop=True), sgl.tks + [self.cbf_t], pCt)
            pcs.append((pc, pct, pC, pCt))
        pw_, pwt = self.bk(0, 0, 128)
        for blk in range(DC):
            self.E("pe", lambda e, blk=blk: e.matmul(pw_[:, blk * 16:(blk + 1) * 16], sgh.ap[:, blk * 128:(blk + 1) * 128],
                                                     onesb[:, 0:16], start=True, stop=False), sgh.tks + [self.cbf_t], pwt)
            self.E("pe", lambda e, blk=blk: e.matmul(pw_[:, blk * 16:(blk + 1) * 16], sgl.ap[:, blk * 128:(blk + 1) * 128],
                                                     onesb[:, 0:16], start=False, stop=True), sgl.tks + [self.cbf_t], pwt)
        self.E("act", lambda e: e.activation(sm.ap[:, 7, 0:8], pw_.rearrange("p (b n) -> p b n", n=16)[:, :, 0], AF.Exp, scale=NEG_E),
               pwt, sm.tks)
        for hf in range(2):
            hs_ = slice(hf * 512, (hf + 1) * 512)
            pc, pct, pC, pCt = pcs[hf]
            self.E("act", lambda e, pc=pc, hs_=hs_: e.activation(ex.ap[:, hs_], pc[:], AF.Exp, scale=NEG_E), pct, ex.tks)
            self.E("act", lambda e, pc=pc, hs_=hs_: e.activation(en.ap[:, hs_], pc[:], AF.Exp, scale=-NEG_E), pct, en.tks)
            self.E("dve", lambda e, pc=pc, hs_=hs_: e.scalar_tensor_tensor(
                sg.ap[:, hs_], sg.ap[:, hs_], -1.0, pc[:], ALU.mult, ALU.add), sg.tks + pct, sg.tks)
        self.E("act", lambda e: e.activation(sg.ap, sg.ap, AF.Exp, scale=NEG_E), sg.tks, sg.tks)
        self.E("dve", lambda e: e.scalar_tensor_tensor(at.ap, kkb.ap, -1.0, sg.ap, ALU.mult, ALU.mult), kkb.tks + sg.tks, at.tks)
        self.E("dve", lambda e: e.tensor_tensor(rt.ap, rb.ap, ex.ap, ALU.mult), rb.tks + ex.tks, rt.tks)
        self.E("dve", lambda e: e.tensor_tensor(kt.ap, km.ap, en.ap, ALU.mult), km.tks + en.tks, kt.tks)
        self.E("dve", lambda e: e.tensor_tensor(bt.ap, bb.ap, en.ap, ALU.mult), bb.tks + en.tks, bt.tks)
        for hf in range(2):
            hs_ = slice(hf * 512, (hf + 1) * 512)
            pc, pct, pC, pCt = pcs[hf]
            self.E("act", lambda e, pC=pC, hs_=hs_: e.activation(ex.ap[:, hs_], pC[:], AF.Exp, scale=NEG_E), pCt, ex.tks)
        self.E("dve", lambda e: e.tensor_tensor(kh.ap, kt.ap, ex.ap, ALU.mult), kt.tks + ex.tks, kh.tks)
        self.E("dve", lambda e: e.tensor_tensor(bh.ap, bt.ap, ex.ap, ALU.mult), bt.tks + ex.tks, bh.tks)
        self.E("pool", lambda e: e.tensor_tensor(bb.ap, rb.ap, km.ap, ALU.mult), rb.tks + km.tks, bb.tks)
        self.E("pool", lambda e: e.tensor_tensor(bb.ap, bb.ap, self.rowb[:, 2, :], ALU.mult), bb.tks + [self.rowb_t], bb.tks)
        self.E("dve", lambda e: e.tensor_reduce(sm2.ap[:, 0, :], v3(bb), AX.X, ALU.add), bb.tks, sm2.tks)
        self.E("pool", lambda e: e.tensor_tensor(v3(bv), v3(vb), sm2.ap[:, 0, :].unsqueeze(2).to_broadcast([128, 16, 64]), ALU.mult),
               vb.tks + sm2.tks, bv.tks)
        if RWDBG <= 4:
            return
        for j, src in enumerate((at, rt, bt, kt)):
            for hf in range(2):
                ph, pht = self.bank((2 * j + hf) % 8)
                for b4 in range(4):
                    blk = hf * 4 + b4
                    self.E("pe", lambda e, src=src, blk=blk, b4=b4, ph=ph: e.matmul(
                        ph[:, b4 * 128:(b4 + 1) * 128], src.ap[:, blk * 128:(blk + 1) * 128], identb, start=True, stop=True),
                        src.tks + [self.cbf_t], pht)
                if (2 * j + hf) % 2 == 0:
                    self.E("act", lambda e, j=j, hf=hf, ph=ph: e.copy(fmT.ap[:, hf * 4:hf * 4 + 4, j, :], ph[:].rearrange("p (b t) -> p b t", b=4)), pht, fmT.tks)
                else:
                    self.E("dve", lambda e, j=j, hf=hf, ph=ph: e.tensor_copy(fmT.ap[:, hf * 4:hf * 4 + 4, j, :], ph[:].rearrange("p (b t) -> p b t", b=4)), pht, fmT.tks)
        if RWDBG <= 5:
            return
        self.ar_ptr = mark2
        Wb = [A([768], BF16) for _ in range(8)]
        tinv = [A([128], BF16) for _ in range(8)]
        pm = [A([256], BF16) for _ in range(8)]
        ut = [A([64], BF16) for _ in range(8)]

        def evac(hh, out, in_, reads, writes):
            if hh % 2 == 0:
                self.E("act", lambda e: e.copy(out, in_), reads, writes)
            else:
                self.E("dve", lambda e: e.tensor_copy(out, in_), reads, writes)
        for grp in range(2):
            hs = [grp * 8 + hh for hh in range(8)]
            for hh, h in enumerate(hs):
                blk, p0 = h // 2, (h % 2) * 64
                f = lambda j, n=1, blk=blk, p0=p0: fmT.ap[p0:p0 + 64, blk, j:j + n, :]
                W = Wb[hh]
                pb, pbt = self.bank(hh)
                self.E("pe", lambda e, pb=pb, f=f: e.matmul(pb[:, 0:256].rearrange("p (a t) -> p a t", a=2), f(2)[:, 0, :], f(0, 2), start=True, stop=True),
                       fmT.tks, pbt)
                self.E("pe", lambda e, pb=pb, f=f: e.matmul(pb[:, 256:384], f(3)[:, 0, :], f(1)[:, 0, :], start=True, stop=True),
                       fmT.tks, pbt)
                self.E("dve", lambda e, pb=pb, W=W: e.tensor_tensor(W.ap[:, 0:128], pb[:, 0:128], cm[:, CM_SU:CM_SU + 128], ALU.mult),
                       pbt + [self.cmat_t], W.tks)
                self.E("dve", lambda e, pb=pb, W=W: e.tensor_tensor(W.ap[:, 512:768], pb[:, 128:384], cm[:, CM_UI:CM_UI + 256], ALU.mult),
                       pbt + [self.cmat_t], W.tks)
                self.E("act", lambda e, W=W: e.copy(W.ap[:, 128:256], identb), [self.cbf_t], W.tks)
            for hh, h in enumerate(hs):
                blk, p0 = h // 2, (h % 2) * 64
                f = lambda j, n=1, blk=blk, p0=p0: fmT.ap[p0:p0 + 64, blk, j:j + n, :]
                W = Wb[hh]
                pb, pbt = self.bank(hh)
                self.E("pe", lambda e, pb=pb, f=f: e.matmul(pb[:, 0:256].rearrange("p (a t) -> p a t", a=2), f(0)[:, 0, :], f(2, 2), start=True, stop=True),
                       fmT.tks, pbt)
                self.E("dve", lambda e, pb=pb, W=W: e.tensor_tensor(W.ap[:, 256:512], pb[:, 0:256], cm[:, CM_SL:CM_SL + 256], ALU.mult),
                       pbt + [self.cmat_t], W.tks)
            for k in range(6):
                for hh in range(8):
                    W = Wb[hh]
                    X, P, XT = W.ap[:, 0:128], W.ap[:, 128:256], W.ap[:, 256:384]
                    pb, pbt = self.bank(hh)
                    self.E("pe", lambda e, pb=pb, X=X, XT=XT: e.matmul(pb[:, 0:128], XT, X, start=True, stop=True), W.tks, pbt)
                    self.E("pe", lambda e, pb=pb, P=P, XT=XT: e.matmul(pb[:, 128:256], XT, P, start=True, stop=False), W.tks, pbt)
                    self.E("pe", lambda e, pb=pb, P=P: e.matmul(pb[:, 128:256], identb, P, start=False, stop=True), W.tks + [self.cbf_t], pbt)
                    self.E("pe", lambda e, pb=pb, X=X, XT=XT: e.matmul(pb[:, 256:384], X, XT, start=True, stop=True), W.tks, pbt)
                    evac(hh, W.ap[:, 0:384], pb[:, 0:384], pbt, W.tks)
            for hh in range(8):
                W = Wb[hh]
                P, XT = W.ap[:, 128:256], W.ap[:, 256:384]
                pb, pbt = self.bank(hh)
                self.E("pe", lambda e, pb=pb, P=P, XT=XT: e.matmul(pb[:, 0:128], XT, P, start=True, stop=False), W.tks, pbt)
                self.E("pe", lambda e, pb=pb, P=P: e.matmul(pb[:, 0:128], identb, P, start=False, stop=True), W.tks + [self.cbf_t], pbt)
                evac(hh, tinv[hh].ap, pb[:, 0:128], pbt, tinv[hh].tks)
            for hh, h in enumerate(hs):
                blk = h // 2
                W = Wb[hh]
                pb, pbt = self.bank(hh)
                self.E("pe", lambda e, pb=pb, hh=hh, blk=blk: e.matmul(pb[:, 0:128], at.ap[:, blk * 128:(blk + 1) * 128], tinv[hh].ap, start=True, stop=True),
                       at.tks + tinv[hh].tks, pbt)
                self.E("pe", lambda e, pb=pb, hh=hh, W=W: e.matmul(pb[:, 128:256], W.ap[:, 384:512], tinv[hh].ap, start=True, stop=True),
                       W.tks + tinv[hh].tks, pbt)
                evac(hh, pm[hh].ap, pb[:, 0:256], pbt, pm[hh].tks)
            for hh, h in enumerate(hs):
                pb, pbt = self.bank(hh)
                self.E("pe", lambda e, pb=pb, hh=hh, h=h: e.matmul(pb[:, 0:64], pm[hh].ap[:, 0:128], self.sbf[:, h, :], start=True, stop=False),
                       pm[hh].tks + [self.sbf_t[h]], pbt)
                self.E("pe", lambda e, pb=pb, hh=hh, h=h: e.matmul(pb[:, 0:64], pm[hh].ap[:, 128:256], vb.ap[:, h * 64:(h + 1) * 64], start=False, stop=True),
                       pm[hh].tks + vb.tks, pbt)
                evac(hh, ut[hh].ap, pb[:, 0:64], pbt, ut[hh].tks)
            for hh, h in enumerate(hs):
                blk, p0 = h // 2, (h % 2) * 64
                W = Wb[hh]
                pb, pbt = self.bank(hh)
                self.E("pe", lambda e, pb=pb, h=h, blk=blk: e.matmul(pb[:, 0:64], fmT.ap[:, blk, 1, :], self.sbf[:, h, :], start=True, stop=False),
                       fmT.tks + [self.sbf_t[h]], pbt)
                self.E("pe", lambda e, pb=pb, hh=hh, W=W: e.matmul(pb[:, 0:64], W.ap[:, 512:640], ut[hh].ap, start=False, stop=False),
                       W.tks + ut[hh].tks, pbt)
                self.E("pe", lambda e, pb=pb, W=W, h=h: e.matmul(pb[:, 0:64], W.ap[:, 640:768], vb.ap[:, h * 64:(h + 1) * 64], start=False, stop=True),
                       W.tks + vb.tks, pbt)
                self.E("pe", lambda e, pb=pb, hh=hh, blk=blk: e.matmul(pb[:, 64:128], bh.ap[:, blk * 128:(blk + 1) * 128], ut[hh].ap, start=True, stop=False),
                       bh.tks + ut[hh].tks, pbt)
                self.E("pe", lambda e, pb=pb, h=h, blk=blk: e.matmul(pb[:, 64:128], kh.ap[:, blk * 128:(blk + 1) * 128], vb.ap[:, h * 64:(h + 1) * 64], start=False, stop=True),
                       kh.tks + vb.tks, pbt)
                evac(hh, y.ap[:, h * 64:(h + 1) * 64], pb[:, 0:64], pbt, y.tks)
                self.E("dve", lambda e, pb=pb, h=h, blk=blk, p0=p0: e.scalar_tensor_tensor(
                    self.spad[p0:p0 + 64, h, :], self.spad[p0:p0 + 64, h, :], sm.ap[p0:p0 + 64, 7, blk:blk + 1], pb[p0:p0 + 64, 64:128],
                    ALU.mult, ALU.add), [self.spad_t[h]] + sm.tks + pbt, [self.spad_t[h]])
                self.E("act", lambda e, h=h, p0=p0: e.copy(self.sbf[p0:p0 + 64, h, :], self.spad[p0:p0 + 64, h, :]),
                       [self.spad_t[h]], [self.sbf_t[h]])
        if RWDBG <= 6:
            return
        self.E("dve", lambda e: e.tensor_tensor(bb.ap, y.ap, y.ap, ALU.mult), y.tks, bb.tks)
        self.E("dve", lambda e: e.tensor_reduce(sm.ap[:, 2, :], v3(y), AX.X, ALU.add), y.tks, sm.tks)
        self.E("dve", lambda e: e.tensor_reduce(sm.ap[:, 3, :], v3(bb), AX.X, ALU.add), bb.tks, sm.tks)
        self.E("dve", lambda e: e.tensor_scalar(sm.ap[:, 2, :], sm.ap[:, 2, :], 1.0 / 64, None, ALU.mult), sm.tks, sm.tks)
        self.E("dve", lambda e: e.tensor_tensor(sm.ap[:, 4, :], sm.ap[:, 2, :], sm.ap[:, 2, :], ALU.mult), sm.tks, sm.tks)
        self.E("dve", lambda e: e.scalar_tensor_tensor(sm.ap[:, 3, :], sm.ap[:, 3, :], 1.0 / 64, sm.ap[:, 4, :], ALU.mult, ALU.subtract),
               sm.tks, sm.tks)
        self.E("act", lambda e: e.activation(sm.ap[:, 3, :], sm.ap[:, 3, :], AF.Sqrt, bias=cv[:, CV_GNEPS:CV_GNEPS + 1]),
               sm.tks + [self.colvec_t], sm.tks)
        self.E("dve", lambda e: e.reciprocal(sm.ap[:, 3, :], sm.ap[:, 3, :]), sm.tks, sm.tks)
        self.E("dve", lambda e: e.tensor_tensor(v3(y), v3(y), bc(2), ALU.subtract), y.tks + sm.tks, y.tks)
        self.E("dve", lambda e: e.tensor_tensor(v3(zb), v3(y), bc(3), ALU.mult), y.tks + sm.tks, zb.tks)
        for blk in range(DC):
            phb, pht = self.bk(blk % 4, 0, 256)
            self.E("pe", lambda e, blk=blk, phb=phb: e.matmul(phb[:, 0:128], zb.ap[:, blk * 128:(blk + 1) * 128], identb, start=True, stop=True),
                   zb.tks + [self.cbf_t], pht)
            self.E("pe", lambda e, blk=blk, phb=phb: e.matmul(phb[:, 128:256], bv.ap[:, blk * 128:(blk + 1) * 128], identb, start=True, stop=True),
                   bv.tks + [self.cbf_t], pht)
            pg, pgt = self.bk(4 + blk % 4, 0, 256)
            self.E("pe", lambda e, blk=blk, pg=pg: e.matmul(pg[:, 0:128], self.g2b[:, blk * 128:(blk + 1) * 128], g1T.ap[:, cs], start=True, stop=True),
                   [self.cw_t] + g1T.tks, pgt)
            tf = tmpf[blk % 2]
            self.E("dve", lambda e, blk=blk, phb=phb, tf=tf: e.tensor_scalar(
                tf.ap, phb[:, 0:128], cv[:, CV_LNW + blk:CV_LNW + blk + 1], cv[:, CV_LNB + blk:CV_LNB + blk + 1], ALU.mult, ALU.add),
                pht + [self.colvec_t], tf.tks)
            self.E("dve", lambda e, phb=phb, tf=tf: e.tensor_tensor(tf.ap, tf.ap, phb[:, 128:256], ALU.add), pht + tf.tks, tf.tks)
            self.E("dve", lambda e, blk=blk, pg=pg, tf=tf: e.tensor_tensor(yT[blk].ap[:, cs], tf.ap, pg[:, 0:128], ALU.mult),
                   tf.tks + pgt, yT[blk].tks)


    def run_stage(self, ti, s):
        if s.startswith("ffn"):
            self.ffn(int(s[3:]))
        elif s == "ab":
            self.ab(ti)
        elif s == "rwkv":
            self.rwkv(ti)
        elif s == "dbgnorm":
            self.ar_reset()
            self.rmsnorm(0)
            for dc in range(DC):
                self.E("dve", lambda e, dc=dc: e.tensor_copy(self.xT[:, dc, :], self.hv(dc)),
                       [self.h_t[dc]], [self.xT_t[dc]])


_CACHE = {}


def get_program(T, stages):
    key = (T, tuple(stages))
    if key not in _CACHE:
        b = Builder(T, list(stages))
        nc = b.build()
        _CACHE[key] = (nc, b)
    return _CACHE[key]


ALL_STAGES = ["ffn0", "ab", "ffn1", "ffn2", "rwkv", "ffn3"]


def make_in_maps(inp, T, stages, ncores):
    x = np.asarray(inp["x"], np.float32)
    f32 = lambda a: np.ascontiguousarray(np.asarray(a, np.float32))
    common = {"colvec": host_colvec(inp), "cmat": host_consts()}
    if any(s_.startswith("ffn") for s_ in stages):
        common["wg"] = f32(inp["ffn_w_gate"]).reshape(4, D, DFF)
        common["wu"] = f32(inp["ffn_w_up"]).reshape(4, D, DFF)
        common["wd"] = f32(inp["ffn_w_down"]).reshape(4, DFF, D)
    if "ab" in stages:
        w_in = f32(inp["ab_w_in"]).reshape(D, 1280)
        qcols = w_in[:, :512].reshape(D, 2, 4, 64).transpose(0, 2, 1, 3).reshape(D, 512)
        common["w_in"] = np.ascontiguousarray(np.concatenate([qcols, w_in[:, 512:]], axis=1))
        w_out = f32(inp["ab_w_out"]).reshape(D, D)
        arows = w_out[:512].reshape(2, 4, 64, D).transpose(1, 0, 2, 3).reshape(512, D)
        common["w_out"] = np.ascontiguousarray(np.concatenate([arows, w_out[512:]], axis=0))
        common["pool_w"] = f32(inp["pool_w"]).reshape(4, 128, 128)
    if "rwkv" in stages:
        for nm in ("c_w_r", "c_w_k", "c_w_v", "c_w_o"):
            common[nm] = f32(inp[nm]).reshape(D, D)
        common["c_w1"] = f32(inp["c_w1"]).reshape(D, 64)
        common["c_a1"] = f32(inp["c_a1"]).reshape(D, 64)
        common["c_g1"] = f32(inp["c_g1"]).reshape(D, 128)
        common["c_w2e"] = np.ascontiguousarray(np.concatenate([f32(inp["c_w2"]).reshape(64, D), f32(inp["c_w0"]).reshape(1, D)], axis=0))
        common["c_a2e"] = np.ascontiguousarray(np.concatenate([f32(inp["c_a2"]).reshape(64, D), f32(inp["c_a0"]).reshape(1, D)], axis=0))
        common["c_g2"] = f32(inp["c_g2"]).reshape(128, D)
        common["c_rows"] = np.ascontiguousarray(np.stack([f32(inp["c_k_k"]).reshape(D), f32(inp["c_k_a"]).reshape(D),
                                                          f32(inp["c_r_k"]).reshape(D)], axis=0))
    in_maps = []
    for ci in range(ncores):
        m = dict(common)
        m["x"] = np.ascontiguousarray(x[ci, :T])
        in_maps.append(m)
    return in_maps


def run(inp, T=SEQ, stages=ALL_STAGES, ncores=8, trace=False):
    nc, b = get_program(T, stages)
    in_maps = make_in_maps(inp, T, stages, ncores)
    res = run_bass_kernel_spmd(nc, in_maps, core_ids=list(range(ncores)), trace=trace)
    out = np.stack([np.asarray(r["y"]) for r in res.results], axis=0)
    return out, res


def kernel(**inputs):
    out, _ = run(inputs)
    return out.astype(np.float32)
```

```python
import contextlib
import os
import numpy as np
import concourse.bass as bass
import concourse.mybir as mybir
from concourse.bass_utils import run_bass_kernel_spmd

F32 = mybir.dt.float32
BF16 = mybir.dt.bfloat16
ALU = mybir.AluOpType
AF = mybir.ActivationFunctionType
AX = mybir.AxisListType

D = 1024
DC = 8
DFF = 2816
FC = 22
TT = 512
NB = TT // 128
SEQ = 8192
RMS_EPS = 1e-6
GN_EPS = 64e-5
NEG_E = -float(np.exp(-0.5))
ENGS = ["pe", "act", "dve", "pool", "sp"]
SEM_CH = int(os.environ.get("SEM_CH", "12000"))
ARENA_KB = 98
RWDBG = float(os.environ.get('RWDBG', '9'))
NG_RING = 6


class Tk:
    __slots__ = ("name", "w", "r", "excl")

    def __init__(self, name, excl=False):
        self.name = name
        self.w = {}
        self.r = {}
        self.excl = excl


class Buf:
    __slots__ = ("ap", "tks")

    def __init__(self, ap, tks):
        self.ap = ap
        self.tks = tks


class Slot:
    def __init__(self, key, group=False):
        self.key = key
        self.count = 0
        self.sem = None
        self.group = group


class Op:
    __slots__ = ("eng", "fn", "deps", "tok", "slot", "snap", "waits", "signal", "sig")

    def __init__(self, eng, fn, deps, tok, slot):
        self.eng = eng
        self.fn = fn
        self.deps = deps
        self.tok = tok
        self.slot = slot
        self.snap = None
        self.waits = ()
        self.signal = False
        self.sig = None


class Ctx:
    def __init__(self, nc):
        self.nc = nc
        self.oplist = []
        self.eng_ops = {e: [] for e in ENGS}
        self.tokmap = {}
        self.slots = []
        self.es = contextlib.ExitStack()
        self.out_slots = []

    def sb(self, name, shape, dt):
        return self.es.enter_context(self.nc.sbuf_tensor("sb_" + name, list(shape), dt))

    def ps(self, name, shape, dt=F32):
        return self.es.enter_context(self.nc.psum_tensor("ps_" + name, list(shape), dt))

    def slot(self, name, group=False):
        s = Slot(("dma", name), group)
        self.slots.append(s)
        return s

    def emit(self, eng, fn, reads=(), writes=(), slot=None):
        if any(t.excl for t in reads):
            writes = list(writes) + [t for t in reads if t.excl]
            reads = [t for t in reads if not t.excl]
        deps = {}
        for t in reads:
            for k, v in t.w.items():
                if deps.get(k, -1) < v:
                    deps[k] = v
        for t in writes:
            for k, v in t.w.items():
                if deps.get(k, -1) < v:
                    deps[k] = v
            for k, v in t.r.items():
                if deps.get(k, -1) < v:
                    deps[k] = v
        if eng == "pe":
            deps.pop("pe", None)
        if slot is not None:
            slot.count += 1
            tok = (slot.key, slot.count)
        else:
            tok = (eng, len(self.eng_ops[eng]))
        op = Op(eng, fn, deps, tok, slot)
        self.oplist.append(op)
        self.eng_ops[eng].append(op)
        self.tokmap[tok] = op
        k, v = tok
        for t in reads:
            if t.r.get(k, -1) < v:
                t.r[k] = v
        for t in writes:
            t.w = {k: v}
            t.r = {}
        return tok

    def finalize(self):
        nc = self.nc
        known = {e: {} for e in ENGS}
        for op in self.oplist:
            kn = known[op.eng]
            waits = []
            copied = False
            for k, v in op.deps.items():
                if kn.get(k, -1) >= v:
                    continue
                if not copied:
                    kn = dict(kn)
                    known[op.eng] = kn
                    copied = True
                waits.append((k, v))
                prod = self.tokmap[(k, v)]
                prod.signal = True
                for kk, vv in prod.snap.items():
                    if kn.get(kk, -1) < vv:
                        kn[kk] = vv
                kn[k] = v
            op.waits = waits
            op.snap = kn
        self.eng_sems = {e: [] for e in ENGS}
        for e in ENGS:
            n = 0
            for op in self.eng_ops[e]:
                if op.slot is None and op.signal:
                    op.sig = n
                    n += 1
            nch = (n + SEM_CH - 1) // SEM_CH
            for c in range(nch):
                self.eng_sems[e].append(self.es.enter_context(nc.semaphore(f"s_{e}_{c}")))
        for s in self.slots:
            if s.count > 0:
                s.sem = self.es.enter_context(nc.semaphore("d_" + s.key[1]))

        def resolve(k, v):
            if isinstance(k, tuple):
                s = self.tokmap[(k, v)].slot
                return s.sem, 16 * (s.count if s.group else v)
            sig = self.tokmap[(k, v)].sig
            return self.eng_sems[k][sig // SEM_CH], sig % SEM_CH + 1

        ctx = self

        def run_engine(ename):
            def body(e):
                for op in ctx.eng_ops[ename]:
                    ws = [resolve(k, v) for (k, v) in op.waits]
                    if op.slot is not None:
                        for (s, v) in ws:
                            e.wait_ge(s, v)
                        ins = op.fn(e)
                        ins.then_inc(op.slot.sem, 16)
                    else:
                        for (s, v) in ws[:-1]:
                            e.wait_ge(s, v)
                        ins = op.fn(e)
                        if ws:
                            ins._wait_ge(*ws[-1])
                        if op.signal:
                            ins.then_inc(ctx.eng_sems[ename][op.sig // SEM_CH], 1)
                if ename == "sp":
                    for s in ctx.out_slots:
                        if s.count:
                            e.wait_ge(s.sem, 16 * s.count)
            return body

        with nc.Block() as block:
            block.tensor(run_engine("pe"))
            block.scalar(run_engine("act"))
            block.vector(run_engine("dve"))
            block.gpsimd(run_engine("pool"))
            block.sync(run_engine("sp"))
        self.stats = {e: len(self.eng_ops[e]) for e in ENGS}
        self.stats["waits"] = sum(len(op.waits) for op in self.oplist)
        self.stats["sems"] = sum(len(v) for v in self.eng_sems.values()) + sum(1 for s in self.slots if s.count)


CV_FFN = 0
CV_AB = 32
CV_C = 40
CV_QN = 48
CV_KN = 49
CV_PS = 50
CV_MU = 54
CV_SINK = 102
CV_LNW = 110
CV_LNB = 118
CV_GNEPS = 126
CV_EPS = 127
NCOL = 128
CM_IDENT = 0
CM_ONES = 128
CM_BD = 256
CM_AM = 384
CM_INVC = CM_AM + 2048
CM_TRIS = CM_INVC + 64
CM_ONESS = CM_TRIS + 128
CM_SU = CM_ONESS + 128
CM_UI = CM_SU + 128
CM_SL = CM_UI + 256
NCMAT = CM_SL + 256


def host_consts():
    cm = np.zeros((128, NCMAT), np.float32)
    cm[:, CM_IDENT:CM_IDENT + 128] = np.eye(128, dtype=np.float32)
    cm[:, CM_ONES:CM_ONES + 128] = 1.0
    cm[0:64, CM_BD:CM_BD + 64] = 1.0
    cm[64:128, CM_BD + 64:CM_BD + 128] = 1.0
    kk = np.arange(128)[:, None].astype(np.float64)
    qq = np.arange(128)[None, :].astype(np.float64)
    am = np.zeros((128, 2, 2, 4, 128), np.float64)
    for hk in range(2):
        for g in range(4):
            hq = hk * 4 + g
            slope = 2.0 ** (-8.0 * (hq + 1) / 8.0)
            dist_cur = qq - kk
            am[:, hk, 1, g, :] = np.where(dist_cur >= 0, np.exp(-slope * dist_cur), 0.0)
            dist_prev = qq - kk + 128
            am[:, hk, 0, g, :] = np.where(dist_prev < 128, np.exp(-slope * dist_prev), 0.0)
    cm[:, CM_AM:CM_AM + 2048] = am.reshape(128, 2048).astype(np.float32)
    invc = np.zeros((4, 16), np.float64)
    for gi, w in enumerate((2, 4, 8, 16)):
        invc[gi] = 1.0 / np.minimum(np.arange(1, 17), w)
    cm[:, CM_INVC:CM_INVC + 64] = invc.reshape(1, 64).astype(np.float32)
    s_ = np.arange(128)
    cm[:, CM_TRIS:CM_TRIS + 128] = NEG_E * (s_[:, None] <= s_[None, :]).astype(np.float32)
    cm[:, CM_ONESS:CM_ONESS + 128] = NEG_E
    cm[:, CM_SU:CM_SU + 128] = (s_[:, None] < s_[None, :]).astype(np.float32)
    cm[:, CM_UI:CM_UI + 128] = (s_[:, None] <= s_[None, :]).astype(np.float32)
    cm[:, CM_UI + 128:CM_UI + 256] = (s_[:, None] <= s_[None, :]).astype(np.float32)
    sl = (s_[:, None] > s_[None, :]).astype(np.float32)
    cm[:, CM_SL:CM_SL + 128] = sl
    cm[:, CM_SL + 128:CM_SL + 256] = sl
    return cm


def host_colvec(inp):
    cv = np.zeros((128, NCOL), np.float32)
    cv[:, CV_EPS] = RMS_EPS
    cv[:, CV_GNEPS] = GN_EPS
    fn = np.asarray(inp["ffn_norm"], np.float32).reshape(4, DC, 128)
    for li in range(4):
        cv[:, CV_FFN + li * 8:CV_FFN + li * 8 + 8] = fn[li].T
    cv[:, CV_AB:CV_AB + 8] = np.asarray(inp["ab_norm"], np.float32).reshape(DC, 128).T
    cv[:, CV_C:CV_C + 8] = np.asarray(inp["c_norm"], np.float32).reshape(DC, 128).T
    cv[:, CV_QN] = np.tile(np.asarray(inp["q_norm"], np.float32).reshape(64), 2)
    cv[:, CV_KN] = np.tile(np.asarray(inp["k_norm"], np.float32).reshape(64), 2)
    cv[:, CV_PS:CV_PS + 4] = np.asarray(inp["pool_scale"], np.float32).reshape(4, 128).T
    cv[:, CV_SINK:CV_SINK + 8] = np.asarray(inp["attn_sinks"], np.float32).reshape(1, 8)
    mu = np.asarray(inp["c_mu"], np.float32).reshape(6, DC, 128)
    for i in range(6):
        cv[:, CV_MU + i * 8:CV_MU + i * 8 + 8] = mu[i].T
    cv[:, CV_LNW:CV_LNW + 8] = np.asarray(inp["c_lnx_w"], np.float32).reshape(DC, 128).T
    cv[:, CV_LNB:CV_LNB + 8] = np.asarray(inp["c_lnx_b"], np.float32).reshape(DC, 128).T
    return cv


class Builder:
    def __init__(self, T, stages):
        self.T = T
        self.NT = T // TT
        self.stages = stages
        nc = bass.Bass("TRN2", target_bir_lowering=False)
        self.nc = nc
        self.c = Ctx(nc)

    def dram_in(self, name, shape):
        return self.nc.dram_tensor(name, list(shape), F32, kind="ExternalInput").ap()

    def E(self, eng, fn, reads=(), writes=()):
        self.c.emit(eng, fn, reads=reads, writes=writes)

    def ar_reset(self):
        self.ar_ptr = 0

    def ar_alloc(self, free_shape, dt):
        n = 1
        for s_ in free_shape:
            n *= s_
        nbytes = n * (2 if dt == BF16 else 4)
        nb = (nbytes + 255) // 256
        b0 = self.ar_ptr
        self.ar_ptr += nb
        assert self.ar_ptr <= ARENA_KB * 4, f"arena overflow {self.ar_ptr}"
        self.ar_peak = max(getattr(self, "ar_peak", 0), self.ar_ptr)
        ap = self.arena[:, b0 * 64:b0 * 64 + (nbytes + 3) // 4]
        if dt == BF16:
            ap = ap.bitcast(BF16)
        if len(free_shape) == 2:
            ap = ap.rearrange("p (a b) -> p a b", a=free_shape[0])
        elif len(free_shape) == 3:
            ap = ap.rearrange("p (a b c) -> p a b c", a=free_shape[0], b=free_shape[1])
        return Buf(ap, self.ar_t[b0:b0 + nb])

    def bank(self, b):
        return self.psb[b], [self.psh_t[b]]

    def bk(self, b, lo, hi):
        return self.psb[b][:, lo:hi], [self.psh_t[b]]

    def gload(self, src):
        i = self.gcount % NG_RING
        self.gcount += 1
        buf, t, sl = self.gring[i], self.gring_t[i], self.gring_s[i]
        assert (not t.w) or t.r, "G ring buffer overwritten before it was consumed"
        self.c.emit("pool", lambda e: e.dma_start(out=buf[:].rearrange("p (c n) -> p c n", c=DC), in_=src),
                    writes=[t], slot=sl)
        return buf[:].rearrange("p (c n) -> p c n", c=DC), t

    def dload(self, src):
        i = self.dcount % 4
        self.dcount += 1
        buf, t, sl = self.dring[i], self.dring_t[i], self.dring_s[i]
        assert (not t.w) or t.r, "D ring buffer overwritten before it was consumed"
        self.c.emit("pool", lambda e: e.dma_start(out=buf[:].rearrange("p (c n) -> p c n", c=FC // 2), in_=src),
                    writes=[t], slot=sl)
        return buf[:].rearrange("p (c n) -> p c n", c=FC // 2), t

    def lazy(self, srcs, ahead):
        items = [None] * len(srcs)
        state = {"next": 0}

        def get(i):
            while state["next"] <= min(i + ahead, len(srcs) - 1):
                j = state["next"]
                items[j] = self.gload(srcs[j])
                state["next"] += 1
            return items[i]
        return get

    def build(self):
        nc, c = self.nc, self.c
        T = self.T
        st = self.stages
        self.x = self.dram_in("x", [T, D])
        self.y = nc.dram_tensor("y", [T, D], F32, kind="ExternalOutput").ap()
        self.colvec_d = self.dram_in("colvec", [128, NCOL])
        self.cmat_d = self.dram_in("cmat", [128, NCMAT])
        need_ffn = any(s.startswith("ffn") for s in st)
        if need_ffn:
            self.wg = self.dram_in("wg", [4, D, DFF])
            self.wu = self.dram_in("wu", [4, D, DFF])
            self.wd = self.dram_in("wd", [4, DFF, D])
        self.alloc_common()
        self.setup_consts()
        if "ab" in st:
            self.setup_ab()
        if "rwkv" in st:
            self.setup_rwkv()
        for ti in range(self.NT):
            self.load_x(ti)
            for s in st:
                self.run_stage(ti, s)
            self.store_x(ti)
        c.finalize()
        return nc

    def alloc_common(self):
        c = self.c
        self.colvec = c.sb("colvec", [128, NCOL], F32)
        self.colvec_t = Tk("colvec")
        self.cmat = c.sb("cmat", [128, NCMAT], F32)
        self.cmat_t = Tk("cmat")
        self.cbf = c.sb("cbf", [128, 512], BF16)
        self.cbf_t = Tk("cbf")
        self.xT = c.sb("xT", [128, DC, TT], F32)
        self.xT_t = [Tk(f"xT{i}") for i in range(DC)]
        self.xin_slot = c.slot("xin")
        self.xout_slot = c.slot("xout")
        c.out_slots.append(self.xout_slot)
        self.h = c.sb("h", [128, DC, TT + 1], BF16)
        self.h_t = [Tk(f"h{i}") for i in range(DC)]
        self.arena = c.sb("arena", [128, ARENA_KB * 256], F32)
        self.ar_t = [Tk(f"ar{i}") for i in range(ARENA_KB * 4)]
        self.ar_ptr = 0
        self.psb = [c.ps(f"psb{i}", [128, 512], F32) for i in range(8)]
        self.psh_t = [Tk(f"psb{i}", excl=True) for i in range(8)]
        self.const_slot = c.slot("const", group=True)
        self.gring = [c.sb(f"gring{i}", [128, 2048], BF16) for i in range(NG_RING)]
        self.gring_t = [Tk(f"gring{i}") for i in range(NG_RING)]
        self.gring_s = [c.slot(f"gring{i}") for i in range(NG_RING)]
        self.dring = [c.sb(f"dring{i}", [128, (FC // 2) * 128], BF16) for i in range(4)]
        self.dring_t = [Tk(f"dring{i}") for i in range(4)]
        self.dring_s = [c.slot(f"dring{i}") for i in range(4)]
        self.gcount = 0
        self.dcount = 0

    def setup_consts(self):
        c = self.c
        cv, cm = self.colvec, self.cmat
        c.emit("sp", lambda e: e.dma_start(out=cv[:], in_=self.colvec_d[:, :]),
               writes=[self.colvec_t], slot=self.const_slot)
        c.emit("sp", lambda e: e.dma_start(out=cm[:], in_=self.cmat_d[:, :]),
               writes=[self.cmat_t], slot=self.const_slot)
        cb = self.cbf
        self.E("dve", lambda e: e.tensor_copy(cb[:, 0:128], cm[:, CM_ONES:CM_ONES + 128]),
               [self.cmat_t], [self.cbf_t])
        self.E("dve", lambda e: e.tensor_copy(cb[:, 128:256], cm[:, CM_IDENT:CM_IDENT + 128]),
               [self.cmat_t, self.cbf_t], [self.cbf_t])
        self.E("dve", lambda e: e.tensor_copy(cb[:, 256:384], cm[:, CM_BD:CM_BD + 128]),
               [self.cmat_t, self.cbf_t], [self.cbf_t])
        self.E("dve", lambda e: e.tensor_copy(cb[:, 384:512], cm[:, CM_UI:CM_UI + 128]),
               [self.cmat_t, self.cbf_t], [self.cbf_t])
        self.E("dve", lambda e: e.memset(self.h[:], 0.0), [], self.h_t)

    def load_x(self, ti):
        c = self.c
        self.ar_reset()
        xin = self.ar_alloc([NB, D], F32)
        src = self.x[ti * TT:(ti + 1) * TT, :].rearrange("(b p) d -> p b d", p=128)
        c.emit("sp", lambda e: e.dma_start(out=xin.ap, in_=src), writes=xin.tks, slot=self.xin_slot)
        ident = self.cmat[:, CM_IDENT:CM_IDENT + 128]
        for dc in range(DC):
            ps, pst = self.bank(dc % 2)
            for b in range(NB):
                self.E("pe", lambda e, b=b, dc=dc, ps=ps: e.transpose(
                    ps[:, b * 128:(b + 1) * 128], xin.ap[:, b, dc * 128:(dc + 1) * 128], ident),
                    xin.tks + [self.cmat_t], pst)
            if dc % 2 == 0:
                self.E("act", lambda e, dc=dc, ps=ps: e.copy(self.xT[:, dc, :], ps[:]), pst, [self.xT_t[dc]])
            else:
                self.E("dve", lambda e, dc=dc, ps=ps: e.tensor_copy(self.xT[:, dc, :], ps[:]), pst, [self.xT_t[dc]])

    def store_x(self, ti):
        c = self.c
        self.ar_reset()
        xo = self.ar_alloc([NB, D], F32)
        ident = self.cmat[:, CM_IDENT:CM_IDENT + 128]
        k = 0
        for b in range(NB):
            for half in range(2):
                ps, pst = self.bank(k % 2)
                k += 1
                for q in range(4):
                    dc = half * 4 + q
                    self.E("pe", lambda e, b=b, dc=dc, q=q, ps=ps: e.transpose(
                        ps[:, q * 128:(q + 1) * 128], self.xT[:, dc, b * 128:(b + 1) * 128], ident),
                        [self.xT_t[dc], self.cmat_t], pst)
                if k % 2 == 0:
                    self.E("act", lambda e, b=b, half=half, ps=ps: e.copy(xo.ap[:, b, half * 512:(half + 1) * 512], ps[:]),
                           pst, xo.tks)
                else:
                    self.E("dve", lambda e, b=b, half=half, ps=ps: e.tensor_copy(xo.ap[:, b, half * 512:(half + 1) * 512], ps[:]),
                           pst, xo.tks)
        dst = self.y[ti * TT:(ti + 1) * TT, :].rearrange("(b p) d -> p b d", p=128)
        c.emit("sp", lambda e: e.dma_start(out=dst, in_=xo.ap), reads=xo.tks, slot=self.xout_slot)

    def rmsnorm(self, gcol):
        ones = self.cbf[:, 0:128]
        sq = [self.ar_alloc([TT], BF16) for _ in range(2)]
        rstd = self.ar_alloc([TT], F32)
        ps, pst = self.bank(2)
        for dc in range(DC):
            s_ = sq[dc % 2]
            self.E("act", lambda e, dc=dc, s_=s_: e.activation(s_.ap, self.xT[:, dc, :], AF.Square),
                   [self.xT_t[dc]], s_.tks)
            self.E("pe", lambda e, dc=dc, s_=s_: e.matmul(ps[:], ones, s_.ap, start=(dc == 0), stop=(dc == DC - 1)),
                   s_.tks + [self.cbf_t], pst)
        self.E("act", lambda e: e.activation(rstd.ap, ps[:], AF.Sqrt, bias=self.colvec[:, CV_EPS:CV_EPS + 1], scale=1.0 / D),
               pst + [self.colvec_t], rstd.tks)
        self.E("dve", lambda e: e.reciprocal(rstd.ap, rstd.ap), rstd.tks, rstd.tks)
        for dc in range(DC):
            self.E("dve", lambda e, dc=dc: e.scalar_tensor_tensor(
                self.h[:, dc, 1:TT + 1], self.xT[:, dc, :], self.colvec[:, gcol + dc:gcol + dc + 1], rstd.ap,
                ALU.mult, ALU.mult),
                [self.xT_t[dc], self.colvec_t] + rstd.tks, [self.h_t[dc]])

    def hv(self, dc, lo=0, hi=TT):
        return self.h[:, dc, 1 + lo:1 + hi]

    def ffn(self, li):
        self.ar_reset()
        wgv = self.wg[li].rearrange("(c p) f -> p c f", p=128)
        wuv = self.wu[li].rearrange("(c p) f -> p c f", p=128)
        wdv = self.wd[li].rearrange("(c p) n -> p c n", p=128)
        NG = FC // 2
        q = []

        def issue(g):
            q.append((self.gload(wgv[:, :, g * 256:(g + 1) * 256]), self.gload(wuv[:, :, g * 256:(g + 1) * 256])))
        issue(0)
        issue(1)
        HF = FC // 2
        dsrc = [wdv[:, hf * HF:(hf + 1) * HF, dc * 128:(dc + 1) * 128] for dc in range(DC) for hf in range(2)]
        dq = [self.dload(dsrc[0]), self.dload(dsrc[1])]
        self.rmsnorm(CV_FFN + li * 8)
        act = [self.ar_alloc([TT], BF16) for _ in range(FC)]
        silu = [self.ar_alloc([TT], F32) for _ in range(2)]
        for g in range(NG):
            if g + 2 < NG:
                issue(g + 2)
            (gb, gt), (ub, ut) = q.pop(0)
            for fc in range(2):
                f = g * 2 + fc
                psg, psgt = self.bank(3 + (f % 2))
                psu, psut = self.bank(5 + (f % 2))
                for dc in range(DC):
                    self.E("pe", lambda e, dc=dc, fc=fc, gb=gb, psg=psg: e.matmul(
                        psg[:], gb[:, dc, fc * 128:(fc + 1) * 128], self.hv(dc), start=(dc == 0), stop=(dc == DC - 1)),
                        [gt, self.h_t[dc]], psgt)
                for dc in range(DC):
                    self.E("pe", lambda e, dc=dc, fc=fc, ub=ub, psu=psu: e.matmul(
                        psu[:], ub[:, dc, fc * 128:(fc + 1) * 128], self.hv(dc), start=(dc == 0), stop=(dc == DC - 1)),
                        [ut, self.h_t[dc]], psut)
                st_ = silu[f % 2]
                self.E("act", lambda e, st_=st_, psg=psg: e.activation(st_.ap, psg[:], AF.Silu), psgt, st_.tks)
                self.E("dve", lambda e, st_=st_, psu=psu, f=f: e.tensor_tensor(act[f].ap, st_.ap, psu[:], ALU.mult),
                       st_.tks + psut, act[f].tks)
        for dc in range(DC):
            psy, psyt = self.bank(7 if dc % 2 == 0 else 2)
            for hf in range(2):
                nxt = dc * 2 + hf + 2
                if nxt < 2 * DC:
                    dq.append(self.dload(dsrc[nxt]))
                db, dt_ = dq.pop(0)
                for ff in range(HF):
                    f = hf * HF + ff
                    self.E("pe", lambda e, f=f, ff=ff, db=db, psy=psy: e.matmul(
                        psy[:], db[:, ff, :], act[f].ap, start=(f == 0), stop=(f == FC - 1)),
                        [dt_] + act[f].tks, psyt)
            self.E("dve", lambda e, dc=dc, psy=psy: e.scalar_tensor_tensor(
                self.xT[:, dc, :], psy[:], 0.5, self.xT[:, dc, :], ALU.mult, ALU.add),
                psyt + [self.xT_t[dc]], [self.xT_t[dc]])

    def setup_ab(self):
        c = self.c
        self.w_in_d = self.dram_in("w_in", [D, 1280])
        self.w_out_d = self.dram_in("w_out", [D, D])
        self.pool_w_d = self.dram_in("pool_w", [4, 128, 128])
        self.pw = c.sb("pw", [128, 4, 128], BF16)
        self.pw_t = Tk("pw")
        self.abw_slot = c.slot("abw", group=True)
        self.kcar = c.sb("kcar", [128, 128], BF16)
        self.kcar_t = Tk("kcar")
        self.vcar = c.sb("vcar", [128, 128], BF16)
        self.vcar_t = Tk("vcar")
        self.pcar = c.sb("pcar", [128, 4, 16], F32)
        self.pcar_t = Tk("pcar")
        self.esk = c.sb("esink", [128, 512], F32)
        self.esb = c.sb("esb", [128, 8], F32)
        self.esk_t = Tk("esink")
        src4 = self.pool_w_d.rearrange("g c d -> c g d")
        c.emit("pool", lambda e: e.dma_start(out=self.pw[:], in_=src4), writes=[self.pw_t], slot=self.abw_slot)
        self.E("act", lambda e: e.activation(self.esb[:], self.colvec[:, CV_SINK:CV_SINK + 8], AF.Exp),
               [self.colvec_t], [self.esk_t])
        for hk in range(2):
            for g in range(4):
                hq = hk * 4 + g
                self.E("dve", lambda e, hk=hk, g=g, hq=hq: e.tensor_copy(
                    self.esk[hk * 64:(hk + 1) * 64, g * 128:(g + 1) * 128],
                    self.esb[hk * 64:(hk + 1) * 64, hq:hq + 1].to_broadcast([64, 128])),
                    [self.esk_t], [self.esk_t])
        self.E("dve", lambda e: e.memset(self.pcar[:], 0.0), [], [self.pcar_t])
        self.E("dve", lambda e: e.memset(self.kcar[:], 0.0), [], [self.kcar_t])
        self.E("dve", lambda e: e.memset(self.vcar[:], 0.0), [], [self.vcar_t])

    def ab(self, ti):
        self.ar_reset()
        bd = self.cbf[:, 256:384]
        ones = self.cbf[:, 0:128]
        winv = self.w_in_d.rearrange("(c p) n -> p c n", p=128)
        woutv = self.w_out_d.rearrange("(c p) n -> p c n", p=128)
        wq = self.lazy([winv[:, :, i * 256:(i + 1) * 256] for i in range(5)], 1)
        wq(0)
        self.rmsnorm(CV_AB)
        qn = [self.ar_alloc([TT], BF16) for _ in range(4)]
        kn = self.ar_alloc([128 + TT], BF16)
        vtm = self.ar_alloc([NB + 1, 128], BF16)
        qraw = [self.ar_alloc([TT], F32) for _ in range(2)]
        sqb = [self.ar_alloc([TT], BF16) for _ in range(2)]
        rstd2 = self.ar_alloc([TT], F32)
        W = 16 + TT
        pbuf = [self.ar_alloc([W], F32) for _ in range(4)]
        ptmp = [self.ar_alloc([W], F32) for _ in range(2)]
        pdiff = [self.ar_alloc([TT], BF16) for _ in range(4)]
        opool = [self.ar_alloc([TT], BF16) for _ in range(4)]
        oT = self.ar_alloc([4, TT], BF16)
        ebuf = [self.ar_alloc([512], F32) for _ in range(2)]
        pt = [[self.ar_alloc([512], BF16) for _ in range(2)] for _ in range(2)]
        rden = [self.ar_alloc([512], F32) for _ in range(2)]
        self.E("dve", lambda e: e.tensor_copy(kn.ap[:, 0:128], self.kcar[:]), [self.kcar_t], kn.tks)
        self.E("dve", lambda e: e.tensor_copy(vtm.ap[:, 0, :], self.vcar[:]), [self.vcar_t], vtm.tks)
        for gi in range(4):
            self.E("dve", lambda e, gi=gi: e.tensor_copy(pbuf[gi].ap[:, 0:16], self.pcar[:, gi, :]), [self.pcar_t], pbuf[gi].tks)
        for ci in range(5):
            wb, wt = wq(ci // 2)
            n0 = (ci % 2) * 128
            ps, pst = self.bank(ci % 2)
            for dc in range(DC):
                self.E("pe", lambda e, dc=dc, n0=n0, ps=ps, wb=wb: e.matmul(
                    ps[:], wb[:, dc, n0:n0 + 128], self.hv(dc), start=(dc == 0), stop=(dc == DC - 1)),
                    [wt, self.h_t[dc]], pst)
            qr, s_ = qraw[ci % 2], sqb[ci % 2]
            ps2, ps2t = self.bank(2)
            self.E("act", lambda e, qr=qr, ps=ps: e.copy(qr.ap, ps[:]), pst, qr.tks)
            self.E("act", lambda e, s_=s_, ps=ps: e.activation(s_.ap, ps[:], AF.Square), pst, s_.tks)
            self.E("pe", lambda e, s_=s_, ps2=ps2: e.matmul(ps2[:], bd, s_.ap, start=True, stop=True),
                   s_.tks + [self.cbf_t], ps2t)
            self.E("act", lambda e, ps2=ps2: e.activation(rstd2.ap, ps2[:], AF.Sqrt,
                                                          bias=self.colvec[:, CV_EPS:CV_EPS + 1], scale=1.0 / 64),
                   ps2t + [self.colvec_t], rstd2.tks)
            self.E("dve", lambda e: e.reciprocal(rstd2.ap, rstd2.ap), rstd2.tks, rstd2.tks)
            if ci < 4:
                self.E("dve", lambda e, ci=ci, qr=qr: e.scalar_tensor_tensor(
                    qn[ci].ap, qr.ap, self.colvec[:, CV_QN:CV_QN + 1], rstd2.ap, ALU.mult, ALU.mult),
                    qr.tks + rstd2.tks + [self.colvec_t], qn[ci].tks)
            else:
                self.E("dve", lambda e, qr=qr: e.scalar_tensor_tensor(
                    kn.ap[:, 128:128 + TT], qr.ap, self.colvec[:, CV_KN:CV_KN + 1], rstd2.ap, ALU.mult, ALU.mult),
                    qr.tks + rstd2.tks + [self.colvec_t], kn.tks)
        wb, wt = wq(2)
        ps, pst = self.bank(1)
        for b in range(NB):
            for dc in range(DC):
                self.E("pe", lambda e, dc=dc, b=b, ps=ps, wb=wb: e.matmul(
                    ps[:, b * 128:(b + 1) * 128], self.hv(dc, b * 128, (b + 1) * 128), wb[:, dc, 128:256],
                    start=(dc == 0), stop=(dc == DC - 1)),
                    [wt, self.h_t[dc]], pst)
        self.E("act", lambda e, ps=ps: e.copy(vtm.ap[:, 1:NB + 1, :], ps[:].rearrange("p (b n) -> p b n", b=NB)),
               pst, vtm.tks)
        for gi in range(4):
            wb, wt = wq(3 + gi // 2)
            n0 = (gi % 2) * 128
            ps, pst = self.bank(gi % 2)
            for dc in range(DC):
                self.E("pe", lambda e, dc=dc, n0=n0, ps=ps, wb=wb: e.matmul(
                    ps[:], wb[:, dc, n0:n0 + 128], self.hv(dc), start=(dc == 0), stop=(dc == DC - 1)),
                    [wt, self.h_t[dc]], pst)
            self.E("act", lambda e, gi=gi, ps=ps: e.copy(pbuf[gi].ap[:, 16:16 + TT], ps[:]), pst, pbuf[gi].tks)
        woq = self.lazy([woutv[:, :, i * 256:(i + 1) * 256] for i in range(4)], 1)
        woq(0)
        for gi, w in enumerate((2, 4, 8, 16)):
            cur = pbuf[gi]
            lo = 0
            k = 0
            step = 1
            while step < w:
                nxt = ptmp[k % 2]
                lo += step
                self.E("dve", lambda e, cur=cur, nxt=nxt, lo=lo, step=step: e.tensor_tensor(
                    nxt.ap[:, lo:W], cur.ap[:, lo:W], cur.ap[:, lo - step:W - step], ALU.add),
                    cur.tks, nxt.tks)
                cur = nxt
                step *= 2
                k += 1
            self.E("dve", lambda e, cur=cur, gi=gi, w=w: e.scalar_tensor_tensor(
                pdiff[gi].ap, cur.ap[:, 16:W], 1.0 / w, pbuf[gi].ap[:, 16:W], ALU.mult, ALU.subtract),
                cur.tks + pbuf[gi].tks, pdiff[gi].tks)
            if ti == 0:
                tmp = qraw[0]
                self.E("dve", lambda e, cur=cur, gi=gi, tmp=tmp: e.tensor_tensor(
                    tmp.ap[:, 0:16], cur.ap[:, 16:32], self.cmat[:, CM_INVC + gi * 16:CM_INVC + gi * 16 + 16], ALU.mult),
                    cur.tks + [self.cmat_t], tmp.tks)
                self.E("dve", lambda e, gi=gi, tmp=tmp: e.tensor_tensor(
                    pdiff[gi].ap[:, 0:16], tmp.ap[:, 0:16], pbuf[gi].ap[:, 16:32], ALU.subtract),
                    tmp.tks + pbuf[gi].tks, pdiff[gi].tks)
            self.E("dve", lambda e, gi=gi: e.tensor_copy(self.pcar[:, gi, :], pbuf[gi].ap[:, TT:TT + 16]),
                   pbuf[gi].tks, [self.pcar_t])
            ps, pst = self.bank(gi % 2)
            self.E("pe", lambda e, gi=gi, ps=ps: e.matmul(ps[:], self.pw[:, gi, :], pdiff[gi].ap, start=True, stop=True),
                   [self.pw_t] + pdiff[gi].tks, pst)
            self.E("act", lambda e, gi=gi, ps=ps: e.activation(opool[gi].ap, ps[:], AF.Copy,
                                                               scale=self.colvec[:, CV_PS + gi:CV_PS + gi + 1]),
                   pst + [self.colvec_t], opool[gi].tks)
        def kbs_of(b):
            return [0, 1] if ti * NB + b > 0 else [1]

        def scores(b):
            for hk in range(2):
                p0 = hk * 64
                for kb in kbs_of(b):
                    ps, pst = self.bank(3 + 2 * hk + kb)
                    for g in range(4):
                        self.E("pe", lambda e, ps=ps, p0=p0, b=b, kb=kb, g=g: e.matmul(
                            ps[:, g * 128:(g + 1) * 128],
                            kn.ap[p0:p0 + 64, (b + kb) * 128:(b + kb + 1) * 128],
                            qn[g].ap[p0:p0 + 64, b * 128:(b + 1) * 128], start=True, stop=True),
                            kn.tks + qn[g].tks, pst)
        scores(0)
        for b in range(NB):
            kbs = kbs_of(b)
            for hk in range(2):
                ptb = pt[hk]
                for kb in kbs:
                    ps, pst = self.bank(3 + 2 * hk + kb)
                    eb = ebuf[kb]
                    self.E("act", lambda e, eb=eb, ps=ps: e.activation(eb.ap, ps[:], AF.Exp, scale=0.125), pst, eb.tks)
                    m0 = CM_AM + (hk * 2 + kb) * 512
                    self.E("dve", lambda e, eb=eb, ptb=ptb, kb=kb, m0=m0: e.tensor_tensor(
                        ptb[kb].ap, eb.ap, self.cmat[:, m0:m0 + 512], ALU.mult),
                        eb.tks + [self.cmat_t], ptb[kb].tks)
            if b + 1 < NB:
                scores(b + 1)
            for hk in range(2):
                p0 = hk * 64
                ptb = pt[hk]
                pv, pvt = self.bank(7 if hk == 0 else 0)
                den, dent = self.bank(2 if hk == 0 else 1)
                for i, kb in enumerate(kbs):
                    self.E("pe", lambda e, i=i, kb=kb, b=b, ptb=ptb, pv=pv, n=len(kbs): e.matmul(
                        pv[:, :], vtm.ap[:, b + kb, :], ptb[kb].ap, start=(i == 0), stop=(i == n - 1)),
                        vtm.tks + ptb[kb].tks, pvt)
                for i, kb in enumerate(kbs):
                    self.E("pe", lambda e, i=i, kb=kb, ptb=ptb, den=den, n=len(kbs): e.matmul(
                        den[:, :], ones, ptb[kb].ap, start=(i == 0), stop=(i == n - 1)),
                        [self.cbf_t] + ptb[kb].tks, dent)
                rd = rden[hk]
                self.E("dve", lambda e, rd=rd, den=den, p0=p0: e.tensor_tensor(
                    rd.ap[p0:p0 + 64, :], den[p0:p0 + 64, :], self.esk[p0:p0 + 64, :], ALU.add),
                    dent + [self.esk_t], rd.tks)
                self.E("dve", lambda e, rd=rd, p0=p0: e.reciprocal(rd.ap[p0:p0 + 64, :], rd.ap[p0:p0 + 64, :]), rd.tks, rd.tks)
                self.E("dve", lambda e, rd=rd, pv=pv, p0=p0, b=b: e.tensor_tensor(
                    oT.ap[p0:p0 + 64, :, b * 128:(b + 1) * 128],
                    pv[p0:p0 + 64, :].rearrange("p (g q) -> p g q", g=4),
                    rd.ap[p0:p0 + 64, :].rearrange("p (g q) -> p g q", g=4), ALU.mult),
                    pvt + rd.tks, oT.tks)
        self.E("dve", lambda e: e.tensor_copy(self.kcar[:], kn.ap[:, TT:TT + 128]), kn.tks, [self.kcar_t])
        self.E("dve", lambda e: e.tensor_copy(self.vcar[:], vtm.ap[:, NB, :]), vtm.tks, [self.vcar_t])
        for dc in range(DC):
            wb, wt = woq(dc // 2)
            n0 = (dc % 2) * 128
            ps, pst = self.bank(3 + dc % 2)
            for ch in range(4):
                self.E("pe", lambda e, ch=ch, ps=ps, wb=wb, n0=n0: e.matmul(
                    ps[:], wb[:, ch, n0:n0 + 128], oT.ap[:, ch, :], start=(ch == 0), stop=False),
                    [wt] + oT.tks, pst)
            for gi in range(4):
                self.E("pe", lambda e, gi=gi, ps=ps, wb=wb, n0=n0: e.matmul(
                    ps[:], wb[:, 4 + gi, n0:n0 + 128], opool[gi].ap, start=False, stop=(gi == 3)),
                    [wt] + opool[gi].tks, pst)
            self.E("dve", lambda e, dc=dc, ps=ps: e.tensor_tensor(self.xT[:, dc, :], ps[:], self.xT[:, dc, :], ALU.add),
                   pst + [self.xT_t[dc]], [self.xT_t[dc]])

    def setup_rwkv(self):
        c = self.c
        self.cw = {nm: self.dram_in(nm, [D, D]) for nm in ("c_w_r", "c_w_k", "c_w_v", "c_w_o")}
        w1_d = self.dram_in("c_w1", [D, 64])
        a1_d = self.dram_in("c_a1", [D, 64])
        g1_d = self.dram_in("c_g1", [D, 128])
        w2_d = self.dram_in("c_w2e", [65, D])
        a2_d = self.dram_in("c_a2e", [65, D])
        g2_d = self.dram_in("c_g2", [128, D])
        rows_d = self.dram_in("c_rows", [3, D])
        self.w1b = c.sb("w1b", [128, DC, 64], BF16)
        self.a1b = c.sb("a1b", [128, DC, 64], BF16)
        self.g1b = c.sb("g1b", [128, DC, 128], BF16)
        self.w2b = c.sb("w2b", [65, D], BF16)
        self.a2b = c.sb("a2b", [65, D], BF16)
        self.g2b = c.sb("g2b", [128, D], BF16)
        self.rowb = c.sb("rowb", [128, 3, D], F32)
        self.cw_t = Tk("cw")
        self.cw_slot = c.slot("cw", group=True)
        self.spad = c.sb("spad", [128, 16, 64], F32)
        self.sbf = c.sb("sbf", [128, 16, 64], BF16)
        self.spad_t = [Tk(f"spad{i}") for i in range(16)]
        self.sbf_t = [Tk(f"sbf{i}") for i in range(16)]
        sl = self.cw_slot
        lds = [(self.w1b[:], w1_d.rearrange("(c p) n -> p c n", p=128)),
               (self.a1b[:], a1_d.rearrange("(c p) n -> p c n", p=128)),
               (self.g1b[:], g1_d.rearrange("(c p) n -> p c n", p=128)),
               (self.w2b[:], w2_d[:, :]), (self.a2b[:], a2_d[:, :]), (self.g2b[:], g2_d[:, :])]
        for n_, (dst, src) in enumerate(lds):
            c.emit("pool", lambda e, dst=dst, src=src: e.dma_start(out=dst, in_=src),
                   writes=[self.cw_t if n_ == len(lds) - 1 else Tk("x")], slot=sl)
        self.rowb_t = Tk("rowb")
        sl2 = c.slot("cwrow", group=True)
        for i in range(3):
            c.emit("sp", lambda e, i=i: e.dma_start(out=self.rowb[:, i:i + 1, :], in_=rows_d[i:i + 1, :].partition_broadcast(128)),
                   writes=[self.rowb_t if i == 2 else Tk("x")], slot=sl2)
        self.E("dve", lambda e: e.memset(self.spad[:], 0.0), [], self.spad_t)
        self.E("dve", lambda e: e.memset(self.sbf[:], 0.0), [], self.sbf_t)

    def rwkv(self, ti):
        self.ar_reset()
        cv = self.colvec
        identb = self.cbf[:, 128:256]
        cm = self.cmat
        cwt = [self.cw_t]

        def wview(nm, q):
            return self.cw[nm].rearrange("(c p) n -> p c n", p=128)[:, :, q * 256:(q + 1) * 256]
        self.rmsnorm(CV_C)
        rb = [self.ar_alloc([D], BF16) for _ in range(NB)]
        kb_ = [self.ar_alloc([D], BF16) for _ in range(NB)]
        vb = [self.ar_alloc([D], BF16) for _ in range(NB)]
        t1e = self.ar_alloc([TT], BF16)
        a1e = self.ar_alloc([TT], BF16)
        g1T = self.ar_alloc([TT], BF16)
        yT = [self.ar_alloc([TT], BF16) for _ in range(DC)]
        mark = self.ar_ptr
        xx = [self.ar_alloc([TT], BF16) for _ in range(DC)]
        xm = [self.ar_alloc([TT], BF16) for _ in range(DC)]
        for dc in range(DC):
            self.E("dve", lambda e, dc=dc: e.tensor_tensor(xx[dc].ap, self.h[:, dc, 0:TT], self.h[:, dc, 1:TT + 1], ALU.subtract),
                   [self.h_t[dc]], xx[dc].tks)
        self.E("dve", lambda e: e.tensor_copy(self.h[:, :, 0:1], self.h[:, :, TT:TT + 1]), self.h_t, self.h_t)

        def mix(i):
            for dc in range(DC):
                self.E("dve", lambda e, dc=dc, i=i: e.scalar_tensor_tensor(
                    xm[dc].ap, xx[dc].ap, cv[:, CV_MU + i * 8 + dc:CV_MU + i * 8 + dc + 1], self.hv(dc), ALU.mult, ALU.add),
                    xx[dc].tks + [self.h_t[dc], self.colvec_t], xm[dc].tks)
        mix(1)
        ps, pst = self.bank(0)
        for dc in range(DC):
            self.E("pe", lambda e, dc=dc, ps=ps: e.matmul(ps[0:64, :], self.w1b[:, dc, :], xm[dc].ap, start=(dc == 0), stop=(dc == DC - 1)),
                   cwt + xm[dc].tks, pst)
        self.E("act", lambda e, ps=ps: e.activation(t1e.ap[0:64, :], ps[0:64, :], AF.Tanh), pst, t1e.tks)
        self.E("dve", lambda e: e.memset(t1e.ap[64:65, :], 1.0), [], t1e.tks)
        mix(4)
        ps, pst = self.bank(1)
        for dc in range(DC):
            self.E("pe", lambda e, dc=dc, ps=ps: e.matmul(ps[0:64, :], self.a1b[:, dc, :], xm[dc].ap, start=(dc == 0), stop=(dc == DC - 1)),
                   cwt + xm[dc].tks, pst)
        self.E("act", lambda e, ps=ps: e.copy(a1e.ap[0:64, :], ps[0:64, :]), pst, a1e.tks)
        self.E("dve", lambda e: e.memset(a1e.ap[64:65, :], 1.0), [], a1e.tks)
        mix(5)
        ps, pst = self.bank(0)
        for dc in range(DC):
            self.E("pe", lambda e, dc=dc, ps=ps: e.matmul(ps[:, :], self.g1b[:, dc, :], xm[dc].ap, start=(dc == 0), stop=(dc == DC - 1)),
                   cwt + xm[dc].tks, pst)
        self.E("act", lambda e, ps=ps: e.activation(g1T.ap, ps[:, :], AF.Sigmoid), pst, g1T.tks)
        hcnt = 0
        for (i, nm, dst) in ((0, "c_w_r", rb), (2, "c_w_k", kb_), (3, "c_w_v", vb)):
            mix(i)
            wl = self.lazy([wview(nm, q_) for q_ in range(4)], 1)
            for q in range(4):
                wb, wt = wl(q)
                for cb in range(NB):
                    ph, pht = self.bk(hcnt % 8, 0, 256)
                    hcnt += 1
                    for dc in range(DC):
                        self.E("pe", lambda e, dc=dc, cb=cb, ph=ph, wb=wb: e.matmul(
                            ph, xm[dc].ap[:, cb * 128:(cb + 1) * 128], wb[:, dc, :], start=(dc == 0), stop=(dc == DC - 1)),
                            [wt] + xm[dc].tks, pht)
                    if hcnt % 2 == 0:
                        self.E("act", lambda e, cb=cb, q=q, ph=ph, dst=dst: e.copy(dst[cb].ap[:, q * 256:(q + 1) * 256], ph), pht, dst[cb].tks)
                    else:
                        self.E("dve", lambda e, cb=cb, q=q, ph=ph, dst=dst: e.tensor_copy(dst[cb].ap[:, q * 256:(q + 1) * 256], ph), pht, dst[cb].tks)
        if RWDBG <= 1:
            return
        for cb in range(NB):
            self.ar_ptr = mark
            self.rwkv_chunk(ti, cb, rb[cb], kb_[cb], vb[cb], t1e, a1e, g1T, yT)
        wl = self.lazy([wview("c_w_o", q_) for q_ in range(4)], 1)
        for q in range(4):
            wb, wt = wl(q)
            for nn in range(2):
                dco = q * 2 + nn
                ps, pst = self.bank(dco % 2)
                for blk in range(DC):
                    self.E("pe", lambda e, blk=blk, nn=nn, ps=ps, wb=wb: e.matmul(
                        ps[:], wb[:, blk, nn * 128:(nn + 1) * 128], yT[blk].ap, start=(blk == 0), stop=(blk == DC - 1)),
                        [wt] + yT[blk].tks, pst)
                self.E("dve", lambda e, dco=dco, ps=ps: e.tensor_tensor(self.xT[:, dco, :], ps[:], self.xT[:, dco, :], ALU.add),
                       pst + [self.xT_t[dco]], [self.xT_t[dco]])

    def rwkv_chunk(self, ti, cb, rb, kb_, vb, t1e, a1e, g1T, yT):
        cv, cm = self.colvec, self.cmat
        identb = self.cbf[:, 128:256]
        cwt = [self.cw_t]
        cs = slice(cb * 128, (cb + 1) * 128)
        A = self.ar_alloc
        sg = A([D], F32)
        bb = A([D], F32)
        a_ = A([D], BF16)
        at, rt, kt, bt, kh, bh, bv, zb = [A([D], BF16) for _ in range(8)]
        sm = A([8, 16], F32)
        sm2 = A([1, 16], F32)
        fmT = A([DC, 4, 128], BF16)
        tmpf = [A([128], F32) for _ in range(2)]
        mark2 = self.ar_ptr
        ex = A([D], F32)
        en = A([D], F32)
        kkb = A([D], F32)
        km = A([D], F32)
        y = sg
        v3 = lambda b_: b_.ap.rearrange("p (h j) -> p h j", h=16)
        bc = lambda i: sm.ap[:, i, :].unsqueeze(2).to_broadcast([128, 16, 64])
        for hf in range(2):
            hs_ = slice(hf * 512, (hf + 1) * 512)
            ps, pst = self.bank(hf)
            self.E("pe", lambda e, ps=ps, hs_=hs_: e.matmul(ps[:], t1e.ap[0:65, cs], self.w2b[0:65, hs_], start=True, stop=True),
                   t1e.tks + cwt, pst)
            self.E("act", lambda e, ps=ps, hs_=hs_: e.activation(sg.ap[:, hs_], ps[:], AF.Sigmoid), pst, sg.tks)
            ps, pst = self.bank(2 + hf)
            self.E("pe", lambda e, ps=ps, hs_=hs_: e.matmul(ps[:], a1e.ap[0:65, cs], self.a2b[0:65, hs_], start=True, stop=True),
                   a1e.tks + cwt, pst)
            self.E("act", lambda e, ps=ps, hs_=hs_: e.activation(a_.ap[:, hs_], ps[:], AF.Sigmoid), pst, a_.tks)
        if RWDBG <= 2:
            return
        if RWDBG <= 3:
            return
        self.E("dve", lambda e: e.tensor_tensor(kkb.ap, kb_.ap, self.rowb[:, 0, :], ALU.mult), kb_.tks + [self.rowb_t], kkb.tks)
        self.E("dve", lambda e: e.tensor_tensor(bb.ap, kkb.ap, kkb.ap, ALU.mult), kkb.tks, bb.tks)
        self.E("dve", lambda e: e.tensor_reduce(sm.ap[:, 0, :], v3(bb), AX.X, ALU.add), bb.tks, sm.tks)
        self.E("dve", lambda e: e.tensor_scalar(sm.ap[:, 0, :], sm.ap[:, 0, :], 1e-24, None, ALU.max), sm.tks, sm.tks)
        self.E("act", lambda e: e.activation(sm.ap[:, 0, :], sm.ap[:, 0, :], AF.Sqrt), sm.tks, sm.tks)
        self.E("dve", lambda e: e.reciprocal(sm.ap[:, 0, :], sm.ap[:, 0, :]), sm.tks, sm.tks)
        self.E("dve", lambda e: e.tensor_tensor(v3(kkb), v3(kkb), bc(0), ALU.mult), kkb.tks + sm.tks, kkb.tks)
        self.E("dve", lambda e: e.scalar_tensor_tensor(km.ap, a_.ap, 1.0, self.rowb[:, 1, :], ALU.subtract, ALU.mult),
               a_.tks + [self.rowb_t], km.tks)
        self.E("dve", lambda e: e.scalar_tensor_tensor(km.ap, km.ap, 1.0, kb_.ap, ALU.add, ALU.mult), km.tks + kb_.tks, km.tks)
        self.E("dve", lambda e: e.tensor_tensor(bb.ap, kkb.ap, a_.ap, ALU.mult), kkb.tks + a_.tks, bb.tks)
        tri = self.cbf[:, 384:512]
        onesb = self.cbf[:, 0:128]
        sgh, sgl = kh, bh
        self.E("act", lambda e: e.copy(sgh.ap, sg.ap), sg.tks, sgh.tks)
        self.E("dve", lambda e: e.tensor_tensor(sgl.ap, sg.ap, sgh.ap, ALU.subtract), sg.tks + sgh.tks, sgl.tks)
        pcs = []
        for hf in range(2):
            hs_ = slice(hf * 512, (hf + 1) * 512)
            pc, pct = self.bank(4 + hf)
            self.E("pe", lambda e, pc=pc, hs_=hs_: e.matmul(pc[:], tri, sgh.ap[:, hs_], start=True, stop=False), sgh.tks + [self.cbf_t], pct)
            self.E("pe", lambda e, pc=pc, hs_=hs_: e.matmul(pc[:], tri, sgl.ap[:, hs_], start=False, stop=True), sgl.tks + [self.cbf_t], pct)
            pC, pCt = self.bank(6 + hf)
            self.E("pe", lambda e, pC=pC, hs_=hs_: e.matmul(pC[:], onesb, sgh.ap[:, hs_], start=True, stop=False), sgh.tks + [self.cbf_t], pCt)
            self.E("pe", lambda e, pC=pC, hs_=hs_: e.matmul(pC[:], onesb, sgl.ap[:, hs_], start=False, stop=True), sgl.tks + [self.cbf_t], pCt)
            pcs.append((pc, pct, pC, pCt))
        pw_, pwt = self.bk(0, 0, 128)
        for blk in range(DC):
            self.E("pe", lambda e, blk=blk: e.matmul(pw_[:, blk * 16:(blk + 1) * 16], sgh.ap[:, blk * 128:(blk + 1) * 128],
                                                     onesb[:, 0:16], start=True, stop=False), sgh.tks + [self.cbf_t], pwt)
            self.E("pe", lambda e, blk=blk: e.matmul(pw_[:, blk * 16:(blk + 1) * 16], sgl.ap[:, blk * 128:(blk + 1) * 128],
                                                     onesb[:, 0:16], start=False, stop=True), sgl.tks + [self.cbf_t], pwt)
        self.E("act", lambda e: e.activation(sm.ap[:, 7, 0:8], pw_.rearrange("p (b n) -> p b n", n=16)[:, :, 0], AF.Exp, scale=NEG_E),
               pwt, sm.tks)
        for hf in range(2):
            hs_ = slice(hf * 512, (hf + 1) * 512)
            pc, pct, pC, pCt = pcs[hf]
            self.E("act", lambda e, pc=pc, hs_=hs_: e.activation(ex.ap[:, hs_], pc[:], AF.Exp, scale=NEG_E), pct, ex.tks)
            self.E("act", lambda e, pc=pc, hs_=hs_: e.activation(en.ap[:, hs_], pc[:], AF.Exp, scale=-NEG_E), pct, en.tks)
            self.E("dve", lambda e, pc=pc, hs_=hs_: e.scalar_tensor_tensor(
                sg.ap[:, hs_], sg.ap[:, hs_], -1.0, pc[:], ALU.mult, ALU.add), sg.tks + pct, sg.tks)
        self.E("act", lambda e: e.activation(sg.ap, sg.ap, AF.Exp, scale=NEG_E), sg.tks, sg.tks)
        self.E("dve", lambda e: e.scalar_tensor_tensor(at.ap, kkb.ap, -1.0, sg.ap, ALU.mult, ALU.mult), kkb.tks + sg.tks, at.tks)
        self.E("dve", lambda e: e.tensor_tensor(rt.ap, rb.ap, ex.ap, ALU.mult), rb.tks + ex.tks, rt.tks)
        self.E("dve", lambda e: e.tensor_tensor(kt.ap, km.ap, en.ap, ALU.mult), km.tks + en.tks, kt.tks)
        self.E("dve", lambda e: e.tensor_tensor(bt.ap, bb.ap, en.ap, ALU.mult), bb.tks + en.tks, bt.tks)
        for hf in range(2):
            hs_ = slice(hf * 512, (hf + 1) * 512)
            pc, pct, pC, pCt = pcs[hf]
            self.E("act", lambda e, pC=pC, hs_=hs_: e.activation(ex.ap[:, hs_], pC[:], AF.Exp, scale=NEG_E), pCt, ex.tks)
        if RWDBG <= 4:
            return
        for j, src in enumerate((at, rt, bt, kt)):
            for hf in range(2):
                ph, pht = self.bank((2 * j + hf) % 8)
                for b4 in range(4):
                    blk = hf * 4 + b4
                    self.E("pe", lambda e, src=src, blk=blk, b4=b4, ph=ph: e.matmul(
                        ph[:, b4 * 128:(b4 + 1) * 128], src.ap[:, blk * 128:(blk + 1) * 128], identb, start=True, stop=True),
                        src.tks + [self.cbf_t], pht)
                if (2 * j + hf) % 2 == 0:
                    self.E("act", lambda e, j=j, hf=hf, ph=ph: e.copy(fmT.ap[:, hf * 4:hf * 4 + 4, j, :], ph[:].rearrange("p (b t) -> p b t", b=4)), pht, fmT.tks)
                else:
                    self.E("dve", lambda e, j=j, hf=hf, ph=ph: e.tensor_copy(fmT.ap[:, hf * 4:hf * 4 + 4, j, :], ph[:].rearrange("p (b t) -> p b t", b=4)), pht, fmT.tks)
        self.E("dve", lambda e: e.tensor_tensor(kh.ap, kt.ap, ex.ap, ALU.mult), kt.tks + ex.tks, kh.tks)
        self.E("dve", lambda e: e.tensor_tensor(bh.ap, bt.ap, ex.ap, ALU.mult), bt.tks + ex.tks, bh.tks)
        self.E("pool", lambda e: e.tensor_tensor(bb.ap, rb.ap, km.ap, ALU.mult), rb.tks + km.tks, bb.tks)
        self.E("pool", lambda e: e.tensor_tensor(bb.ap, bb.ap, self.rowb[:, 2, :], ALU.mult), bb.tks + [self.rowb_t], bb.tks)
        self.E("dve", lambda e: e.tensor_reduce(sm2.ap[:, 0, :], v3(bb), AX.X, ALU.add), bb.tks, sm2.tks)
        self.E("pool", lambda e: e.tensor_tensor(v3(bv), v3(vb), sm2.ap[:, 0, :].unsqueeze(2).to_broadcast([128, 16, 64]), ALU.mult),
               vb.tks + sm2.tks, bv.tks)
        if RWDBG <= 5:
            return
        self.ar_ptr = mark2
        Wb = [A([768], BF16) for _ in range(8)]
        tinv = [A([128], BF16) for _ in range(8)]
        pm = [A([256], BF16) for _ in range(8)]
        ut = [A([64], BF16) for _ in range(8)]

        def evac(hh, out, in_, reads, writes):
            if hh % 2 == 0:
                self.E("act", lambda e: e.copy(out, in_), reads, writes)
            else:
                self.E("dve", lambda e: e.tensor_copy(out, in_), reads, writes)
        for grp in range(2):
            hs = [grp * 8 + hh for hh in range(8)]
            for hh, h in enumerate(hs):
                blk, p0 = h // 2, (h % 2) * 64
                f = lambda j, n=1, blk=blk, p0=p0: fmT.ap[p0:p0 + 64, blk, j:j + n, :]
                W = Wb[hh]
                pb, pbt = self.bank(hh)
                self.E("pe", lambda e, pb=pb, f=f: e.matmul(pb[:, 0:256].rearrange("p (a t) -> p a t", a=2), f(2)[:, 0, :], f(0, 2), start=True, stop=True),
                       fmT.tks, pbt)
                self.E("pe", lambda e, pb=pb, f=f: e.matmul(pb[:, 256:384], f(3)[:, 0, :], f(1)[:, 0, :], start=True, stop=True),
                       fmT.tks, pbt)
                self.E("dve", lambda e, pb=pb, W=W: e.tensor_tensor(W.ap[:, 0:128], pb[:, 0:128], cm[:, CM_SU:CM_SU + 128], ALU.mult),
                       pbt + [self.cmat_t], W.tks)
                self.E("dve", lambda e, pb=pb, W=W: e.tensor_tensor(W.ap[:, 512:768], pb[:, 128:384], cm[:, CM_UI:CM_UI + 256], ALU.mult),
                       pbt + [self.cmat_t], W.tks)
                self.E("act", lambda e, W=W: e.copy(W.ap[:, 128:256], identb), [self.cbf_t], W.tks)
            for hh, h in enumerate(hs):
                blk, p0 = h // 2, (h % 2) * 64
                f = lambda j, n=1, blk=blk, p0=p0: fmT.ap[p0:p0 + 64, blk, j:j + n, :]
                W = Wb[hh]
                pb, pbt = self.bank(hh)
                self.E("pe", lambda e, pb=pb, f=f: e.matmul(pb[:, 0:256].rearrange("p (a t) -> p a t", a=2), f(0)[:, 0, :], f(2, 2), start=True, stop=True),
                       fmT.tks, pbt)
                self.E("dve", lambda e, pb=pb, W=W: e.tensor_tensor(W.ap[:, 256:512], pb[:, 0:256], cm[:, CM_SL:CM_SL + 256], ALU.mult),
                       pbt + [self.cmat_t], W.tks)
            for k in range(6):
                for hh in range(8):
                    W = Wb[hh]
                    X, P, XT = W.ap[:, 0:128], W.ap[:, 128:256], W.ap[:, 256:384]
                    pb, pbt = self.bank(hh)
                    self.E("pe", lambda e, pb=pb, X=X, XT=XT: e.matmul(pb[:, 0:128], XT, X, start=True, stop=True), W.tks, pbt)
                    self.E("pe", lambda e, pb=pb, P=P, XT=XT: e.matmul(pb[:, 128:256], XT, P, start=True, stop=False), W.tks, pbt)
                    self.E("pe", lambda e, pb=pb, P=P: e.matmul(pb[:, 128:256], identb, P, start=False, stop=True), W.tks + [self.cbf_t], pbt)
                    self.E("pe", lambda e, pb=pb, X=X, XT=XT: e.matmul(pb[:, 256:384], X, XT, start=True, stop=True), W.tks, pbt)
                    evac(hh, W.ap[:, 0:384], pb[:, 0:384], pbt, W.tks)
            for hh in range(8):
                W = Wb[hh]
                P, XT = W.ap[:, 128:256], W.ap[:, 256:384]
                pb, pbt = self.bank(hh)
                self.E("pe", lambda e, pb=pb, P=P, XT=XT: e.matmul(pb[:, 0:128], XT, P, start=True, stop=False), W.tks, pbt)
                self.E("pe", lambda e, pb=pb, P=P: e.matmul(pb[:, 0:128], identb, P, start=False, stop=True), W.tks + [self.cbf_t], pbt)
                evac(hh, tinv[hh].ap, pb[:, 0:128], pbt, tinv[hh].tks)
            for hh, h in enumerate(hs):
                blk = h // 2
                W = Wb[hh]
                pb, pbt = self.bank(hh)
                self.E("pe", lambda e, pb=pb, hh=hh, blk=blk: e.matmul(pb[:, 0:128], at.ap[:, blk * 128:(blk + 1) * 128], tinv[hh].ap, start=True, stop=True),
                       at.tks + tinv[hh].tks, pbt)
                self.E("pe", lambda e, pb=pb, hh=hh, W=W: e.matmul(pb[:, 128:256], W.ap[:, 384:512], tinv[hh].ap, start=True, stop=True),
                       W.tks + tinv[hh].tks, pbt)
                evac(hh, pm[hh].ap, pb[:, 0:256], pbt, pm[hh].tks)
            for hh, h in enumerate(hs):
                pb, pbt = self.bank(hh)
                self.E("pe", lambda e, pb=pb, hh=hh, h=h: e.matmul(pb[:, 0:64], pm[hh].ap[:, 0:128], self.sbf[:, h, :], start=True, stop=False),
                       pm[hh].tks + [self.sbf_t[h]], pbt)
                self.E("pe", lambda e, pb=pb, hh=hh, h=h: e.matmul(pb[:, 0:64], pm[hh].ap[:, 128:256], vb.ap[:, h * 64:(h + 1) * 64], start=False, stop=True),
                       pm[hh].tks + vb.tks, pbt)
                evac(hh, ut[hh].ap, pb[:, 0:64], pbt, ut[hh].tks)
            for hh, h in enumerate(hs):
                blk, p0 = h // 2, (h % 2) * 64
                W = Wb[hh]
                pb, pbt = self.bank(hh)
                self.E("pe", lambda e, pb=pb, h=h, blk=blk: e.matmul(pb[:, 0:64], fmT.ap[:, blk, 1, :], self.sbf[:, h, :], start=True, stop=False),
                       fmT.tks + [self.sbf_t[h]], pbt)
                self.E("pe", lambda e, pb=pb, hh=hh, W=W: e.matmul(pb[:, 0:64], W.ap[:, 512:640], ut[hh].ap, start=False, stop=False),
                       W.tks + ut[hh].tks, pbt)
                self.E("pe", lambda e, pb=pb, W=W, h=h: e.matmul(pb[:, 0:64], W.ap[:, 640:768], vb.ap[:, h * 64:(h + 1) * 64], start=False, stop=True),
                       W.tks + vb.tks, pbt)
                self.E("pe", lambda e, pb=pb, hh=hh, blk=blk: e.matmul(pb[:, 64:128], bh.ap[:, blk * 128:(blk + 1) * 128], ut[hh].ap, start=True, stop=False),
                       bh.tks + ut[hh].tks, pbt)
                self.E("pe", lambda e, pb=pb, h=h, blk=blk: e.matmul(pb[:, 64:128], kh.ap[:, blk * 128:(blk + 1) * 128], vb.ap[:, h * 64:(h + 1) * 64], start=False, stop=True),
                       kh.tks + vb.tks, pbt)
                evac(hh, y.ap[:, h * 64:(h + 1) * 64], pb[:, 0:64], pbt, y.tks)
                self.E("dve", lambda e, pb=pb, h=h, blk=blk, p0=p0: e.scalar_tensor_tensor(
                    self.spad[p0:p0 + 64, h, :], self.spad[p0:p0 + 64, h, :], sm.ap[p0:p0 + 64, 7, blk:blk + 1], pb[p0:p0 + 64, 64:128],
                    ALU.mult, ALU.add), [self.spad_t[h]] + sm.tks + pbt, [self.spad_t[h]])
                self.E("act", lambda e, h=h, p0=p0: e.copy(self.sbf[p0:p0 + 64, h, :], self.spad[p0:p0 + 64, h, :]),
                       [self.spad_t[h]], [self.sbf_t[h]])
        if RWDBG <= 6:
            return
        self.E("dve", lambda e: e.tensor_tensor(bb.ap, y.ap, y.ap, ALU.mult), y.tks, bb.tks)
        self.E("dve", lambda e: e.tensor_reduce(sm.ap[:, 2, :], v3(y), AX.X, ALU.add), y.tks, sm.tks)
        self.E("dve", lambda e: e.tensor_reduce(sm.ap[:, 3, :], v3(bb), AX.X, ALU.add), bb.tks, sm.tks)
        self.E("dve", lambda e: e.tensor_scalar(sm.ap[:, 2, :], sm.ap[:, 2, :], 1.0 / 64, None, ALU.mult), sm.tks, sm.tks)
        self.E("dve", lambda e: e.tensor_tensor(sm.ap[:, 4, :], sm.ap[:, 2, :], sm.ap[:, 2, :], ALU.mult), sm.tks, sm.tks)
        self.E("dve", lambda e: e.scalar_tensor_tensor(sm.ap[:, 3, :], sm.ap[:, 3, :], 1.0 / 64, sm.ap[:, 4, :], ALU.mult, ALU.subtract),
               sm.tks, sm.tks)
        self.E("act", lambda e: e.activation(sm.ap[:, 3, :], sm.ap[:, 3, :], AF.Sqrt, bias=cv[:, CV_GNEPS:CV_GNEPS + 1]),
               sm.tks + [self.colvec_t], sm.tks)
        self.E("dve", lambda e: e.reciprocal(sm.ap[:, 3, :], sm.ap[:, 3, :]), sm.tks, sm.tks)
        self.E("dve", lambda e: e.tensor_tensor(v3(y), v3(y), bc(2), ALU.subtract), y.tks + sm.tks, y.tks)
        self.E("dve", lambda e: e.tensor_tensor(v3(zb), v3(y), bc(3), ALU.mult), y.tks + sm.tks, zb.tks)
        for blk in range(DC):
            phb, pht = self.bk(blk % 4, 0, 256)
            self.E("pe", lambda e, blk=blk, phb=phb: e.matmul(phb[:, 0:128], zb.ap[:, blk * 128:(blk + 1) * 128], identb, start=True, stop=True),
                   zb.tks + [self.cbf_t], pht)
            self.E("pe", lambda e, blk=blk, phb=phb: e.matmul(phb[:, 128:256], bv.ap[:, blk * 128:(blk + 1) * 128], identb, start=True, stop=True),
                   bv.tks + [self.cbf_t], pht)
            pg, pgt = self.bk(4 + blk % 4, 0, 256)
            self.E("pe", lambda e, blk=blk, pg=pg: e.matmul(pg[:, 0:128], self.g2b[:, blk * 128:(blk + 1) * 128], g1T.ap[:, cs], start=True, stop=True),
                   [self.cw_t] + g1T.tks, pgt)
            tf = tmpf[blk % 2]
            self.E("dve", lambda e, blk=blk, phb=phb, tf=tf: e.tensor_scalar(
                tf.ap, phb[:, 0:128], cv[:, CV_LNW + blk:CV_LNW + blk + 1], cv[:, CV_LNB + blk:CV_LNB + blk + 1], ALU.mult, ALU.add),
                pht + [self.colvec_t], tf.tks)
            self.E("dve", lambda e, phb=phb, tf=tf: e.tensor_tensor(tf.ap, tf.ap, phb[:, 128:256], ALU.add), pht + tf.tks, tf.tks)
            self.E("dve", lambda e, blk=blk, pg=pg, tf=tf: e.tensor_tensor(yT[blk].ap[:, cs], tf.ap, pg[:, 0:128], ALU.mult),
                   tf.tks + pgt, yT[blk].tks)


    def run_stage(self, ti, s):
        if s.startswith("ffn"):
            self.ffn(int(s[3:]))
        elif s == "ab":
            self.ab(ti)
        elif s == "rwkv":
            self.rwkv(ti)
        elif s == "dbgnorm":
            self.ar_reset()
            self.rmsnorm(0)
            for dc in range(DC):
                self.E("dve", lambda e, dc=dc: e.tensor_copy(self.xT[:, dc, :], self.hv(dc)),
                       [self.h_t[dc]], [self.xT_t[dc]])


_CACHE = {}


def get_program(T, stages):
    key = (T, tuple(stages))
    if key not in _CACHE:
        b = Builder(T, list(stages))
        nc = b.build()
        _CACHE[key] = (nc, b)
    return _CACHE[key]


ALL_STAGES = ["ffn0", "ab", "ffn1", "ffn2", "rwkv", "ffn3"]


def make_in_maps(inp, T, stages, ncores):
    x = np.asarray(inp["x"], np.float32)
    f32 = lambda a: np.ascontiguousarray(np.asarray(a, np.float32))
    common = {"colvec": host_colvec(inp), "cmat": host_consts()}
    if any(s_.startswith("ffn") for s_ in stages):
        common["wg"] = f32(inp["ffn_w_gate"]).reshape(4, D, DFF)
        common["wu"] = f32(inp["ffn_w_up"]).reshape(4, D, DFF)
        common["wd"] = f32(inp["ffn_w_down"]).reshape(4, DFF, D)
    if "ab" in stages:
        w_in = f32(inp["ab_w_in"]).reshape(D, 1280)
        qcols = w_in[:, :512].reshape(D, 2, 4, 64).transpose(0, 2, 1, 3).reshape(D, 512)
        common["w_in"] = np.ascontiguousarray(np.concatenate([qcols, w_in[:, 512:]], axis=1))
        w_out = f32(inp["ab_w_out"]).reshape(D, D)
        arows = w_out[:512].reshape(2, 4, 64, D).transpose(1, 0, 2, 3).reshape(512, D)
        common["w_out"] = np.ascontiguousarray(np.concatenate([arows, w_out[512:]], axis=0))
        common["pool_w"] = f32(inp["pool_w"]).reshape(4, 128, 128)
    if "rwkv" in stages:
        for nm in ("c_w_r", "c_w_k", "c_w_v", "c_w_o"):
            common[nm] = f32(inp[nm]).reshape(D, D)
        common["c_w1"] = f32(inp["c_w1"]).reshape(D, 64)
        common["c_a1"] = f32(inp["c_a1"]).reshape(D, 64)
        common["c_g1"] = f32(inp["c_g1"]).reshape(D, 128)
        common["c_w2e"] = np.ascontiguousarray(np.concatenate([f32(inp["c_w2"]).reshape(64, D), f32(inp["c_w0"]).reshape(1, D)], axis=0))
        common["c_a2e"] = np.ascontiguousarray(np.concatenate([f32(inp["c_a2"]).reshape(64, D), f32(inp["c_a0"]).reshape(1, D)], axis=0))
        common["c_g2"] = f32(inp["c_g2"]).reshape(128, D)
        common["c_rows"] = np.ascontiguousarray(np.stack([f32(inp["c_k_k"]).reshape(D), f32(inp["c_k_a"]).reshape(D),
                                                          f32(inp["c_r_k"]).reshape(D)], axis=0))
    in_maps = []
    for ci in range(ncores):
        m = dict(common)
        m["x"] = np.ascontiguousarray(x[ci, :T])
        in_maps.append(m)
    return in_maps


def run(inp, T=SEQ, stages=ALL_STAGES, ncores=8, trace=False):
    nc, b = get_program(T, stages)
    in_maps = make_in_maps(inp, T, stages, ncores)
    res = run_bass_kernel_spmd(nc, in_maps, core_ids=list(range(ncores)), trace=trace)
    out = np.stack([np.asarray(r["y"]) for r in res.results], axis=0)
    return out, res


def kernel(**inputs):
    out, _ = run(inputs)
    return out.astype(np.float32)
```

```python
import contextlib
import os
import numpy as np
import concourse.bass as bass
import concourse.mybir as mybir
from concourse.bass_utils import run_bass_kernel_spmd

F32 = mybir.dt.float32
BF16 = mybir.dt.bfloat16
ALU = mybir.AluOpType
AF = mybir.ActivationFunctionType
AX = mybir.AxisListType

D = 1024
DC = 8
DFF = 2816
FC = 22
TT = 512
NB = TT // 128
SEQ = 8192
RMS_EPS = 1e-6
GN_EPS = 64e-5
NEG_E = -float(np.exp(-0.5))
ENGS = ["pe", "act", "dve", "pool", "sp"]
SEM_CH = int(os.environ.get("SEM_CH", "12000"))
ARENA_KB = 98
RWDBG = float(os.environ.get('RWDBG', '9'))
NG_RING = 6


class Tk:
    __slots__ = ("name", "w", "r", "excl")

    def __init__(self, name, excl=False):
        self.name = name
        self.w = {}
        self.r = {}
        self.excl = excl


class Buf:
    __slots__ = ("ap", "tks")

    def __init__(self, ap, tks):
        self.ap = ap
        self.tks = tks


class Slot:
    def __init__(self, key, group=False):
        self.key = key
        self.count = 0
        self.sem = None
        self.group = group


class Op:
    __slots__ = ("eng", "fn", "deps", "tok", "slot", "snap", "waits", "signal", "sig")

    def __init__(self, eng, fn, deps, tok, slot):
        self.eng = eng
        self.fn = fn
        self.deps = deps
        self.tok = tok
        self.slot = slot
        self.snap = None
        self.waits = ()
        self.signal = False
        self.sig = None


class Ctx:
    def __init__(self, nc):
        self.nc = nc
        self.oplist = []
        self.eng_ops = {e: [] for e in ENGS}
        self.tokmap = {}
        self.slots = []
        self.es = contextlib.ExitStack()
        self.out_slots = []

    def sb(self, name, shape, dt):
        return self.es.enter_context(self.nc.sbuf_tensor("sb_" + name, list(shape), dt))

    def ps(self, name, shape, dt=F32):
        return self.es.enter_context(self.nc.psum_tensor("ps_" + name, list(shape), dt))

    def slot(self, name, group=False):
        s = Slot(("dma", name), group)
        self.slots.append(s)
        return s

    def emit(self, eng, fn, reads=(), writes=(), slot=None):
        if any(t.excl for t in reads):
            writes = list(writes) + [t for t in reads if t.excl]
            reads = [t for t in reads if not t.excl]
        deps = {}
        for t in reads:
            for k, v in t.w.items():
                if deps.get(k, -1) < v:
                    deps[k] = v
        for t in writes:
            for k, v in t.w.items():
                if deps.get(k, -1) < v:
                    deps[k] = v
            for k, v in t.r.items():
                if deps.get(k, -1) < v:
                    deps[k] = v
        if eng == "pe":
            deps.pop("pe", None)
        if slot is not None:
            slot.count += 1
            tok = (slot.key, slot.count)
        else:
            tok = (eng, len(self.eng_ops[eng]))
        op = Op(eng, fn, deps, tok, slot)
        self.oplist.append(op)
        self.eng_ops[eng].append(op)
        self.tokmap[tok] = op
        k, v = tok
        for t in reads:
            if t.r.get(k, -1) < v:
                t.r[k] = v
        for t in writes:
            t.w = {k: v}
            t.r = {}
        return tok

    def finalize(self):
        nc = self.nc
        known = {e: {} for e in ENGS}
        for op in self.oplist:
            kn = known[op.eng]
            waits = []
            copied = False
            for k, v in op.deps.items():
                if kn.get(k, -1) >= v:
                    continue
                if not copied:
                    kn = dict(kn)
                    known[op.eng] = kn
                    copied = True
                waits.append((k, v))
                prod = self.tokmap[(k, v)]
                prod.signal = True
                for kk, vv in prod.snap.items():
                    if kn.get(kk, -1) < vv:
                        kn[kk] = vv
                kn[k] = v
            op.waits = waits
            op.snap = kn
        self.eng_sems = {e: [] for e in ENGS}
        for e in ENGS:
            n = 0
            for op in self.eng_ops[e]:
                if op.slot is None and op.signal:
                    op.sig = n
                    n += 1
            nch = (n + SEM_CH - 1) // SEM_CH
            for c in range(nch):
                self.eng_sems[e].append(self.es.enter_context(nc.semaphore(f"s_{e}_{c}")))
        for s in self.slots:
            if s.count > 0:
                s.sem = self.es.enter_context(nc.semaphore("d_" + s.key[1]))

        def resolve(k, v):
            if isinstance(k, tuple):
                s = self.tokmap[(k, v)].slot
                return s.sem, 16 * (s.count if s.group else v)
            sig = self.tokmap[(k, v)].sig
            return self.eng_sems[k][sig // SEM_CH], sig % SEM_CH + 1

        ctx = self

        def run_engine(ename):
            def body(e):
                for op in ctx.eng_ops[ename]:
                    ws = [resolve(k, v) for (k, v) in op.waits]
                    if op.slot is not None:
                        for (s, v) in ws:
                            e.wait_ge(s, v)
                        ins = op.fn(e)
                        ins.then_inc(op.slot.sem, 16)
                    else:
                        for (s, v) in ws[:-1]:
                            e.wait_ge(s, v)
                        ins = op.fn(e)
                        if ws:
                            ins._wait_ge(*ws[-1])
                        if op.signal:
                            ins.then_inc(ctx.eng_sems[ename][op.sig // SEM_CH], 1)
                if ename == "sp":
                    for s in ctx.out_slots:
                        if s.count:
                            e.wait_ge(s.sem, 16 * s.count)
            return body

        with nc.Block() as block:
            block.tensor(run_engine("pe"))
            block.scalar(run_engine("act"))
            block.vector(run_engine("dve"))
            block.gpsimd(run_engine("pool"))
            block.sync(run_engine("sp"))
        self.stats = {e: len(self.eng_ops[e]) for e in ENGS}
        self.stats["waits"] = sum(len(op.waits) for op in self.oplist)
        self.stats["sems"] = sum(len(v) for v in self.eng_sems.values()) + sum(1 for s in self.slots if s.count)


CV_FFN = 0
CV_AB = 32
CV_C = 40
CV_QN = 48
CV_KN = 49
CV_PS = 50
CV_MU = 54
CV_SINK = 102
CV_LNW = 110
CV_LNB = 118
CV_GNEPS = 126
CV_EPS = 127
NCOL = 128
CM_IDENT = 0
CM_ONES = 128
CM_BD = 256
CM_AM = 384
CM_INVC = CM_AM + 2048
CM_TRIS = CM_INVC + 64
CM_ONESS = CM_TRIS + 128
CM_SU = CM_ONESS + 128
CM_UI = CM_SU + 128
CM_SL = CM_UI + 256
NCMAT = CM_SL + 256


def host_consts():
    cm = np.zeros((128, NCMAT), np.float32)
    cm[:, CM_IDENT:CM_IDENT + 128] = np.eye(128, dtype=np.float32)
    cm[:, CM_ONES:CM_ONES + 128] = 1.0
    cm[0:64, CM_BD:CM_BD + 64] = 1.0
    cm[64:128, CM_BD + 64:CM_BD + 128] = 1.0
    kk = np.arange(128)[:, None].astype(np.float64)
    qq = np.arange(128)[None, :].astype(np.float64)
    am = np.zeros((128, 2, 2, 4, 128), np.float64)
    for hk in range(2):
        for g in range(4):
            hq = hk * 4 + g
            slope = 2.0 ** (-8.0 * (hq + 1) / 8.0)
            dist_cur = qq - kk
            am[:, hk, 1, g, :] = np.where(dist_cur >= 0, np.exp(-slope * dist_cur), 0.0)
            dist_prev = qq - kk + 128
            am[:, hk, 0, g, :] = np.where(dist_prev < 128, np.exp(-slope * dist_prev), 0.0)
    cm[:, CM_AM:CM_AM + 2048] = am.reshape(128, 2048).astype(np.float32)
    invc = np.zeros((4, 16), np.float64)
    for gi, w in enumerate((2, 4, 8, 16)):
        invc[gi] = 1.0 / np.minimum(np.arange(1, 17), w)
    cm[:, CM_INVC:CM_INVC + 64] = invc.reshape(1, 64).astype(np.float32)
    s_ = np.arange(128)
    cm[:, CM_TRIS:CM_TRIS + 128] = NEG_E * (s_[:, None] <= s_[None, :]).astype(np.float32)
    cm[:, CM_ONESS:CM_ONESS + 128] = NEG_E
    cm[:, CM_SU:CM_SU + 128] = (s_[:, None] < s_[None, :]).astype(np.float32)
    cm[:, CM_UI:CM_UI + 128] = (s_[:, None] <= s_[None, :]).astype(np.float32)
    cm[:, CM_UI + 128:CM_UI + 256] = (s_[:, None] <= s_[None, :]).astype(np.float32)
    sl = (s_[:, None] > s_[None, :]).astype(np.float32)
    cm[:, CM_SL:CM_SL + 128] = sl
    cm[:, CM_SL + 128:CM_SL + 256] = sl
    return cm


def host_colvec(inp):
    cv = np.zeros((128, NCOL), np.float32)
    cv[:, CV_EPS] = RMS_EPS
    cv[:, CV_GNEPS] = GN_EPS
    fn = np.asarray(inp["ffn_norm"], np.float32).reshape(4, DC, 128)
    for li in range(4):
        cv[:, CV_FFN + li * 8:CV_FFN + li * 8 + 8] = fn[li].T
    cv[:, CV_AB:CV_AB + 8] = np.asarray(inp["ab_norm"], np.float32).reshape(DC, 128).T
    cv[:, CV_C:CV_C + 8] = np.asarray(inp["c_norm"], np.float32).reshape(DC, 128).T
    cv[:, CV_QN] = np.tile(np.asarray(inp["q_norm"], np.float32).reshape(64), 2)
    cv[:, CV_KN] = np.tile(np.asarray(inp["k_norm"], np.float32).reshape(64), 2)
    cv[:, CV_PS:CV_PS + 4] = np.asarray(inp["pool_scale"], np.float32).reshape(4, 128).T
    cv[:, CV_SINK:CV_SINK + 8] = np.asarray(inp["attn_sinks"], np.float32).reshape(1, 8)
    mu = np.asarray(inp["c_mu"], np.float32).reshape(6, DC, 128)
    for i in range(6):
        cv[:, CV_MU + i * 8:CV_MU + i * 8 + 8] = mu[i].T
    cv[:, CV_LNW:CV_LNW + 8] = np.asarray(inp["c_lnx_w"], np.float32).reshape(DC, 128).T
    cv[:, CV_LNB:CV_LNB + 8] = np.asarray(inp["c_lnx_b"], np.float32).reshape(DC, 128).T
    return cv


class Builder:
    def __init__(self, T, stages):
        self.T = T
        self.NT = T // TT
        self.stages = stages
        nc = bass.Bass("TRN2", target_bir_lowering=False)
        self.nc = nc
        self.c = Ctx(nc)

    def dram_in(self, name, shape):
        return self.nc.dram_tensor(name, list(shape), F32, kind="ExternalInput").ap()

    def E(self, eng, fn, reads=(), writes=()):
        self.c.emit(eng, fn, reads=reads, writes=writes)

    def ar_reset(self):
        self.ar_ptr = 0

    def ar_alloc(self, free_shape, dt):
        n = 1
        for s_ in free_shape:
            n *= s_
        nbytes = n * (2 if dt == BF16 else 4)
        nb = (nbytes + 255) // 256
        b0 = self.ar_ptr
        self.ar_ptr += nb
        assert self.ar_ptr <= ARENA_KB * 4, f"arena overflow {self.ar_ptr}"
        self.ar_peak = max(getattr(self, "ar_peak", 0), self.ar_ptr)
        ap = self.arena[:, b0 * 64:b0 * 64 + (nbytes + 3) // 4]
        if dt == BF16:
            ap = ap.bitcast(BF16)
        if len(free_shape) == 2:
            ap = ap.rearrange("p (a b) -> p a b", a=free_shape[0])
        elif len(free_shape) == 3:
            ap = ap.rearrange("p (a b c) -> p a b c", a=free_shape[0], b=free_shape[1])
        return Buf(ap, self.ar_t[b0:b0 + nb])

    def bank(self, b):
        return self.psb[b], [self.psh_t[b]]

    def bk(self, b, lo, hi):
        return self.psb[b][:, lo:hi], [self.psh_t[b]]

    def gload(self, src):
        i = self.gcount % NG_RING
        self.gcount += 1
        buf, t, sl = self.gring[i], self.gring_t[i], self.gring_s[i]
        assert (not t.w) or t.r, "G ring buffer overwritten before it was consumed"
        self.c.emit("pool", lambda e: e.dma_start(out=buf[:].rearrange("p (c n) -> p c n", c=DC), in_=src),
                    writes=[t], slot=sl)
        return buf[:].rearrange("p (c n) -> p c n", c=DC), t

    def dload(self, src):
        i = self.dcount % 4
        self.dcount += 1
        buf, t, sl = self.dring[i], self.dring_t[i], self.dring_s[i]
        assert (not t.w) or t.r, "D ring buffer overwritten before it was consumed"
        self.c.emit("pool", lambda e: e.dma_start(out=buf[:].rearrange("p (c n) -> p c n", c=FC // 2), in_=src),
                    writes=[t], slot=sl)
        return buf[:].rearrange("p (c n) -> p c n", c=FC // 2), t

    def lazy(self, srcs, ahead):
        items = [None] * len(srcs)
        state = {"next": 0}

        def get(i):
            while state["next"] <= min(i + ahead, len(srcs) - 1):
                j = state["next"]
                items[j] = self.gload(srcs[j])
                state["next"] += 1
            return items[i]
        return get

    def build(self):
        nc, c = self.nc, self.c
        T = self.T
        st = self.stages
        self.x = self.dram_in("x", [T, D])
        self.y = nc.dram_tensor("y", [T, D], F32, kind="ExternalOutput").ap()
        self.colvec_d = self.dram_in("colvec", [128, NCOL])
        self.cmat_d = self.dram_in("cmat", [128, NCMAT])
        need_ffn = any(s.startswith("ffn") for s in st)
        if need_ffn:
            self.wg = self.dram_in("wg", [4, D, DFF])
            self.wu = self.dram_in("wu", [4, D, DFF])
            self.wd = self.dram_in("wd", [4, DFF, D])
        self.alloc_common()
        self.setup_consts()
        if "ab" in st:
            self.setup_ab()
        if "rwkv" in st:
            self.setup_rwkv()
        self.xpre = None
        for ti in range(self.NT):
            self.load_x(ti)
            for s in st:
                if s == st[-1] and ti + 1 < self.NT and len(st) > 1:
                    self.prefetch_x(ti + 1)
                self.run_stage(ti, s)
            self.store_x(ti)
        c.finalize()
        return nc

    def alloc_common(self):
        c = self.c
        self.colvec = c.sb("colvec", [128, NCOL], F32)
        self.colvec_t = Tk("colvec")
        self.cmat = c.sb("cmat", [128, NCMAT], F32)
        self.cmat_t = Tk("cmat")
        self.cbf = c.sb("cbf", [128, 512], BF16)
        self.cbf_t = Tk("cbf")
        self.xT = c.sb("xT", [128, DC, TT], F32)
        self.xT_t = [Tk(f"xT{i}") for i in range(DC)]
        self.xin_slot = c.slot("xin")
        self.xout_slot = c.slot("xout")
        c.out_slots.append(self.xout_slot)
        self.h = c.sb("h", [128, DC, TT + 1], BF16)
        self.h_t = [Tk(f"h{i}") for i in range(DC)]
        self.arena = c.sb("arena", [128, ARENA_KB * 256], F32)
        self.ar_t = [Tk(f"ar{i}") for i in range(ARENA_KB * 4)]
        self.ar_ptr = 0
        self.psb = [c.ps(f"psb{i}", [128, 512], F32) for i in range(8)]
        self.psh_t = [Tk(f"psb{i}", excl=True) for i in range(8)]
        self.const_slot = c.slot("const", group=True)
        self.gring = [c.sb(f"gring{i}", [128, 2048], BF16) for i in range(NG_RING)]
        self.gring_t = [Tk(f"gring{i}") for i in range(NG_RING)]
        self.gring_s = [c.slot(f"gring{i}") for i in range(NG_RING)]
        self.dring = [c.sb(f"dring{i}", [128, (FC // 2) * 128], BF16) for i in range(4)]
        self.dring_t = [Tk(f"dring{i}") for i in range(4)]
        self.dring_s = [c.slot(f"dring{i}") for i in range(4)]
        self.gcount = 0
        self.dcount = 0

    def setup_consts(self):
        c = self.c
        cv, cm = self.colvec, self.cmat
        c.emit("sp", lambda e: e.dma_start(out=cv[:], in_=self.colvec_d[:, :]),
               writes=[self.colvec_t], slot=self.const_slot)
        c.emit("sp", lambda e: e.dma_start(out=cm[:], in_=self.cmat_d[:, :]),
               writes=[self.cmat_t], slot=self.const_slot)
        cb = self.cbf
        self.E("dve", lambda e: e.tensor_copy(cb[:, 0:128], cm[:, CM_ONES:CM_ONES + 128]),
               [self.cmat_t], [self.cbf_t])
        self.E("dve", lambda e: e.tensor_copy(cb[:, 128:256], cm[:, CM_IDENT:CM_IDENT + 128]),
               [self.cmat_t, self.cbf_t], [self.cbf_t])
        self.E("dve", lambda e: e.tensor_copy(cb[:, 256:384], cm[:, CM_BD:CM_BD + 128]),
               [self.cmat_t, self.cbf_t], [self.cbf_t])
        self.E("dve", lambda e: e.tensor_copy(cb[:, 384:512], cm[:, CM_UI:CM_UI + 128]),
               [self.cmat_t, self.cbf_t], [self.cbf_t])
        self.E("dve", lambda e: e.memset(self.h[:], 0.0), [], self.h_t)

    def prefetch_x(self, ti):
        nb = NB * D * 4 // 256
        b0 = ARENA_KB * 4 - nb
        ap = self.arena[:, b0 * 64:(b0 + nb) * 64].rearrange("p (a b) -> p a b", a=NB)
        xin = Buf(ap, self.ar_t[b0:b0 + nb])
        src = self.x[ti * TT:(ti + 1) * TT, :].rearrange("(b p) d -> p b d", p=128)
        self.c.emit("sp", lambda e: e.dma_start(out=xin.ap, in_=src), writes=xin.tks, slot=self.xin_slot)
        self.xpre = xin

    def load_x(self, ti):
        c = self.c
        self.ar_reset()
        if self.xpre is not None:
            xin = self.xpre
            self.xpre = None
        else:
            xin = self.ar_alloc([NB, D], F32)
            src = self.x[ti * TT:(ti + 1) * TT, :].rearrange("(b p) d -> p b d", p=128)
            c.emit("sp", lambda e: e.dma_start(out=xin.ap, in_=src), writes=xin.tks, slot=self.xin_slot)
        ident = self.cmat[:, CM_IDENT:CM_IDENT + 128]
        for dc in range(DC):
            ps, pst = self.bank(dc % 2)
            for b in range(NB):
                self.E("pe", lambda e, b=b, dc=dc, ps=ps: e.transpose(
                    ps[:, b * 128:(b + 1) * 128], xin.ap[:, b, dc * 128:(dc + 1) * 128], ident),
                    xin.tks + [self.cmat_t], pst)
            if dc % 2 == 0:
                self.E("act", lambda e, dc=dc, ps=ps: e.copy(self.xT[:, dc, :], ps[:]), pst, [self.xT_t[dc]])
            else:
                self.E("dve", lambda e, dc=dc, ps=ps: e.tensor_copy(self.xT[:, dc, :], ps[:]), pst, [self.xT_t[dc]])

    def store_x(self, ti):
        c = self.c
        self.ar_reset()
        xo = self.ar_alloc([NB, D], F32)
        ident = self.cmat[:, CM_IDENT:CM_IDENT + 128]
        k = 0
        for b in range(NB):
            for half in range(2):
                ps, pst = self.bank(k % 2)
                k += 1
                for q in range(4):
                    dc = half * 4 + q
                    self.E("pe", lambda e, b=b, dc=dc, q=q, ps=ps: e.transpose(
                        ps[:, q * 128:(q + 1) * 128], self.xT[:, dc, b * 128:(b + 1) * 128], ident),
                        [self.xT_t[dc], self.cmat_t], pst)
                if k % 2 == 0:
                    self.E("act", lambda e, b=b, half=half, ps=ps: e.copy(xo.ap[:, b, half * 512:(half + 1) * 512], ps[:]),
                           pst, xo.tks)
                else:
                    self.E("dve", lambda e, b=b, half=half, ps=ps: e.tensor_copy(xo.ap[:, b, half * 512:(half + 1) * 512], ps[:]),
                           pst, xo.tks)
        dst = self.y[ti * TT:(ti + 1) * TT, :].rearrange("(b p) d -> p b d", p=128)
        c.emit("sp", lambda e: e.dma_start(out=dst, in_=xo.ap), reads=xo.tks, slot=self.xout_slot)

    def rmsnorm(self, gcol):
        ones = self.cbf[:, 0:128]
        sq = [self.ar_alloc([TT], BF16) for _ in range(2)]
        rstd = self.ar_alloc([TT], F32)
        ps, pst = self.bank(2)
        for dc in range(DC):
            s_ = sq[dc % 2]
            self.E("act", lambda e, dc=dc, s_=s_: e.activation(s_.ap, self.xT[:, dc, :], AF.Square),
                   [self.xT_t[dc]], s_.tks)
            self.E("pe", lambda e, dc=dc, s_=s_: e.matmul(ps[:], ones, s_.ap, start=(dc == 0), stop=(dc == DC - 1)),
                   s_.tks + [self.cbf_t], pst)
        self.E("act", lambda e: e.activation(rstd.ap, ps[:], AF.Sqrt, bias=self.colvec[:, CV_EPS:CV_EPS + 1], scale=1.0 / D),
               pst + [self.colvec_t], rstd.tks)
        self.E("dve", lambda e: e.reciprocal(rstd.ap, rstd.ap), rstd.tks, rstd.tks)
        for dc in range(DC):
            self.E("dve", lambda e, dc=dc: e.scalar_tensor_tensor(
                self.h[:, dc, 1:TT + 1], self.xT[:, dc, :], self.colvec[:, gcol + dc:gcol + dc + 1], rstd.ap,
                ALU.mult, ALU.mult),
                [self.xT_t[dc], self.colvec_t] + rstd.tks, [self.h_t[dc]])

    def hv(self, dc, lo=0, hi=TT):
        return self.h[:, dc, 1 + lo:1 + hi]

    def ffn(self, li):
        self.ar_reset()
        wgv = self.wg[li].rearrange("(c p) f -> p c f", p=128)
        wuv = self.wu[li].rearrange("(c p) f -> p c f", p=128)
        wdv = self.wd[li].rearrange("(c p) n -> p c n", p=128)
        NG = FC // 2
        q = []

        def issue(g):
            q.append((self.gload(wgv[:, :, g * 256:(g + 1) * 256]), self.gload(wuv[:, :, g * 256:(g + 1) * 256])))
        issue(0)
        issue(1)
        HF = FC // 2
        dsrc = [wdv[:, hf * HF:(hf + 1) * HF, dc * 128:(dc + 1) * 128] for dc in range(DC) for hf in range(2)]
        dq = [self.dload(dsrc[0]), self.dload(dsrc[1])]
        self.rmsnorm(CV_FFN + li * 8)
        act = [self.ar_alloc([TT], BF16) for _ in range(FC)]
        silu = [self.ar_alloc([TT], F32) for _ in range(2)]
        for g in range(NG):
            if g + 2 < NG:
                issue(g + 2)
            (gb, gt), (ub, ut) = q.pop(0)
            for fc in range(2):
                f = g * 2 + fc
                psg, psgt = self.bank(3 + (f % 2))
                psu, psut = self.bank(5 + (f % 2))
                for dc in range(DC):
                    self.E("pe", lambda e, dc=dc, fc=fc, gb=gb, psg=psg: e.matmul(
                        psg[:], gb[:, dc, fc * 128:(fc + 1) * 128], self.hv(dc), start=(dc == 0), stop=(dc == DC - 1)),
                        [gt, self.h_t[dc]], psgt)
                for dc in range(DC):
                    self.E("pe", lambda e, dc=dc, fc=fc, ub=ub, psu=psu: e.matmul(
                        psu[:], ub[:, dc, fc * 128:(fc + 1) * 128], self.hv(dc), start=(dc == 0), stop=(dc == DC - 1)),
                        [ut, self.h_t[dc]], psut)
                st_ = silu[f % 2]
                self.E("act", lambda e, st_=st_, psg=psg: e.activation(st_.ap, psg[:], AF.Silu), psgt, st_.tks)
                self.E("dve", lambda e, st_=st_, psu=psu, f=f: e.tensor_tensor(act[f].ap, st_.ap, psu[:], ALU.mult),
                       st_.tks + psut, act[f].tks)
        for dc in range(DC):
            psy, psyt = self.bank(7 if dc % 2 == 0 else 2)
            for hf in range(2):
                nxt = dc * 2 + hf + 2
                if nxt < 2 * DC:
                    dq.append(self.dload(dsrc[nxt]))
                db, dt_ = dq.pop(0)
                for ff in range(HF):
                    f = hf * HF + ff
                    self.E("pe", lambda e, f=f, ff=ff, db=db, psy=psy: e.matmul(
                        psy[:], db[:, ff, :], act[f].ap, start=(f == 0), stop=(f == FC - 1)),
                        [dt_] + act[f].tks, psyt)
            self.E("dve", lambda e, dc=dc, psy=psy: e.scalar_tensor_tensor(
                self.xT[:, dc, :], psy[:], 0.5, self.xT[:, dc, :], ALU.mult, ALU.add),
                psyt + [self.xT_t[dc]], [self.xT_t[dc]])

    def setup_ab(self):
        c = self.c
        self.w_in_d = self.dram_in("w_in", [D, 1280])
        self.w_out_d = self.dram_in("w_out", [D, D])
        self.pool_w_d = self.dram_in("pool_w", [4, 128, 128])
        self.pw = c.sb("pw", [128, 4, 128], BF16)
        self.pw_t = Tk("pw")
        self.abw_slot = c.slot("abw", group=True)
        self.kcar = c.sb("kcar", [128, 128], BF16)
        self.kcar_t = Tk("kcar")
        self.vcar = c.sb("vcar", [128, 128], BF16)
        self.vcar_t = Tk("vcar")
        self.pcar = c.sb("pcar", [128, 4, 16], F32)
        self.pcar_t = Tk("pcar")
        self.esk = c.sb("esink", [128, 512], F32)
        self.esb = c.sb("esb", [128, 8], F32)
        self.esk_t = Tk("esink")
        src4 = self.pool_w_d.rearrange("g c d -> c g d")
        c.emit("pool", lambda e: e.dma_start(out=self.pw[:], in_=src4), writes=[self.pw_t], slot=self.abw_slot)
        self.E("act", lambda e: e.activation(self.esb[:], self.colvec[:, CV_SINK:CV_SINK + 8], AF.Exp),
               [self.colvec_t], [self.esk_t])
        for hk in range(2):
            for g in range(4):
                hq = hk * 4 + g
                self.E("dve", lambda e, hk=hk, g=g, hq=hq: e.tensor_copy(
                    self.esk[hk * 64:(hk + 1) * 64, g * 128:(g + 1) * 128],
                    self.esb[hk * 64:(hk + 1) * 64, hq:hq + 1].to_broadcast([64, 128])),
                    [self.esk_t], [self.esk_t])
        self.E("dve", lambda e: e.memset(self.pcar[:], 0.0), [], [self.pcar_t])
        self.E("dve", lambda e: e.memset(self.kcar[:], 0.0), [], [self.kcar_t])
        self.E("dve", lambda e: e.memset(self.vcar[:], 0.0), [], [self.vcar_t])

    def ab(self, ti):
        self.ar_reset()
        bd = self.cbf[:, 256:384]
        ones = self.cbf[:, 0:128]
        winv = self.w_in_d.rearrange("(c p) n -> p c n", p=128)
        woutv = self.w_out_d.rearrange("(c p) n -> p c n", p=128)
        wq = self.lazy([winv[:, :, i * 256:(i + 1) * 256] for i in range(5)], 1)
        wq(0)
        self.rmsnorm(CV_AB)
        qn = [self.ar_alloc([TT], BF16) for _ in range(4)]
        kn = self.ar_alloc([128 + TT], BF16)
        vtm = self.ar_alloc([NB + 1, 128], BF16)
        qraw = [self.ar_alloc([TT], F32) for _ in range(2)]
        sqb = [self.ar_alloc([TT], BF16) for _ in range(2)]
        rstd2 = self.ar_alloc([TT], F32)
        W = 16 + TT
        pbuf = [self.ar_alloc([W], F32) for _ in range(4)]
        ptmp = [self.ar_alloc([W], F32) for _ in range(2)]
        pdiff = [self.ar_alloc([TT], BF16) for _ in range(4)]
        opool = [self.ar_alloc([TT], BF16) for _ in range(4)]
        oT = self.ar_alloc([4, TT], BF16)
        ebuf = [self.ar_alloc([512], F32) for _ in range(2)]
        pt = [[self.ar_alloc([512], BF16) for _ in range(2)] for _ in range(2)]
        rden = [self.ar_alloc([512], F32) for _ in range(2)]
        self.E("dve", lambda e: e.tensor_copy(kn.ap[:, 0:128], self.kcar[:]), [self.kcar_t], kn.tks)
        self.E("dve", lambda e: e.tensor_copy(vtm.ap[:, 0, :], self.vcar[:]), [self.vcar_t], vtm.tks)
        for gi in range(4):
            self.E("dve", lambda e, gi=gi: e.tensor_copy(pbuf[gi].ap[:, 0:16], self.pcar[:, gi, :]), [self.pcar_t], pbuf[gi].tks)
        for ci in range(5):
            wb, wt = wq(ci // 2)
            n0 = (ci % 2) * 128
            ps, pst = self.bank(ci % 2)
            for dc in range(DC):
                self.E("pe", lambda e, dc=dc, n0=n0, ps=ps, wb=wb: e.matmul(
                    ps[:], wb[:, dc, n0:n0 + 128], self.hv(dc), start=(dc == 0), stop=(dc == DC - 1)),
                    [wt, self.h_t[dc]], pst)
            qr, s_ = qraw[ci % 2], sqb[ci % 2]
            ps2, ps2t = self.bank(2)
            self.E("act", lambda e, qr=qr, ps=ps: e.copy(qr.ap, ps[:]), pst, qr.tks)
            self.E("act", lambda e, s_=s_, ps=ps: e.activation(s_.ap, ps[:], AF.Square), pst, s_.tks)
            self.E("pe", lambda e, s_=s_, ps2=ps2: e.matmul(ps2[:], bd, s_.ap, start=True, stop=True),
                   s_.tks + [self.cbf_t], ps2t)
            self.E("act", lambda e, ps2=ps2: e.activation(rstd2.ap, ps2[:], AF.Sqrt,
                                                          bias=self.colvec[:, CV_EPS:CV_EPS + 1], scale=1.0 / 64),
                   ps2t + [self.colvec_t], rstd2.tks)
            self.E("dve", lambda e: e.reciprocal(rstd2.ap, rstd2.ap), rstd2.tks, rstd2.tks)
            if ci < 4:
                self.E("dve", lambda e, ci=ci, qr=qr: e.scalar_tensor_tensor(
                    qn[ci].ap, qr.ap, self.colvec[:, CV_QN:CV_QN + 1], rstd2.ap, ALU.mult, ALU.mult),
                    qr.tks + rstd2.tks + [self.colvec_t], qn[ci].tks)
            else:
                self.E("dve", lambda e, qr=qr: e.scalar_tensor_tensor(
                    kn.ap[:, 128:128 + TT], qr.ap, self.colvec[:, CV_KN:CV_KN + 1], rstd2.ap, ALU.mult, ALU.mult),
                    qr.tks + rstd2.tks + [self.colvec_t], kn.tks)
        wb, wt = wq(2)
        ps, pst = self.bank(1)
        for b in range(NB):
            for dc in range(DC):
                self.E("pe", lambda e, dc=dc, b=b, ps=ps, wb=wb: e.matmul(
                    ps[:, b * 128:(b + 1) * 128], self.hv(dc, b * 128, (b + 1) * 128), wb[:, dc, 128:256],
                    start=(dc == 0), stop=(dc == DC - 1)),
                    [wt, self.h_t[dc]], pst)
        self.E("act", lambda e, ps=ps: e.copy(vtm.ap[:, 1:NB + 1, :], ps[:].rearrange("p (b n) -> p b n", b=NB)),
               pst, vtm.tks)
        for gi in range(4):
            wb, wt = wq(3 + gi // 2)
            n0 = (gi % 2) * 128
            ps, pst = self.bank(gi % 2)
            for dc in range(DC):
                self.E("pe", lambda e, dc=dc, n0=n0, ps=ps, wb=wb: e.matmul(
                    ps[:], wb[:, dc, n0:n0 + 128], self.hv(dc), start=(dc == 0), stop=(dc == DC - 1)),
                    [wt, self.h_t[dc]], pst)
            self.E("act", lambda e, gi=gi, ps=ps: e.copy(pbuf[gi].ap[:, 16:16 + TT], ps[:]), pst, pbuf[gi].tks)
        woq = self.lazy([woutv[:, :, i * 256:(i + 1) * 256] for i in range(4)], 1)
        woq(0)
        for gi, w in enumerate((2, 4, 8, 16)):
            cur = pbuf[gi]
            lo = 0
            k = 0
            step = 1
            while step < w:
                nxt = ptmp[k % 2]
                lo += step
                self.E("dve", lambda e, cur=cur, nxt=nxt, lo=lo, step=step: e.tensor_tensor(
                    nxt.ap[:, lo:W], cur.ap[:, lo:W], cur.ap[:, lo - step:W - step], ALU.add),
                    cur.tks, nxt.tks)
                cur = nxt
                step *= 2
                k += 1
            self.E("dve", lambda e, cur=cur, gi=gi, w=w: e.scalar_tensor_tensor(
                pdiff[gi].ap, cur.ap[:, 16:W], 1.0 / w, pbuf[gi].ap[:, 16:W], ALU.mult, ALU.subtract),
                cur.tks + pbuf[gi].tks, pdiff[gi].tks)
            if ti == 0:
                tmp = qraw[0]
                self.E("dve", lambda e, cur=cur, gi=gi, tmp=tmp: e.tensor_tensor(
                    tmp.ap[:, 0:16], cur.ap[:, 16:32], self.cmat[:, CM_INVC + gi * 16:CM_INVC + gi * 16 + 16], ALU.mult),
                    cur.tks + [self.cmat_t], tmp.tks)
                self.E("dve", lambda e, gi=gi, tmp=tmp: e.tensor_tensor(
                    pdiff[gi].ap[:, 0:16], tmp.ap[:, 0:16], pbuf[gi].ap[:, 16:32], ALU.subtract),
                    tmp.tks + pbuf[gi].tks, pdiff[gi].tks)
            self.E("dve", lambda e, gi=gi: e.tensor_copy(self.pcar[:, gi, :], pbuf[gi].ap[:, TT:TT + 16]),
                   pbuf[gi].tks, [self.pcar_t])
            ps, pst = self.bank(gi % 2)
            self.E("pe", lambda e, gi=gi, ps=ps: e.matmul(ps[:], self.pw[:, gi, :], pdiff[gi].ap, start=True, stop=True),
                   [self.pw_t] + pdiff[gi].tks, pst)
            self.E("act", lambda e, gi=gi, ps=ps: e.activation(opool[gi].ap, ps[:], AF.Copy,
                                                               scale=self.colvec[:, CV_PS + gi:CV_PS + gi + 1]),
                   pst + [self.colvec_t], opool[gi].tks)
        def kbs_of(b):
            return [0, 1] if ti * NB + b > 0 else [1]

        def scores(b):
            for hk in range(2):
                p0 = hk * 64
                for kb in kbs_of(b):
                    ps, pst = self.bank(3 + 2 * hk + kb)
                    for g in range(4):
                        self.E("pe", lambda e, ps=ps, p0=p0, b=b, kb=kb, g=g: e.matmul(
                            ps[:, g * 128:(g + 1) * 128],
                            kn.ap[p0:p0 + 64, (b + kb) * 128:(b + kb + 1) * 128],
                            qn[g].ap[p0:p0 + 64, b * 128:(b + 1) * 128], start=True, stop=True),
                            kn.tks + qn[g].tks, pst)
        scores(0)
        for b in range(NB):
            kbs = kbs_of(b)
            for hk in range(2):
                ptb = pt[hk]
                for kb in kbs:
                    ps, pst = self.bank(3 + 2 * hk + kb)
                    eb = ebuf[kb]
                    self.E("act", lambda e, eb=eb, ps=ps: e.activation(eb.ap, ps[:], AF.Exp, scale=0.125), pst, eb.tks)
                    m0 = CM_AM + (hk * 2 + kb) * 512
                    self.E("dve", lambda e, eb=eb, ptb=ptb, kb=kb, m0=m0: e.tensor_tensor(
                        ptb[kb].ap, eb.ap, self.cmat[:, m0:m0 + 512], ALU.mult),
                        eb.tks + [self.cmat_t], ptb[kb].tks)
            if b + 1 < NB:
                scores(b + 1)
            for hk in range(2):
                p0 = hk * 64
                ptb = pt[hk]
                pv, pvt = self.bank(7 if hk == 0 else 0)
                den, dent = self.bank(2 if hk == 0 else 1)
                for i, kb in enumerate(kbs):
                    self.E("pe", lambda e, i=i, kb=kb, b=b, ptb=ptb, pv=pv, n=len(kbs): e.matmul(
                        pv[:, :], vtm.ap[:, b + kb, :], ptb[kb].ap, start=(i == 0), stop=(i == n - 1)),
                        vtm.tks + ptb[kb].tks, pvt)
                for i, kb in enumerate(kbs):
                    self.E("pe", lambda e, i=i, kb=kb, ptb=ptb, den=den, n=len(kbs): e.matmul(
                        den[:, :], ones, ptb[kb].ap, start=(i == 0), stop=(i == n - 1)),
                        [self.cbf_t] + ptb[kb].tks, dent)
                rd = rden[hk]
                self.E("dve", lambda e, rd=rd, den=den, p0=p0: e.tensor_tensor(
                    rd.ap[p0:p0 + 64, :], den[p0:p0 + 64, :], self.esk[p0:p0 + 64, :], ALU.add),
                    dent + [self.esk_t], rd.tks)
                self.E("dve", lambda e, rd=rd, p0=p0: e.reciprocal(rd.ap[p0:p0 + 64, :], rd.ap[p0:p0 + 64, :]), rd.tks, rd.tks)
                self.E("dve", lambda e, rd=rd, pv=pv, p0=p0, b=b: e.tensor_tensor(
                    oT.ap[p0:p0 + 64, :, b * 128:(b + 1) * 128],
                    pv[p0:p0 + 64, :].rearrange("p (g q) -> p g q", g=4),
                    rd.ap[p0:p0 + 64, :].rearrange("p (g q) -> p g q", g=4), ALU.mult),
                    pvt + rd.tks, oT.tks)
        self.E("dve", lambda e: e.tensor_copy(self.kcar[:], kn.ap[:, TT:TT + 128]), kn.tks, [self.kcar_t])
        self.E("dve", lambda e: e.tensor_copy(self.vcar[:], vtm.ap[:, NB, :]), vtm.tks, [self.vcar_t])
        for dc in range(DC):
            wb, wt = woq(dc // 2)
            n0 = (dc % 2) * 128
            ps, pst = self.bank(3 + dc % 2)
            for ch in range(4):
                self.E("pe", lambda e, ch=ch, ps=ps, wb=wb, n0=n0: e.matmul(
                    ps[:], wb[:, ch, n0:n0 + 128], oT.ap[:, ch, :], start=(ch == 0), stop=False),
                    [wt] + oT.tks, pst)
            for gi in range(4):
                self.E("pe", lambda e, gi=gi, ps=ps, wb=wb, n0=n0: e.matmul(
                    ps[:], wb[:, 4 + gi, n0:n0 + 128], opool[gi].ap, start=False, stop=(gi == 3)),
                    [wt] + opool[gi].tks, pst)
            self.E("dve", lambda e, dc=dc, ps=ps: e.tensor_tensor(self.xT[:, dc, :], ps[:], self.xT[:, dc, :], ALU.add),
                   pst + [self.xT_t[dc]], [self.xT_t[dc]])

    def setup_rwkv(self):
        c = self.c
        self.cw = {nm: self.dram_in(nm, [D, D]) for nm in ("c_w_r", "c_w_k", "c_w_v", "c_w_o")}
        w1_d = self.dram_in("c_w1", [D, 64])
        a1_d = self.dram_in("c_a1", [D, 64])
        g1_d = self.dram_in("c_g1", [D, 128])
        w2_d = self.dram_in("c_w2e", [65, D])
        a2_d = self.dram_in("c_a2e", [65, D])
        g2_d = self.dram_in("c_g2", [128, D])
        rows_d = self.dram_in("c_rows", [3, D])
        self.w1b = c.sb("w1b", [128, DC, 64], BF16)
        self.a1b = c.sb("a1b", [128, DC, 64], BF16)
        self.g1b = c.sb("g1b", [128, DC, 128], BF16)
        self.w2b = c.sb("w2b", [65, D], BF16)
        self.a2b = c.sb("a2b", [65, D], BF16)
        self.g2b = c.sb("g2b", [128, D], BF16)
        self.rowb = c.sb("rowb", [128, 3, D], F32)
        self.cw_t = Tk("cw")
        self.cw_slot = c.slot("cw", group=True)
        self.spad = c.sb("spad", [128, 16, 64], F32)
        self.sbf = c.sb("sbf", [128, 16, 64], BF16)
        self.spad_t = [Tk(f"spad{i}") for i in range(16)]
        self.sbf_t = [Tk(f"sbf{i}") for i in range(16)]
        sl = self.cw_slot
        lds = [(self.w1b[:], w1_d.rearrange("(c p) n -> p c n", p=128)),
               (self.a1b[:], a1_d.rearrange("(c p) n -> p c n", p=128)),
               (self.g1b[:], g1_d.rearrange("(c p) n -> p c n", p=128)),
               (self.w2b[:], w2_d[:, :]), (self.a2b[:], a2_d[:, :]), (self.g2b[:], g2_d[:, :])]
        for n_, (dst, src) in enumerate(lds):
            c.emit("pool", lambda e, dst=dst, src=src: e.dma_start(out=dst, in_=src),
                   writes=[self.cw_t if n_ == len(lds) - 1 else Tk("x")], slot=sl)
        self.rowb_t = Tk("rowb")
        sl2 = c.slot("cwrow", group=True)
        for i in range(3):
            c.emit("sp", lambda e, i=i: e.dma_start(out=self.rowb[:, i:i + 1, :], in_=rows_d[i:i + 1, :].partition_broadcast(128)),
                   writes=[self.rowb_t if i == 2 else Tk("x")], slot=sl2)
        self.E("dve", lambda e: e.memset(self.spad[:], 0.0), [], self.spad_t)
        self.E("dve", lambda e: e.memset(self.sbf[:], 0.0), [], self.sbf_t)

    def rwkv(self, ti):
        self.ar_reset()
        cv = self.colvec
        identb = self.cbf[:, 128:256]
        cm = self.cmat
        cwt = [self.cw_t]

        def wview(nm, q):
            return self.cw[nm].rearrange("(c p) n -> p c n", p=128)[:, :, q * 256:(q + 1) * 256]
        self.rmsnorm(CV_C)
        rb = [self.ar_alloc([D], BF16) for _ in range(NB)]
        kb_ = [self.ar_alloc([D], BF16) for _ in range(NB)]
        vb = [self.ar_alloc([D], BF16) for _ in range(NB)]
        t1e = self.ar_alloc([TT], BF16)
        a1e = self.ar_alloc([TT], BF16)
        g1T = self.ar_alloc([TT], BF16)
        yT = [self.ar_alloc([TT], BF16) for _ in range(DC)]
        mark = self.ar_ptr
        xx = [self.ar_alloc([TT], BF16) for _ in range(DC)]
        xm = [self.ar_alloc([TT], BF16) for _ in range(DC)]
        for dc in range(DC):
            self.E("dve", lambda e, dc=dc: e.tensor_tensor(xx[dc].ap, self.h[:, dc, 0:TT], self.h[:, dc, 1:TT + 1], ALU.subtract),
                   [self.h_t[dc]], xx[dc].tks)
        self.E("dve", lambda e: e.tensor_copy(self.h[:, :, 0:1], self.h[:, :, TT:TT + 1]), self.h_t, self.h_t)

        def mix(i):
            for dc in range(DC):
                self.E("dve", lambda e, dc=dc, i=i: e.scalar_tensor_tensor(
                    xm[dc].ap, xx[dc].ap, cv[:, CV_MU + i * 8 + dc:CV_MU + i * 8 + dc + 1], self.hv(dc), ALU.mult, ALU.add),
                    xx[dc].tks + [self.h_t[dc], self.colvec_t], xm[dc].tks)
        mix(1)
        ps, pst = self.bank(0)
        for dc in range(DC):
            self.E("pe", lambda e, dc=dc, ps=ps: e.matmul(ps[0:64, :], self.w1b[:, dc, :], xm[dc].ap, start=(dc == 0), stop=(dc == DC - 1)),
                   cwt + xm[dc].tks, pst)
        self.E("act", lambda e, ps=ps: e.activation(t1e.ap[0:64, :], ps[0:64, :], AF.Tanh), pst, t1e.tks)
        self.E("dve", lambda e: e.memset(t1e.ap[64:65, :], 1.0), [], t1e.tks)
        mix(4)
        ps, pst = self.bank(1)
        for dc in range(DC):
            self.E("pe", lambda e, dc=dc, ps=ps: e.matmul(ps[0:64, :], self.a1b[:, dc, :], xm[dc].ap, start=(dc == 0), stop=(dc == DC - 1)),
                   cwt + xm[dc].tks, pst)
        self.E("act", lambda e, ps=ps: e.copy(a1e.ap[0:64, :], ps[0:64, :]), pst, a1e.tks)
        self.E("dve", lambda e: e.memset(a1e.ap[64:65, :], 1.0), [], a1e.tks)
        mix(5)
        ps, pst = self.bank(0)
        for dc in range(DC):
            self.E("pe", lambda e, dc=dc, ps=ps: e.matmul(ps[:, :], self.g1b[:, dc, :], xm[dc].ap, start=(dc == 0), stop=(dc == DC - 1)),
                   cwt + xm[dc].tks, pst)
        self.E("act", lambda e, ps=ps: e.activation(g1T.ap, ps[:, :], AF.Sigmoid), pst, g1T.tks)
        hcnt = 0
        for (i, nm, dst) in ((0, "c_w_r", rb), (2, "c_w_k", kb_), (3, "c_w_v", vb)):
            mix(i)
            wl = self.lazy([wview(nm, q_) for q_ in range(4)], 1)
            for q in range(4):
                wb, wt = wl(q)
                for cb in range(NB):
                    ph, pht = self.bk(hcnt % 8, 0, 256)
                    hcnt += 1
                    for dc in range(DC):
                        self.E("pe", lambda e, dc=dc, cb=cb, ph=ph, wb=wb: e.matmul(
                            ph, xm[dc].ap[:, cb * 128:(cb + 1) * 128], wb[:, dc, :], start=(dc == 0), stop=(dc == DC - 1)),
                            [wt] + xm[dc].tks, pht)
                    if hcnt % 2 == 0:
                        self.E("act", lambda e, cb=cb, q=q, ph=ph, dst=dst: e.copy(dst[cb].ap[:, q * 256:(q + 1) * 256], ph), pht, dst[cb].tks)
                    else:
                        self.E("dve", lambda e, cb=cb, q=q, ph=ph, dst=dst: e.tensor_copy(dst[cb].ap[:, q * 256:(q + 1) * 256], ph), pht, dst[cb].tks)
        if RWDBG <= 1:
            return
        for cb in range(NB):
            self.ar_ptr = mark
            self.rwkv_chunk(ti, cb, rb[cb], kb_[cb], vb[cb], t1e, a1e, g1T, yT)
        wl = self.lazy([wview("c_w_o", q_) for q_ in range(4)], 1)
        for q in range(4):
            wb, wt = wl(q)
            for nn in range(2):
                dco = q * 2 + nn
                ps, pst = self.bank(dco % 2)
                for blk in range(DC):
                    self.E("pe", lambda e, blk=blk, nn=nn, ps=ps, wb=wb: e.matmul(
                        ps[:], wb[:, blk, nn * 128:(nn + 1) * 128], yT[blk].ap, start=(blk == 0), stop=(blk == DC - 1)),
                        [wt] + yT[blk].tks, pst)
                self.E("dve", lambda e, dco=dco, ps=ps: e.tensor_tensor(self.xT[:, dco, :], ps[:], self.xT[:, dco, :], ALU.add),
                       pst + [self.xT_t[dco]], [self.xT_t[dco]])

    def rwkv_chunk(self, ti, cb, rb, kb_, vb, t1e, a1e, g1T, yT):
        cv, cm = self.colvec, self.cmat
        identb = self.cbf[:, 128:256]
        cwt = [self.cw_t]
        cs = slice(cb * 128, (cb + 1) * 128)
        A = self.ar_alloc
        sg = A([D], F32)
        bb = A([D], F32)
        a_ = A([D], BF16)
        at, rt, kt, bt, kh, bh, bv, zb = [A([D], BF16) for _ in range(8)]
        sm = A([8, 16], F32)
        sm2 = A([1, 16], F32)
        fmT = A([DC, 4, 128], BF16)
        tmpf = [A([128], F32) for _ in range(2)]
        mark2 = self.ar_ptr
        ex = A([D], F32)
        en = A([D], F32)
        kkb = A([D], F32)
        km = A([D], F32)
        y = sg
        v3 = lambda b_: b_.ap.rearrange("p (h j) -> p h j", h=16)
        bc = lambda i: sm.ap[:, i, :].unsqueeze(2).to_broadcast([128, 16, 64])
        for hf in range(2):
            hs_ = slice(hf * 512, (hf + 1) * 512)
            ps, pst = self.bank(hf)
            self.E("pe", lambda e, ps=ps, hs_=hs_: e.matmul(ps[:], t1e.ap[0:65, cs], self.w2b[0:65, hs_], start=True, stop=True),
                   t1e.tks + cwt, pst)
            self.E("act", lambda e, ps=ps, hs_=hs_: e.activation(sg.ap[:, hs_], ps[:], AF.Sigmoid), pst, sg.tks)
            ps, pst = self.bank(2 + hf)
            self.E("pe", lambda e, ps=ps, hs_=hs_: e.matmul(ps[:], a1e.ap[0:65, cs], self.a2b[0:65, hs_], start=True, stop=True),
                   a1e.tks + cwt, pst)
            self.E("act", lambda e, ps=ps, hs_=hs_: e.activation(a_.ap[:, hs_], ps[:], AF.Sigmoid), pst, a_.tks)
        if RWDBG <= 2:
            return
        if RWDBG <= 3:
            return
        self.E("dve", lambda e: e.tensor_tensor(kkb.ap, kb_.ap, self.rowb[:, 0, :], ALU.mult), kb_.tks + [self.rowb_t], kkb.tks)
        self.E("dve", lambda e: e.tensor_tensor(bb.ap, kkb.ap, kkb.ap, ALU.mult), kkb.tks, bb.tks)
        self.E("dve", lambda e: e.tensor_reduce(sm.ap[:, 0, :], v3(bb), AX.X, ALU.add), bb.tks, sm.tks)
        self.E("dve", lambda e: e.tensor_scalar(sm.ap[:, 0, :], sm.ap[:, 0, :], 1e-24, None, ALU.max), sm.tks, sm.tks)
        self.E("act", lambda e: e.activation(sm.ap[:, 0, :], sm.ap[:, 0, :], AF.Sqrt), sm.tks, sm.tks)
        self.E("dve", lambda e: e.reciprocal(sm.ap[:, 0, :], sm.ap[:, 0, :]), sm.tks, sm.tks)
        self.E("dve", lambda e: e.tensor_tensor(v3(kkb), v3(kkb), bc(0), ALU.mult), kkb.tks + sm.tks, kkb.tks)
        self.E("dve", lambda e: e.scalar_tensor_tensor(km.ap, a_.ap, 1.0, self.rowb[:, 1, :], ALU.subtract, ALU.mult),
               a_.tks + [self.rowb_t], km.tks)
        self.E("dve", lambda e: e.scalar_tensor_tensor(km.ap, km.ap, 1.0, kb_.ap, ALU.add, ALU.mult), km.tks + kb_.tks, km.tks)
        self.E("dve", lambda e: e.tensor_tensor(bb.ap, kkb.ap, a_.ap, ALU.mult), kkb.tks + a_.tks, bb.tks)
        tri = self.cbf[:, 384:512]
        onesb = self.cbf[:, 0:128]
        sgh, sgl = kh, bh
        self.E("act", lambda e: e.copy(sgh.ap, sg.ap), sg.tks, sgh.tks)
        self.E("dve", lambda e: e.tensor_tensor(sgl.ap, sg.ap, sgh.ap, ALU.subtract), sg.tks + sgh.tks, sgl.tks)
        pcs = []
        for hf in range(2):
            hs_ = slice(hf * 512, (hf + 1) * 512)
            pc, pct = self.bank(4 + hf)
            self.E("pe", lambda e, pc=pc, hs_=hs_: e.matmul(pc[:], tri, sgh.ap[:, hs_], start=True, stop=False), sgh.tks + [self.cbf_t], pct)
            self.E("pe", lambda e, pc=pc, hs_=hs_: e.matmul(pc[:], tri, sgl.ap[:, hs_], start=False, stop=True), sgl.tks + [self.cbf_t], pct)
            pC, pCt = self.bank(6 + hf)
            self.E("pe", lambda e, pC=pC, hs_=hs_: e.matmul(pC[:], onesb, sgh.ap[:, hs_], start=True, stop=False), sgh.tks + [self.cbf_t], pCt)
            self.E("pe", lambda e, pC=pC, hs_=hs_: e.matmul(pC[:], onesb, sgl.ap[:, hs_], start=False, stop=True), sgl.tks + [self.cbf_t], pCt)
            pcs.append((pc, pct, pC, pCt))
        pw_, pwt = self.bk(0, 0, 128)
        for blk in range(DC):
            self.E("pe", lambda e, blk=blk: e.matmul(pw_[:, blk * 16:(blk + 1) * 16], sgh.ap[:, blk * 128:(blk + 1) * 128],
                                                     onesb[:, 0:16], start=True, stop=False), sgh.tks + [self.cbf_t], pwt)
            self.E("pe", lambda e, blk=blk: e.matmul(pw_[:, blk * 16:(blk + 1) * 16], sgl.ap[:, blk * 128:(blk + 1) * 128],
                                                     onesb[:, 0:16], start=False, stop=True), sgl.tks + [self.cbf_t], pwt)
        self.E("act", lambda e: e.activation(sm.ap[:, 7, 0:8], pw_.rearrange("p (b n) -> p b n", n=16)[:, :, 0], AF.Exp, scale=NEG_E),
               pwt, sm.tks)
        for hf in range(2):
            hs_ = slice(hf * 512, (hf + 1) * 512)
            pc, pct, pC, pCt = pcs[hf]
            self.E("act", lambda e, pc=pc, hs_=hs_: e.activation(ex.ap[:, hs_], pc[:], AF.Exp, scale=NEG_E), pct, ex.tks)
            self.E("act", lambda e, pc=pc, hs_=hs_: e.activation(en.ap[:, hs_], pc[:], AF.Exp, scale=-NEG_E), pct, en.tks)
            self.E("dve", lambda e, pc=pc, hs_=hs_: e.scalar_tensor_tensor(
                sg.ap[:, hs_], sg.ap[:, hs_], -1.0, pc[:], ALU.mult, ALU.add), sg.tks + pct, sg.tks)
        self.E("act", lambda e: e.activation(sg.ap, sg.ap, AF.Exp, scale=NEG_E), sg.tks, sg.tks)
        self.E("dve", lambda e: e.scalar_tensor_tensor(at.ap, kkb.ap, -1.0, sg.ap, ALU.mult, ALU.mult), kkb.tks + sg.tks, at.tks)
        self.E("dve", lambda e: e.tensor_tensor(rt.ap, rb.ap, ex.ap, ALU.mult), rb.tks + ex.tks, rt.tks)
        self.E("dve", lambda e: e.tensor_tensor(kt.ap, km.ap, en.ap, ALU.mult), km.tks + en.tks, kt.tks)
        self.E("dve", lambda e: e.tensor_tensor(bt.ap, bb.ap, en.ap, ALU.mult), bb.tks + en.tks, bt.tks)
        for hf in range(2):
            hs_ = slice(hf * 512, (hf + 1) * 512)
            pc, pct, pC, pCt = pcs[hf]
            self.E("act", lambda e, pC=pC, hs_=hs_: e.activation(ex.ap[:, hs_], pC[:], AF.Exp, scale=NEG_E), pCt, ex.tks)
        if RWDBG <= 4:
            return
        for j, src in enumerate((at, rt, bt, kt)):
            for hf in range(2):
                ph, pht = self.bank((2 * j + hf) % 8)
                for b4 in range(4):
                    blk = hf * 4 + b4
                    self.E("pe", lambda e, src=src, blk=blk, b4=b4, ph=ph: e.matmul(
                        ph[:, b4 * 128:(b4 + 1) * 128], src.ap[:, blk * 128:(blk + 1) * 128], identb, start=True, stop=True),
                        src.tks + [self.cbf_t], pht)
                if (2 * j + hf) % 2 == 0:
                    self.E("act", lambda e, j=j, hf=hf, ph=ph: e.copy(fmT.ap[:, hf * 4:hf * 4 + 4, j, :], ph[:].rearrange("p (b t) -> p b t", b=4)), pht, fmT.tks)
                else:
                    self.E("dve", lambda e, j=j, hf=hf, ph=ph: e.tensor_copy(fmT.ap[:, hf * 4:hf * 4 + 4, j, :], ph[:].rearrange("p (b t) -> p b t", b=4)), pht, fmT.tks)
        self.E("dve", lambda e: e.tensor_tensor(kh.ap, kt.ap, ex.ap, ALU.mult), kt.tks + ex.tks, kh.tks)
        self.E("dve", lambda e: e.tensor_tensor(bh.ap, bt.ap, ex.ap, ALU.mult), bt.tks + ex.tks, bh.tks)
        self.E("pool", lambda e: e.tensor_tensor(bb.ap, rb.ap, km.ap, ALU.mult), rb.tks + km.tks, bb.tks)
        self.E("pool", lambda e: e.tensor_tensor(bb.ap, bb.ap, self.rowb[:, 2, :], ALU.mult), bb.tks + [self.rowb_t], bb.tks)
        self.E("dve", lambda e: e.tensor_reduce(sm2.ap[:, 0, :], v3(bb), AX.X, ALU.add), bb.tks, sm2.tks)
        self.E("pool", lambda e: e.tensor_tensor(v3(bv), v3(vb), sm2.ap[:, 0, :].unsqueeze(2).to_broadcast([128, 16, 64]), ALU.mult),
               vb.tks + sm2.tks, bv.tks)
        if RWDBG <= 5:
            return
        self.ar_ptr = mark2
        Wb = [A([768], BF16) for _ in range(8)]
        tinv = [A([128], BF16) for _ in range(8)]
        pm = [A([256], BF16) for _ in range(8)]
        ut = [A([64], BF16) for _ in range(8)]

        def evac(hh, out, in_, reads, writes):
            if hh % 2 == 0:
                self.E("act", lambda e: e.copy(out, in_), reads, writes)
            else:
                self.E("dve", lambda e: e.tensor_copy(out, in_), reads, writes)
        for grp in range(2):
            hs = [grp * 8 + hh for hh in range(8)]
            for hh, h in enumerate(hs):
                blk, p0 = h // 2, (h % 2) * 64
                f = lambda j, n=1, blk=blk, p0=p0: fmT.ap[p0:p0 + 64, blk, j:j + n, :]
                W = Wb[hh]
                pb, pbt = self.bank(hh)
                self.E("pe", lambda e, pb=pb, f=f: e.matmul(pb[:, 0:256].rearrange("p (a t) -> p a t", a=2), f(2)[:, 0, :], f(0, 2), start=True, stop=True),
                       fmT.tks, pbt)
                self.E("pe", lambda e, pb=pb, f=f: e.matmul(pb[:, 256:384], f(3)[:, 0, :], f(1)[:, 0, :], start=True, stop=True),
                       fmT.tks, pbt)
                self.E("dve", lambda e, pb=pb, W=W: e.tensor_tensor(W.ap[:, 0:128], pb[:, 0:128], cm[:, CM_SU:CM_SU + 128], ALU.mult),
                       pbt + [self.cmat_t], W.tks)
                self.E("dve", lambda e, pb=pb, W=W: e.tensor_tensor(W.ap[:, 512:768], pb[:, 128:384], cm[:, CM_UI:CM_UI + 256], ALU.mult),
                       pbt + [self.cmat_t], W.tks)
                self.E("act", lambda e, W=W: e.copy(W.ap[:, 128:256], identb), [self.cbf_t], W.tks)
            for hh, h in enumerate(hs):
                blk, p0 = h // 2, (h % 2) * 64
                f = lambda j, n=1, blk=blk, p0=p0: fmT.ap[p0:p0 + 64, blk, j:j + n, :]
                W = Wb[hh]
                pb, pbt = self.bank(hh)
                self.E("pe", lambda e, pb=pb, f=f: e.matmul(pb[:, 0:256].rearrange("p (a t) -> p a t", a=2), f(0)[:, 0, :], f(2, 2), start=True, stop=True),
                       fmT.tks, pbt)
                self.E("dve", lambda e, pb=pb, W=W: e.tensor_tensor(W.ap[:, 256:512], pb[:, 0:256], cm[:, CM_SL:CM_SL + 256], ALU.mult),
                       pbt + [self.cmat_t], W.tks)
            for k in range(6):
                for hh in range(8):
                    W = Wb[hh]
                    X, P, XT = W.ap[:, 0:128], W.ap[:, 128:256], W.ap[:, 256:384]
                    pb, pbt = self.bank(hh)
                    self.E("pe", lambda e, pb=pb, X=X, XT=XT: e.matmul(pb[:, 0:128], XT, X, start=True, stop=True), W.tks, pbt)
                    self.E("pe", lambda e, pb=pb, P=P, XT=XT: e.matmul(pb[:, 128:256], XT, P, start=True, stop=False), W.tks, pbt)
                    self.E("pe", lambda e, pb=pb, P=P: e.matmul(pb[:, 128:256], identb, P, start=False, stop=True), W.tks + [self.cbf_t], pbt)
                    self.E("pe", lambda e, pb=pb, X=X, XT=XT: e.matmul(pb[:, 256:384], X, XT, start=True, stop=True), W.tks, pbt)
                    evac(hh, W.ap[:, 0:384], pb[:, 0:384], pbt, W.tks)
            for hh in range(8):
                W = Wb[hh]
                P, XT = W.ap[:, 128:256], W.ap[:, 256:384]
                pb, pbt = self.bank(hh)
                self.E("pe", lambda e, pb=pb, P=P, XT=XT: e.matmul(pb[:, 0:128], XT, P, start=True, stop=False), W.tks, pbt)
                self.E("pe", lambda e, pb=pb, P=P: e.matmul(pb[:, 0:128], identb, P, start=False, stop=True), W.tks + [self.cbf_t], pbt)
                evac(hh, tinv[hh].ap, pb[:, 0:128], pbt, tinv[hh].tks)
            for hh, h in enumerate(hs):
                blk = h // 2
                W = Wb[hh]
                pb, pbt = self.bank(hh)
                self.E("pe", lambda e, pb=pb, hh=hh, blk=blk: e.matmul(pb[:, 0:128], at.ap[:, blk * 128:(blk + 1) * 128], tinv[hh].ap, start=True, stop=True),
                       at.tks + tinv[hh].tks, pbt)
                self.E("pe", lambda e, pb=pb, hh=hh, W=W: e.matmul(pb[:, 128:256], W.ap[:, 384:512], tinv[hh].ap, start=True, stop=True),
                       W.tks + tinv[hh].tks, pbt)
                evac(hh, pm[hh].ap, pb[:, 0:256], pbt, pm[hh].tks)
            for hh, h in enumerate(hs):
                pb, pbt = self.bank(hh)
                self.E("pe", lambda e, pb=pb, hh=hh, h=h: e.matmul(pb[:, 0:64], pm[hh].ap[:, 0:128], self.sbf[:, h, :], start=True, stop=False),
                       pm[hh].tks + [self.sbf_t[h]], pbt)
                self.E("pe", lambda e, pb=pb, hh=hh, h=h: e.matmul(pb[:, 0:64], pm[hh].ap[:, 128:256], vb.ap[:, h * 64:(h + 1) * 64], start=False, stop=True),
                       pm[hh].tks + vb.tks, pbt)
                evac(hh, ut[hh].ap, pb[:, 0:64], pbt, ut[hh].tks)
            for hh, h in enumerate(hs):
                blk, p0 = h // 2, (h % 2) * 64
                W = Wb[hh]
                pb, pbt = self.bank(hh)
                self.E("pe", lambda e, pb=pb, h=h, blk=blk: e.matmul(pb[:, 0:64], fmT.ap[:, blk, 1, :], self.sbf[:, h, :], start=True, stop=False),
                       fmT.tks + [self.sbf_t[h]], pbt)
                self.E("pe", lambda e, pb=pb, hh=hh, W=W: e.matmul(pb[:, 0:64], W.ap[:, 512:640], ut[hh].ap, start=False, stop=False),
                       W.tks + ut[hh].tks, pbt)
                self.E("pe", lambda e, pb=pb, W=W, h=h: e.matmul(pb[:, 0:64], W.ap[:, 640:768], vb.ap[:, h * 64:(h + 1) * 64], start=False, stop=True),
                       W.tks + vb.tks, pbt)
                self.E("pe", lambda e, pb=pb, hh=hh, blk=blk: e.matmul(pb[:, 64:128], bh.ap[:, blk * 128:(blk + 1) * 128], ut[hh].ap, start=True, stop=False),
                       bh.tks + ut[hh].tks, pbt)
                self.E("pe", lambda e, pb=pb, h=h, blk=blk: e.matmul(pb[:, 64:128], kh.ap[:, blk * 128:(blk + 1) * 128], vb.ap[:, h * 64:(h + 1) * 64], start=False, stop=True),
                       kh.tks + vb.tks, pbt)
                evac(hh, y.ap[:, h * 64:(h + 1) * 64], pb[:, 0:64], pbt, y.tks)
                self.E("dve", lambda e, pb=pb, h=h, blk=blk, p0=p0: e.scalar_tensor_tensor(
                    self.spad[p0:p0 + 64, h, :], self.spad[p0:p0 + 64, h, :], sm.ap[p0:p0 + 64, 7, blk:blk + 1], pb[p0:p0 + 64, 64:128],
                    ALU.mult, ALU.add), [self.spad_t[h]] + sm.tks + pbt, [self.spad_t[h]])
                self.E("act", lambda e, h=h, p0=p0: e.copy(self.sbf[p0:p0 + 64, h, :], self.spad[p0:p0 + 64, h, :]),
                       [self.spad_t[h]], [self.sbf_t[h]])
        if RWDBG <= 6:
            return
        self.E("dve", lambda e: e.tensor_tensor(bb.ap, y.ap, y.ap, ALU.mult), y.tks, bb.tks)
        self.E("dve", lambda e: e.tensor_reduce(sm.ap[:, 2, :], v3(y), AX.X, ALU.add), y.tks, sm.tks)
        self.E("dve", lambda e: e.tensor_reduce(sm.ap[:, 3, :], v3(bb), AX.X, ALU.add), bb.tks, sm.tks)
        self.E("dve", lambda e: e.tensor_scalar(sm.ap[:, 2, :], sm.ap[:, 2, :], 1.0 / 64, None, ALU.mult), sm.tks, sm.tks)
        self.E("dve", lambda e: e.tensor_tensor(sm.ap[:, 4, :], sm.ap[:, 2, :], sm.ap[:, 2, :], ALU.mult), sm.tks, sm.tks)
        self.E("dve", lambda e: e.scalar_tensor_tensor(sm.ap[:, 3, :], sm.ap[:, 3, :], 1.0 / 64, sm.ap[:, 4, :], ALU.mult, ALU.subtract),
               sm.tks, sm.tks)
        self.E("act", lambda e: e.activation(sm.ap[:, 3, :], sm.ap[:, 3, :], AF.Sqrt, bias=cv[:, CV_GNEPS:CV_GNEPS + 1]),
               sm.tks + [self.colvec_t], sm.tks)
        self.E("dve", lambda e: e.reciprocal(sm.ap[:, 3, :], sm.ap[:, 3, :]), sm.tks, sm.tks)
        self.E("dve", lambda e: e.tensor_tensor(v3(y), v3(y), bc(2), ALU.subtract), y.tks + sm.tks, y.tks)
        self.E("dve", lambda e: e.tensor_tensor(v3(zb), v3(y), bc(3), ALU.mult), y.tks + sm.tks, zb.tks)
        for blk in range(DC):
            phb, pht = self.bk(blk % 4, 0, 256)
            self.E("pe", lambda e, blk=blk, phb=phb: e.matmul(phb[:, 0:128], zb.ap[:, blk * 128:(blk + 1) * 128], identb, start=True, stop=True),
                   zb.tks + [self.cbf_t], pht)
            self.E("pe", lambda e, blk=blk, phb=phb: e.matmul(phb[:, 128:256], bv.ap[:, blk * 128:(blk + 1) * 128], identb, start=True, stop=True),
                   bv.tks + [self.cbf_t], pht)
            pg, pgt = self.bk(4 + blk % 4, 0, 256)
            self.E("pe", lambda e, blk=blk, pg=pg: e.matmul(pg[:, 0:128], self.g2b[:, blk * 128:(blk + 1) * 128], g1T.ap[:, cs], start=True, stop=True),
                   [self.cw_t] + g1T.tks, pgt)
            tf = tmpf[blk % 2]
            self.E("dve", lambda e, blk=blk, phb=phb, tf=tf: e.tensor_scalar(
                tf.ap, phb[:, 0:128], cv[:, CV_LNW + blk:CV_LNW + blk + 1], cv[:, CV_LNB + blk:CV_LNB + blk + 1], ALU.mult, ALU.add),
                pht + [self.colvec_t], tf.tks)
            self.E("dve", lambda e, phb=phb, tf=tf: e.tensor_tensor(tf.ap, tf.ap, phb[:, 128:256], ALU.add), pht + tf.tks, tf.tks)
            self.E("dve", lambda e, blk=blk, pg=pg, tf=tf: e.tensor_tensor(yT[blk].ap[:, cs], tf.ap, pg[:, 0:128], ALU.mult),
                   tf.tks + pgt, yT[blk].tks)


    def run_stage(self, ti, s):
        if s.startswith("ffn"):
            self.ffn(int(s[3:]))
        elif s == "ab":
            self.ab(ti)
        elif s == "rwkv":
            self.rwkv(ti)
        elif s == "dbgnorm":
            self.ar_reset()
            self.rmsnorm(0)
            for dc in range(DC):
                self.E("dve", lambda e, dc=dc: e.tensor_copy(self.xT[:, dc, :], self.hv(dc)),
                       [self.h_t[dc]], [self.xT_t[dc]])


_CACHE = {}


def get_program(T, stages):
    key = (T, tuple(stages))
    if key not in _CACHE:
        b = Builder(T, list(stages))
        nc = b.build()
        _CACHE[key] = (nc, b)
    return _CACHE[key]


ALL_STAGES = ["ffn0", "ab", "ffn1", "ffn2", "rwkv", "ffn3"]


def make_in_maps(inp, T, stages, ncores):
    x = np.asarray(inp["x"], np.float32)
    f32 = lambda a: np.ascontiguousarray(np.asarray(a, np.float32))
    common = {"colvec": host_colvec(inp), "cmat": host_consts()}
    if any(s_.startswith("ffn") for s_ in stages):
        common["wg"] = f32(inp["ffn_w_gate"]).reshape(4, D, DFF)
        common["wu"] = f32(inp["ffn_w_up"]).reshape(4, D, DFF)
        common["wd"] = f32(inp["ffn_w_down"]).reshape(4, DFF, D)
    if "ab" in stages:
        w_in = f32(inp["ab_w_in"]).reshape(D, 1280)
        qcols = w_in[:, :512].reshape(D, 2, 4, 64).transpose(0, 2, 1, 3).reshape(D, 512)
        common["w_in"] = np.ascontiguousarray(np.concatenate([qcols, w_in[:, 512:]], axis=1))
        w_out = f32(inp["ab_w_out"]).reshape(D, D)
        arows = w_out[:512].reshape(2, 4, 64, D).transpose(1, 0, 2, 3).reshape(512, D)
        common["w_out"] = np.ascontiguousarray(np.concatenate([arows, w_out[512:]], axis=0))
        common["pool_w"] = f32(inp["pool_w"]).reshape(4, 128, 128)
    if "rwkv" in stages:
        for nm in ("c_w_r", "c_w_k", "c_w_v", "c_w_o"):
            common[nm] = f32(inp[nm]).reshape(D, D)
        common["c_w1"] = f32(inp["c_w1"]).reshape(D, 64)
        common["c_a1"] = f32(inp["c_a1"]).reshape(D, 64)
        common["c_g1"] = f32(inp["c_g1"]).reshape(D, 128)
        common["c_w2e"] = np.ascontiguousarray(np.concatenate([f32(inp["c_w2"]).reshape(64, D), f32(inp["c_w0"]).reshape(1, D)], axis=0))
        common["c_a2e"] = np.ascontiguousarray(np.concatenate([f32(inp["c_a2"]).reshape(64, D), f32(inp["c_a0"]).reshape(1, D)], axis=0))
        common["c_g2"] = f32(inp["c_g2"]).reshape(128, D)
        common["c_rows"] = np.ascontiguousarray(np.stack([f32(inp["c_k_k"]).reshape(D), f32(inp["c_k_a"]).reshape(D),
                                                          f32(inp["c_r_k"]).reshape(D)], axis=0))
    in_maps = []
    for ci in range(ncores):
        m = dict(common)
        m["x"] = np.ascontiguousarray(x[ci, :T])
        in_maps.append(m)
    return in_maps


def run(inp, T=SEQ, stages=ALL_STAGES, ncores=8, trace=False):
    nc, b = get_program(T, stages)
    in_maps = make_in_maps(inp, T, stages, ncores)
    res = run_bass_kernel_spmd(nc, in_maps, core_ids=list(range(ncores)), trace=trace)
    out = np.stack([np.asarray(r["y"]) for r in res.results], axis=0)
    return out, res


def kernel(**inputs):
    out, _ = run(inputs)
    return out.astype(np.float32)
```
